# Optimizing a Trainium2 kernel written in Bass

```python
import math
import jax, jax.numpy as jnp
from jax import lax
import numpy as np

D_MODEL = 1024
BATCH = 32
SEQ = 2048
DEPTH = 1
DEC_BATCH = 8
DEC_SEQ = 16
PAST_LEN = 4096

CHUNK = 64
D_FF = 2816
SSD_EXPAND = 2
D_INNER = SSD_EXPAND * D_MODEL
SSD_HEAD_DIM = 64
SSD_HEADS = D_INNER // SSD_HEAD_DIM
SSD_GROUPS = 8
SSD_HPG = SSD_HEADS // SSD_GROUPS
SSD_STATE = 128
CONV_K = 4
CONV_DIM = D_INNER + 2 * SSD_GROUPS * SSD_STATE
POOL_WINDOWS = (2, 4, 8, 16)
POOL_GROUPS = 4
D_POOL = D_MODEL
POOL_GROUP_DIM = D_POOL // POOL_GROUPS
POOL_BUF = max(POOL_WINDOWS) - 1
N_BRANCHES = 2
D_IN_PROJ = D_INNER + CONV_DIM + SSD_HEADS + D_POOL + N_BRANCHES * D_MODEL
EPS = 1e-6

kernel_name = 'streaming_ssd_pool_hybrid_step'


def _rms_norm(x, g):
    xf = x.astype(jnp.float32)
    y = xf * lax.rsqrt(jnp.mean(xf * xf, axis=-1, keepdims=True) + EPS)
    return (y * g.astype(jnp.float32)).astype(x.dtype)


def _grouped_rms_norm(x, g):
    b, t, _ = x.shape
    xf = x.astype(jnp.float32).reshape(b, t, SSD_GROUPS, D_INNER // SSD_GROUPS)
    y = xf * lax.rsqrt(jnp.mean(xf * xf, axis=-1, keepdims=True) + EPS)
    return (y.reshape(b, t, D_INNER) * g.astype(jnp.float32)).astype(x.dtype)


def _swiglu(x, w_gate, w_up, w_down):
    return (jax.nn.silu(x @ w_gate) * (x @ w_up)) @ w_down


def _causal_dwconv(x, buf, w, b):
    t = x.shape[1]
    xp = jnp.concatenate([buf.astype(x.dtype), x], axis=1)
    out = b.astype(x.dtype)
    for k in range(CONV_K):
        out = out + xp[:, k:k + t] * w[k]
    return out, xp[:, -(CONV_K - 1):]


def _ssd_scan(xh, dt, a, bm, cm, h0):
    b, t = xh.shape[0], xh.shape[1]
    blk = min(CHUNK, t)
    nc = t // blk
    xg = xh.reshape(b, nc, blk, SSD_GROUPS, SSD_HPG, SSD_HEAD_DIM)
    dtg = dt.reshape(b, nc, blk, SSD_GROUPS, SSD_HPG)
    bc = bm.reshape(b, nc, blk, SSD_GROUPS, SSD_STATE)
    cc = cm.reshape(b, nc, blk, SSD_GROUPS, SSD_STATE)
    a_cs = jnp.cumsum(dtg * a.reshape(SSD_GROUPS, SSD_HPG), axis=2)
    xdt = xg * dtg[..., None]
    seg = a_cs[:, :, :, None] - a_cs[:, :, None, :]
    causal = jnp.tril(jnp.ones((blk, blk), dtype=bool))[:, :, None, None]
    decay = jnp.exp(jnp.where(causal, seg, -jnp.inf))
    scores = jnp.einsum('bclgn,bcsgn->bclsg', cc, bc)
    y_diag = jnp.einsum('bclsg,bclsgh,bcsghp->bclghp', scores, decay, xdt)
    decay_to_end = jnp.exp(a_cs[:, :, -1:] - a_cs)
    blk_states = jnp.einsum('bclgn,bclgh,bclghp->bcghpn', bc, decay_to_end, xdt)
    blk_decay = jnp.exp(a_cs[:, :, -1])

    def step(h, inp):
        st, dec = inp
        return dec[..., None, None] * h + st, h

    h0g = h0.reshape(b, SSD_GROUPS, SSD_HPG, SSD_HEAD_DIM, SSD_STATE)
    h_fin, h_prev = lax.scan(step, h0g, (jnp.moveaxis(blk_states, 1, 0), jnp.moveaxis(blk_decay, 1, 0)))
    h_prev = jnp.moveaxis(h_prev, 0, 1)
    y_off = jnp.einsum('bclgn,bclgh,bcghpn->bclghp', cc, jnp.exp(a_cs), h_prev)
    y = (y_diag + y_off).reshape(b, t, SSD_HEADS, SSD_HEAD_DIM)
    return y, h_fin.reshape(b, SSD_HEADS, SSD_HEAD_DIM, SSD_STATE)


def _multiscale_pool(u, buf, pos0):
    t = u.shape[1]
    up = jnp.concatenate([buf.astype(u.dtype), u], axis=1)
    cs = jnp.cumsum(up.astype(jnp.float32), axis=1)
    cs = jnp.pad(cs, ((0, 0), (1, 0), (0, 0)))
    pos = pos0 + jnp.arange(t, dtype=jnp.int32)
    outs = []
    for gi, k in enumerate(POOL_WINDOWS):
        sl = slice(gi * POOL_GROUP_DIM, (gi + 1) * POOL_GROUP_DIM)
        lo = POOL_BUF + 1 - k
        wsum = cs[:, POOL_BUF + 1:POOL_BUF + 1 + t, sl] - cs[:, lo:lo + t, sl]
        cnt = jnp.minimum(pos + 1, k).astype(jnp.float32)[None, :, None]
        outs.append(wsum / cnt)
    pooled = jnp.concatenate(outs, axis=-1).astype(u.dtype) - u
    return pooled, up[:, -POOL_BUF:]


def _mixer(h, conv_buf, ssm_state, pool_buf, pos0, p):
    b, t, _ = h.shape
    proj = h @ p['w_in']
    cuts = np.cumsum([D_INNER, CONV_DIM, SSD_HEADS, D_POOL, D_MODEL]).tolist()
    z, xbc, dt_raw, u_pool, g_a, g_b = jnp.split(proj, cuts, axis=-1)
    xbc, new_conv = _causal_dwconv(xbc, conv_buf, p['conv_w'], p['conv_b'])
    xbc = jax.nn.silu(xbc)
    xs, bm, cm = jnp.split(xbc, [D_INNER, D_INNER + SSD_GROUPS * SSD_STATE], axis=-1)
    xh = xs.reshape(b, t, SSD_HEADS, SSD_HEAD_DIM).astype(jnp.float32)
    dt = jax.nn.softplus(dt_raw.astype(jnp.float32) + p['dt_bias'].astype(jnp.float32))
    a = -jnp.exp(p['a_log'].astype(jnp.float32))
    y, new_ssm = _ssd_scan(xh, dt, a,
                           bm.reshape(b, t, SSD_GROUPS, SSD_STATE).astype(jnp.float32),
                           cm.reshape(b, t, SSD_GROUPS, SSD_STATE).astype(jnp.float32),
                           ssm_state.astype(jnp.float32))
    y = y + p['d_skip'].astype(jnp.float32)[:, None] * xh
    y = y.reshape(b, t, D_INNER).astype(h.dtype) * jax.nn.silu(z)
    y_a = _grouped_rms_norm(y, p['ssd_norm_g']) @ p['w_proj_ssd']
    pooled, new_pool = _multiscale_pool(u_pool, pool_buf, pos0)
    mixed = jnp.einsum('btgc,gcd->btgd', pooled.reshape(b, t, POOL_GROUPS, POOL_GROUP_DIM), p['pool_mix'])
    y_b = (mixed.reshape(b, t, D_POOL) * p['pool_scale']) @ p['w_proj_pool']
    gate_a = jax.nn.sigmoid(g_a.astype(jnp.float32)).astype(h.dtype)
    gate_b = jax.nn.sigmoid(g_b.astype(jnp.float32)).astype(h.dtype)
    out = (gate_a * y_a + gate_b * y_b) @ p['w_out']
    return out, new_conv, new_ssm.astype(ssm_state.dtype), new_pool


def _layer(x, conv_buf, ssm_state, pool_buf, pos0, p):
    f1 = _swiglu(_rms_norm(x, p['ffn1_pre_g']), p['ffn1_w_gate'], p['ffn1_w_up'], p['ffn1_w_down'])
    x = x + 0.5 * _rms_norm(f1, p['ffn1_post_g'])
    m, new_conv, new_ssm, new_pool = _mixer(_rms_norm(x, p['mix_pre_g']), conv_buf, ssm_state, pool_buf, pos0, p)
    x = x + _rms_norm(m, p['mix_post_g'])
    f2 = _swiglu(_rms_norm(x, p['ffn2_pre_g']), p['ffn2_w_gate'], p['ffn2_w_up'], p['ffn2_w_down'])
    x = x + 0.5 * _rms_norm(f2, p['ffn2_post_g'])
    return x, new_conv, new_ssm, new_pool


def setup_inputs(seed: int = 0) -> dict:
    key = jax.random.key(seed)
    ks = jax.random.split(key, 40)

    def nrm(k, shape, scale=1.0):
        return jax.random.normal(k, shape, jnp.float32) * scale

    def gain(k, n):
        return 1.0 + 0.02 * jax.random.normal(k, (DEPTH, n), jnp.float32)

    dt0 = jnp.exp(jax.random.uniform(ks[20], (DEPTH, SSD_HEADS), jnp.float32, math.log(1e-3), math.log(1e-1)))
    return {
        'x_prompt': nrm(ks[0], (BATCH, SEQ, D_MODEL)),
        'x_sample': nrm(ks[1], (DEC_BATCH, DEC_SEQ, D_MODEL)),
        'cache_conv': nrm(ks[2], (DEPTH, DEC_BATCH, CONV_K - 1, CONV_DIM)),
        'state_ssm': nrm(ks[3], (DEPTH, DEC_BATCH, SSD_HEADS, SSD_HEAD_DIM, SSD_STATE), 0.1),
        'cache_pool': nrm(ks[4], (DEPTH, DEC_BATCH, POOL_BUF, D_POOL)),
        'ffn1_pre_g': gain(ks[5], D_MODEL),
        'ffn1_post_g': gain(ks[6], D_MODEL),
        'ffn1_w_gate': nrm(ks[7], (DEPTH, D_MODEL, D_FF), D_MODEL ** -0.5),
        'ffn1_w_up': nrm(ks[8], (DEPTH, D_MODEL, D_FF), D_MODEL ** -0.5),
        'ffn1_w_down': nrm(ks[9], (DEPTH, D_FF, D_MODEL), D_FF ** -0.5),
        'mix_pre_g': gain(ks[10], D_MODEL),
        'mix_post_g': gain(ks[11], D_MODEL),
        'w_in': nrm(ks[12], (DEPTH, D_MODEL, D_IN_PROJ), D_MODEL ** -0.5),
        'conv_w': nrm(ks[13], (DEPTH, CONV_K, CONV_DIM), CONV_K ** -0.5),
        'conv_b': nrm(ks[14], (DEPTH, CONV_DIM), 0.02),
        'dt_bias': dt0 + jnp.log(-jnp.expm1(-dt0)),
        'a_log': jnp.log(jax.random.uniform(ks[15], (DEPTH, SSD_HEADS), jnp.float32, 1.0, 16.0)),
        'd_skip': 1.0 + 0.1 * jax.random.normal(ks[16], (DEPTH, SSD_HEADS), jnp.float32),
        'ssd_norm_g': gain(ks[17], D_INNER),
        'w_proj_ssd': nrm(ks[18], (DEPTH, D_INNER, D_MODEL), D_INNER ** -0.5),
        'pool_mix': nrm(ks[19], (DEPTH, POOL_GROUPS, POOL_GROUP_DIM, POOL_GROUP_DIM), POOL_GROUP_DIM ** -0.5),
        'pool_scale': 1.0 + 0.1 * jax.random.normal(ks[21], (DEPTH, D_POOL), jnp.float32),
        'w_proj_pool': nrm(ks[22], (DEPTH, D_POOL, D_MODEL), D_POOL ** -0.5),
        'w_out': nrm(ks[23], (DEPTH, D_MODEL, D_MODEL), D_MODEL ** -0.5),
        'ffn2_pre_g': gain(ks[24], D_MODEL),
        'ffn2_post_g': gain(ks[25], D_MODEL),
        'ffn2_w_gate': nrm(ks[26], (DEPTH, D_MODEL, D_FF), D_MODEL ** -0.5),
        'ffn2_w_up': nrm(ks[27], (DEPTH, D_MODEL, D_FF), D_MODEL ** -0.5),
        'ffn2_w_down': nrm(ks[28], (DEPTH, D_FF, D_MODEL), D_FF ** -0.5),
    }


def reference(x_prompt, x_sample, cache_conv, state_ssm, cache_pool,
              ffn1_pre_g, ffn1_post_g, ffn1_w_gate, ffn1_w_up, ffn1_w_down,
              mix_pre_g, mix_post_g, w_in, conv_w, conv_b, dt_bias, a_log, d_skip,
              ssd_norm_g, w_proj_ssd, pool_mix, pool_scale, w_proj_pool, w_out,
              ffn2_pre_g, ffn2_post_g, ffn2_w_gate, ffn2_w_up, ffn2_w_down):
    y_p, y_s = x_prompt, x_sample
    bp = x_prompt.shape[0]
    conv_p, ssm_p, pool_p, conv_s, ssm_s, pool_s = [], [], [], [], [], []
    for i in range(DEPTH):
        p = {
            'ffn1_pre_g': ffn1_pre_g[i], 'ffn1_post_g': ffn1_post_g[i],
            'ffn1_w_gate': ffn1_w_gate[i], 'ffn1_w_up': ffn1_w_up[i], 'ffn1_w_down': ffn1_w_down[i],
            'mix_pre_g': mix_pre_g[i], 'mix_post_g': mix_post_g[i], 'w_in': w_in[i],
            'conv_w': conv_w[i], 'conv_b': conv_b[i], 'dt_bias': dt_bias[i], 'a_log': a_log[i],
            'd_skip': d_skip[i], 'ssd_norm_g': ssd_norm_g[i], 'w_proj_ssd': w_proj_ssd[i],
            'pool_mix': pool_mix[i], 'pool_scale': pool_scale[i], 'w_proj_pool': w_proj_pool[i],
            'w_out': w_out[i],
            'ffn2_pre_g': ffn2_pre_g[i], 'ffn2_post_g': ffn2_post_g[i],
            'ffn2_w_gate': ffn2_w_gate[i], 'ffn2_w_up': ffn2_w_up[i], 'ffn2_w_down': ffn2_w_down[i],
        }
        zc = jnp.zeros((bp, CONV_K - 1, CONV_DIM), x_prompt.dtype)
        zs = jnp.zeros((bp, SSD_HEADS, SSD_HEAD_DIM, SSD_STATE), x_prompt.dtype)
        zp = jnp.zeros((bp, POOL_BUF, D_POOL), x_prompt.dtype)
        y_p, c1, s1, q1 = _layer(y_p, zc, zs, zp, 0, p)
        y_s, c2, s2, q2 = _layer(y_s, cache_conv[i], state_ssm[i], cache_pool[i], PAST_LEN, p)
        conv_p.append(c1); ssm_p.append(s1); pool_p.append(q1)
        conv_s.append(c2); ssm_s.append(s2); pool_s.append(q2)
    new_conv_prompt = jnp.stack(conv_p)
    new_ssm_prompt = jnp.stack(ssm_p)
    new_pool_prompt = jnp.stack(pool_p)
    new_conv_sample = jnp.stack(conv_s)
    new_ssm_sample = jnp.stack(ssm_s)
    new_pool_sample = jnp.stack(pool_s)
    return (y_p, y_s, new_conv_prompt, new_ssm_prompt, new_pool_prompt, new_conv_sample, new_ssm_sample, new_pool_sample)
```

```python
import numpy as np
from collections import defaultdict
from contextlib import ExitStack

import concourse.bass as bass
import concourse.mybir as mybir
from concourse.bass_utils import run_bass_kernel_spmd

F32 = mybir.dt.float32
BF16 = mybir.dt.bfloat16
I32 = mybir.dt.int32
AF = mybir.ActivationFunctionType
ALU = mybir.AluOpType

D = 1024
KC = 8
DFF = 2816
FC = 22
DIN = 2048
CONVD = 4096
NH = 32
EPS = 1e-6
NCORES = 8

ENGS = ["pe", "act", "dve", "pool", "sp"]
SELF_SYNC = ("act", "dve", "pool")
SAME_ENGINE_NEAR = 2


class Cfg:
    NSEQ = 4
    SEQ = 2048
    TILE = 512
    DSEQ = 16
    SAMPLE = True


class Buf:
    __slots__ = ("name", "w", "r", "excl")

    def __init__(self, name, excl=False):
        self.name = name
        self.w = {}
        self.r = {}
        self.excl = excl


class Prog:
    def __init__(self):
        self.entries = {e: [] for e in ENGS}
        self.cnt = defaultdict(int)
        self.seen = {e: {} for e in ENGS}
        self.keys = set(ENGS)
        self.final_dma = {}

    def op(self, eng, fn, reads=(), writes=(), inc=True, dma=None):
        deps = {}
        for b in reads:
            for k, v in b.w.items():
                if deps.get(k, 0) < v:
                    deps[k] = v
            if b.excl:
                for k, v in b.r.items():
                    if k != eng and deps.get(k, 0) < v:
                        deps[k] = v
        far = self.cnt[eng] - SAME_ENGINE_NEAR
        for b in writes:
            for k, v in b.w.items():
                if (k != eng or v <= far) and deps.get(k, 0) < v:
                    deps[k] = v
            for k, v in b.r.items():
                if (k != eng or v <= far) and deps.get(k, 0) < v:
                    deps[k] = v
        waits = []
        seen = self.seen[eng]
        for k, v in deps.items():
            if k == eng and eng not in SELF_SYNC:
                continue
            if seen.get(k, 0) < v:
                waits.append((k, v))
                seen[k] = v
        if dma is not None:
            self.keys.add(dma)
            self.cnt[dma] += 16
            key, val = dma, self.cnt[dma]
            self.final_dma[dma] = val
        else:
            key = eng
            if inc:
                self.cnt[eng] += 1
                val = self.cnt[eng]
            else:
                val = self.cnt[eng] + 1
        self.entries[eng].append((waits, fn, inc, dma))
        for b in writes:
            b.w = {key: val}
            b.r = {}
        for b in reads:
            if b.r.get(key, 0) < val:
                b.r[key] = val

    def replay(self, nc, es):
        sems = {}
        for k in sorted(self.keys):
            sems[k] = es.enter_context(nc.semaphore("s_" + k.replace(":", "_")))
        block = es.enter_context(nc.Block())
        entries = self.entries
        final_dma = self.final_dma

        def run(eng_name, h):
            for waits, fn, inc, dma in entries[eng_name]:
                for k, v in waits:
                    h.wait_ge(sems[k], v)
                inst = fn(h)
                if dma is not None:
                    inst.then_inc(sems[dma], 16)
                elif inc:
                    inst.then_inc(sems[eng_name], 1)

        @block.tensor
        def _(h):
            run("pe", h)

        @block.scalar
        def _(h):
            run("act", h)

        @block.vector
        def _(h):
            run("dve", h)

        @block.gpsimd
        def _(h):
            run("pool", h)

        @block.sync
        def _(h):
            run("sp", h)
            for k, v in final_dma.items():
                h.wait_ge(sems[k], v)


def bc_last(ap, n):
    sh = list(ap.shape)
    return ap.unsqueeze(len(sh)).broadcast_to(sh + [n])


def bc_mid(ap, n):
    sh = list(ap.shape)
    return ap.unsqueeze(1).broadcast_to([sh[0], n] + sh[1:])


def weight_block_defs():
    blocks = []

    def ffn(pfx):
        for j in range(11):
            blocks.append((f"{pfx}_gu{j}", [(f"{pfx}_w_gate", KC, 256 * j, 256), (f"{pfx}_w_up", KC, 256 * j, 256)]))
        for m in range(8):
            blocks.append((f"{pfx}_dn{m}", [(f"{pfx}_w_down", FC, 128 * m, 128)]))

    ffn("ffn1")
    for kind, j in MIX_ORDER:
        if kind == "z":
            blocks.append((f"z{j}", [("w_in", KC, 512 * j, 512)]))
        elif kind == "x":
            blocks.append((f"xbc{j}", [("w_in", KC, 2048 + 512 * j, 512)]))
        else:
            blocks.append(("dt", [("w_in", KC, 6144, 32)]))
    for j in range(2):
        blocks.append((f"u{j}", [("w_in", KC, 6176 + 512 * j, 512)]))
    for j in range(2):
        blocks.append((f"ga{j}", [("w_in", KC, 7200 + 512 * j, 512)]))
        blocks.append((f"ps{2 * j}", [("w_proj_ssd", 16, 256 * (2 * j), 256)]))
        blocks.append((f"ps{2 * j + 1}", [("w_proj_ssd", 16, 256 * (2 * j + 1), 256)]))
    blocks.append(("pm", [("pool_mix", 8, 0, 256)]))
    for j in range(2):
        blocks.append((f"gb{j}", [("w_in", KC, 8224 + 512 * j, 512)]))
        blocks.append((f"pp{j}", [("w_proj_pool", KC, 512 * j, 512)]))
    for j in range(2):
        blocks.append((f"wo{j}", [("w_out", KC, 512 * j, 512)]))
    ffn("ffn2")
    return blocks


WEIGHT_SHAPES = {
    "ffn1_w_gate": (D, DFF), "ffn1_w_up": (D, DFF), "ffn1_w_down": (DFF, D),
    "ffn2_w_gate": (D, DFF), "ffn2_w_up": (D, DFF), "ffn2_w_down": (DFF, D),
    "w_in": (D, 9248), "w_proj_ssd": (DIN, D), "pool_mix": (1024, 256),
    "w_proj_pool": (D, D), "w_out": (D, D),
}
VEC_SHAPES = {
    "ffn1_pre_g": D, "ffn1_post_g": D, "mix_pre_g": D, "mix_post_g": D, "ffn2_pre_g": D, "ffn2_post_g": D,
    "conv_b": CONVD, "dt_bias": NH, "a_log": NH, "d_skip": NH, "ssd_norm_g": DIN, "pool_scale": D,
}
MIX_ORDER = [("z", 0), ("x", 0), ("dt", 0), ("x", 1), ("z", 1), ("x", 2), ("x", 3), ("z", 2), ("x", 4), ("x", 5),
             ("z", 3), ("x", 6), ("x", 7)]
RING_ELEMS = 4096
NSLOT = 4


def build_program(cfg):
    nc = bass.Bass("TRN2", target_bir_lowering=False)
    es = ExitStack()
    P = Prog()
    NSEQ, SEQ, TILE, DSEQ = cfg.NSEQ, cfg.SEQ, cfg.TILE, cfg.DSEQ
    TILE = min(TILE, SEQ)
    assert SEQ % TILE == 0 and TILE % 128 == 0

    def din(name, shape):
        return nc.dram_tensor(name, list(shape), F32, kind="ExternalInput")

    def dout(name, shape):
        return nc.dram_tensor(name, list(shape), F32, kind="ExternalOutput")

    x_prompt = din("x_prompt", [NSEQ * SEQ, D])
    x_sample = din("x_sample", [DSEQ, D])
    cache_conv = din("cache_conv", [3, CONVD])
    state_ssm = din("state_ssm", [DIN, 128])
    cache_pool = din("cache_pool", [15, D])
    conv_w = din("conv_w", [4, CONVD])
    W = {k: din(k, v) for k, v in WEIGHT_SHAPES.items()}
    V = {k: din(k, [v]) for k, v in VEC_SHAPES.items()}
    y_prompt = dout("y_prompt", [NSEQ * SEQ, D])
    y_sample = dout("y_sample", [DSEQ, D])
    o_conv_p = dout("new_conv_prompt", [NSEQ * 3, CONVD])
    o_ssm_p = dout("new_ssm_prompt", [NSEQ * DIN, 128])
    o_pool_p = dout("new_pool_prompt", [NSEQ * 15, D])
    o_conv_s = dout("new_conv_sample", [3, CONVD])
    o_ssm_s = dout("new_ssm_sample", [DIN, 128])
    o_pool_s = dout("new_pool_sample", [15, D])

    blocks = weight_block_defs()
    NBLK = len(blocks)
    wsc = nc.dram_tensor("wsc", [NBLK, 128, RING_ELEMS], BF16, kind="Internal")

    def sb(name, shape, dt=F32):
        t = es.enter_context(nc.sbuf_tensor(name, list(shape), dt))
        return t, Buf(name)

    ring = [sb(f"ring{i}", [128, RING_ELEMS], BF16) for i in range(NSLOT)]
    xT0, _u0 = sb("xT0", [128, KC, TILE])
    xT1, _u1 = sb("xT1", [128, KC, TILE])
    XS = [(xT0, [Buf(f"xT0_{k}") for k in range(KC)]), (xT1, [Buf(f"xT1_{k}") for k in range(KC)])]
    xT = xT0
    xin = [sb(f"xin{i}", [128, D]) for i in range(2)]
    yout = [sb(f"yout{i}", [128, D]) for i in range(2)]
    xn, b_xn = sb("xn", [128, KC, TILE], BF16)
    bigA, _b_bigA_unused = sb("bigA", [128, 16384], BF16)
    RA = [Buf(f"A{i}") for i in range(16384 // TILE)]

    def AR(lo, n=1):
        return RA[lo:lo + n]
    SQA = 11264 // TILE if TILE == 512 else 22
    SQ2 = 8
    bigB, _b_bigB_unused = sb("bigB", [128, 8192], BF16)
    rstd, b_rstd = sb("rstd", [128, TILE])
    tmpA = [sb(f"tmpA{i}", [128, TILE + 16]) for i in range(2)]
    tmpP = [sb(f"tmpP{i}", [128, TILE]) for i in range(2)]
    mixedS, b_mixedS = sb("mixedS", [128, KC, TILE], BF16)
    b_tick = [Buf(f"tick{k}") for k in range(KC)]
    ust = [sb(f"ust{i}", [128, 15 + TILE]) for i in range(2)]
    hist, b_hist = sb("hist", [128, 32, 4])
    uhist, b_uhist = sb("uhist", [128, 8, 16])
    hT, _b_hT_unused = sb("hT", [128, DIN])
    hTb, _b_hTb_unused = sb("hTb", [128, DIN], BF16)
    NCHM = TILE // 128
    dtb, b_dtb = sb("dtb", [128, NCHM, 32])
    dta, b_dta = sb("dta", [128, NCHM, 32])
    e3, b_e3 = sb("e3", [128, NCHM, 64])
    wdt, b_wdt = sb("wdt", [128, NCHM, 32])
    dec, b_dec = sb("dec", [128, NCHM, 32])
    dtt, b_dtt = sb("dtt", [128, NCHM, 32])
    NSTG = 8
    xtok = [sb(f"xtok{i}", [128, 384], BF16) for i in range(4)]
    xw = [sb(f"xw{i}", [128, 256], BF16) for i in range(2)]
    stm = [sb(f"stm{i}", [128, 128]) for i in range(3)]
    ud = [sb(f"ud{i}", [128, 512]) for i in range(2)]
    eE = [sb(f"eE{i}", [128, 512], BF16) for i in range(2)]
    mt = [sb(f"mt{i}", [128, 512], BF16) for i in range(2)]
    t1 = [sb(f"t1_{i}", [128, 256]) for i in range(2)]
    yg = [sb(f"yg{i}", [128, 256], BF16) for i in range(4)]
    ssq = [sb(f"ssq{i}", [128, 4]) for i in range(3)]
    dgs = [sb(f"dgs{i}", [128, 128], BF16) for i in range(2)]
    htmp = [sb(f"htmp{i}", [128, 256]) for i in range(2)]
    b_hTg = [Buf(f"hT_g{g}") for g in range(8)]
    b_hTbg = [Buf(f"hTb_g{g}") for g in range(8)]
    b_gt = [[Buf(f"gt_{c}_{g}") for g in range(8)] for c in range(NCHM)]
    ALL_GT = [b for row in b_gt for b in row]
    ident, b_ident = sb("ident", [128, 128])
    identb, b_identb = sb("identb", [128, 128], BF16)
    onesf, b_onesf = sb("onesf", [128, 128])
    onesb, b_onesb = sb("onesb", [128, 128], BF16)
    Lm, b_Lm = sb("Lm", [128, 128])
    Um, b_Um = sb("Um", [128, 128])
    gains, b_gains = sb("gains", [128, 6, KC])
    gssd, b_gssd = sb("gssd", [128, 16])
    pscale, b_pscale = sb("pscale", [128, KC])
    cw, b_cw = sb("cw", [128, 4, 32])
    cb, b_cb = sb("cb", [128, 32])
    dtbias, b_dtbias = sb("dtbias", [128, 32])
    abc, b_abc = sb("abc", [128, 32])
    dsk, b_dsk = sb("dsk", [128, 32])
    invc, b_invc = sb("invc", [128, 4, 16])
    gpost, b_gpost = sb("gpost", [128, 3, KC])
    dskf, b_dskf = sb("dskf", [128, 16])
    Dg, b_Dg = sb("Dg", [128, 16, 128], BF16)
    ioti, b_ioti = sb("ioti", [128, 16], I32)
    stg4, b_stg4 = sb("stg4", [16, 1024])

    NBANK = 8
    banks = []
    for i in range(NBANK):
        t = es.enter_context(nc.psum_tensor(f"pb{i}", [128, 512], F32))
        banks.append((t, Buf(f"pb{i}", excl=True)))
    bank_rr = [0]

    def nbank():
        i = bank_rr[0]
        bank_rr[0] = (i + 1) % NBANK
        return banks[i]

    CONSTS = [b_ident, b_identb, b_onesf, b_onesb, b_Lm, b_Um, b_gains, b_gssd, b_pscale, b_cw, b_cb,
              b_dtbias, b_abc, b_dsk, b_invc]

    rr = {"cast": 0}

    def act(fn, reads, writes):
        P.op("act", fn, reads, writes)

    def dve(fn, reads, writes):
        P.op("dve", fn, reads, writes)

    def pool(fn, reads, writes):
        P.op("pool", fn, reads, writes)

    def pe(fn, reads, writes, inc=True):
        P.op("pe", fn, reads, writes, inc=inc)

    def dma(fn, reads, writes, key, eng="sp"):
        P.op(eng, fn, reads, writes, dma=key)

    def mm_group(out_ap, bank_buf, pairs, reads):
        n = len(pairs)
        for i, (l, r) in enumerate(pairs):
            pe(lambda h, l=l, r=r, i=i: h.matmul(out_ap, lhsT=l, rhs=r, start=(i == 0), stop=(i == n - 1)),
               reads, [bank_buf], inc=(i == n - 1))

    pool(lambda h: h.memset(ident[:], 0.0), [], [b_ident])
    pool(lambda h: h.affine_select(out=ident[:], in_=ident[:], pattern=[[-1, 128]], base=0, channel_multiplier=1,
                                   compare_op=ALU.not_equal, fill=1.0), [b_ident], [b_ident])
    pool(lambda h: h.tensor_copy(out=identb[:], in_=ident[:]), [b_ident], [b_identb])
    pool(lambda h: h.memset(onesf[:], 1.0), [], [b_onesf])
    pool(lambda h: h.memset(onesb[:], 1.0), [], [b_onesb])
    pool(lambda h: h.affine_select(out=Lm[:], in_=onesf[:], pattern=[[1, 128]], base=0, channel_multiplier=-1,
                                   compare_op=ALU.is_ge, fill=0.0), [b_onesf], [b_Lm])
    pool(lambda h: h.affine_select(out=Um[:], in_=onesf[:], pattern=[[-1, 128]], base=0, channel_multiplier=1,
                                   compare_op=ALU.is_gt, fill=0.0), [b_onesf], [b_Um])
    pool(lambda h: h.iota(ioti[:], pattern=[[1, 16]], base=1, channel_multiplier=0), [], [b_ioti])
    pool(lambda h: h.tensor_copy(out=invc[:, 0, :], in_=ioti[:]), [b_ioti], [b_invc])
    for gi in range(1, 4):
        pool(lambda h, gi=gi: h.tensor_copy(out=invc[:, gi, :], in_=invc[:, 0, :]), [b_invc], [b_invc])
    for gi in range(4):
        dve(lambda h, gi=gi: h.tensor_scalar(out=invc[:, gi, :], in0=invc[:, gi, :], scalar1=float(2 ** (gi + 1)),
                                             scalar2=None, op0=ALU.min), [b_invc], [b_invc])
    dve(lambda h: h.reciprocal(out=invc[:], in_=invc[:]), [b_invc], [b_invc])
    pool(lambda h: h.memset(hist[:], 0.0), [], [b_hist])
    pool(lambda h: h.memset(uhist[:], 0.0), [], [b_uhist])
    pool(lambda h: h.memset(stg4[:], 0.0), [], [b_stg4])

    GAIN_IDX = {"ffn1_pre_g": 0, "ffn1_post_g": 1, "mix_pre_g": 2, "mix_post_g": 3, "ffn2_pre_g": 4, "ffn2_post_g": 5}
    cs1 = xin[0][0]
    cs2 = xin[1][0]
    b_cs1, b_cs2 = xin[0][1], xin[1][1]
    pool(lambda h: h.memset(cs2[:, 0:128], 0.0), [], [b_cs2])
    dma(lambda h: h.dma_start(out=cs1[:, 0:128], in_=conv_w.ap().rearrange("k (m p) -> (k m) p", p=128)), [], [b_cs1], "xin0")
    for nm, gi in GAIN_IDX.items():
        dma(lambda h, nm=nm, gi=gi: h.dma_start(out=cs2[gi * 8:(gi + 1) * 8, 0:128], in_=V[nm].ap().rearrange("(k p) -> k p", p=128)),
            [], [b_cs2], "xin1")
    dma(lambda h: h.dma_start(out=cs2[64:96, 0:128], in_=V["conv_b"].ap().rearrange("(k p) -> k p", p=128)), [], [b_cs2], "xin1")
    dma(lambda h: h.dma_start(out=cs2[96:112, 0:128], in_=V["ssd_norm_g"].ap().rearrange("(k p) -> k p", p=128)), [], [b_cs2], "xin1")
    dma(lambda h: h.dma_start(out=cs2[112:120, 0:128], in_=V["pool_scale"].ap().rearrange("(k p) -> k p", p=128)), [], [b_cs2], "xin1")
    bt, bb = nbank()
    pe(lambda h: h.transpose(out=bt[:, 0:128], in_=cs1[:, 0:128], identity=ident[:]), [b_cs1, b_ident], [bb], inc=False)
    pe(lambda h: h.transpose(out=bt[:, 128:256], in_=cs2[:, 0:128], identity=ident[:]), [b_cs2, b_ident], [bb], inc=True)
    act(lambda h: h.copy(out=cw[:], in_=bt[:, 0:128].rearrange("p (k m) -> p k m", k=4)), [bb], [b_cw])
    act(lambda h: h.copy(out=gains[:], in_=bt[:, 128:176].rearrange("p (g k) -> p g k", g=6)), [bb], [b_gains])
    act(lambda h: h.copy(out=cb[:], in_=bt[:, 192:224]), [bb], [b_cb])
    act(lambda h: h.copy(out=gssd[:], in_=bt[:, 224:240]), [bb], [b_gssd])
    act(lambda h: h.copy(out=pscale[:], in_=bt[:, 240:248]), [bb], [b_pscale])

    def pbcast(src):
        return bass.AP(tensor=src, offset=0, ap=[[0, 128], [1, NH]])

    dma(lambda h: h.dma_start(out=dtbias[:], in_=pbcast(V["dt_bias"])), [], [b_dtbias], "c_dtbias")
    dma(lambda h: h.dma_start(out=abc[:], in_=pbcast(V["a_log"])), [], [b_abc], "c_abc")
    dma(lambda h: h.dma_start(out=dsk[:], in_=pbcast(V["d_skip"])), [], [b_dsk], "c_dsk")
    for ci, (gi_, cc_) in enumerate(((1, 0.5), (3, 1.0), (5, 0.5))):
        dve(lambda h, ci=ci, gi_=gi_, cc_=cc_: h.tensor_scalar(out=gpost[:, ci, :], in0=gains[:, gi_, :], scalar1=cc_, scalar2=None, op0=ALU.mult),
            [b_gains], [b_gpost])
    for two in range(2):
        dve(lambda h, two=two: h.tensor_copy(out=dskf[two * 64:(two + 1) * 64, :],
                                             in_=dsk[two * 64:(two + 1) * 64, :].rearrange("p (j t) -> p j t", t=2)[:, :, two]),
            [b_dsk], [b_dskf])
    for j in range(16):
        dve(lambda h, j=j: h.tensor_scalar(out=Dg[:, j, :], in0=identb[:], scalar1=dskf[:, j:j + 1], scalar2=None, op0=ALU.mult),
            [b_identb, b_dskf], [b_Dg])
    act(lambda h: h.activation(out=abc[:], in_=abc[:], func=AF.Exp), [b_abc], [b_abc])
    dve(lambda h: h.tensor_scalar(out=abc[:], in0=abc[:], scalar1=-1.0, scalar2=None, op0=ALU.mult), [b_abc], [b_abc])

    NST = 3
    stg32 = [bigA[:, 0:8192].bitcast(F32), bigA[:, 8192:16384].bitcast(F32), xT0[:].rearrange("p a b -> p (a b)")]
    stg16 = [bigB[:, 0:4096], bigB[:, 4096:8192], xn[:].rearrange("p a b -> p (a b)")]
    b_stg32 = [Buf(f"stg32_{i}") for i in range(NST)]
    b_stg16 = [Buf(f"stg16_{i}") for i in range(NST)]
    blk_elems = []

    def pl_load(bi):
        bname, parts = blocks[bi]
        s = bi % NST
        off = 0
        for (wname, nkc, c0, ncol) in parts:
            n = nkc * ncol
            src = W[wname].ap()[:, c0:c0 + ncol].rearrange("(k p) n -> p k n", p=128)
            dst = stg32[s][:, off:off + n].rearrange("p (k n) -> p k n", k=nkc)
            dma(lambda h, src=src, dst=dst: h.dma_start(out=dst, in_=src), [], [b_stg32[s]], f"pl{s}")
            off += n
        blk_elems.append(off)

    def pl_cast_store(bi):
        s = bi % NST
        off = blk_elems[bi]
        cut = (off * 5 // 9) // 2 * 2
        o16 = stg16[s][:, 0:off]
        if blocks[bi][0].startswith("ps"):
            for kc in range(16):
                lo = kc * 256
                if kc % 2 == 0:
                    act(lambda h, s=s, lo=lo, kc=kc: h.activation(out=stg16[s][:, lo:lo + 256], in_=stg32[s][:, lo:lo + 256], func=AF.Copy,
                                                                  scale=gssd[:, kc:kc + 1]), [b_stg32[s], b_gssd], [b_stg16[s]])
                else:
                    dve(lambda h, s=s, lo=lo, kc=kc: h.tensor_scalar(out=stg16[s][:, lo:lo + 256], in0=stg32[s][:, lo:lo + 256],
                                                                     scalar1=gssd[:, kc:kc + 1], scalar2=None, op0=ALU.mult),
                        [b_stg32[s], b_gssd], [b_stg16[s]])
        else:
            act(lambda h, s=s, cut=cut: h.copy(out=stg16[s][:, 0:cut], in_=stg32[s][:, 0:cut]), [b_stg32[s]], [b_stg16[s]])
            dve(lambda h, s=s, cut=cut, off=off: h.tensor_copy(out=stg16[s][:, cut:off], in_=stg32[s][:, cut:off]), [b_stg32[s]], [b_stg16[s]])
        dma(lambda h, bi=bi, o=o16, off=off: h.dma_start(out=wsc.ap()[bi, :, 0:off], in_=o), [b_stg16[s]], [], f"ps{s}")

    for bi in range(min(NST - 1, NBLK)):
        pl_load(bi)
    for bi in range(NBLK):
        if bi + NST - 1 < NBLK:
            pl_load(bi + NST - 1)
        pl_cast_store(bi)
    b_wsc = Buf("wsc")
    b_wsc.w = {f"ps{i}": P.cnt[f"ps{i}"] for i in range(NST)}
    for b in RA + XS[0][1] + XS[1][1] + [b_xn] + ALL_GT:
        for s in range(NST):
            for src in (b_stg32[s], b_stg16[s]):
                for k, v in list(src.w.items()) + list(src.r.items()):
                    if b.r.get(k, 0) < v:
                        b.r[k] = v

    blk_index = {nm: i for i, (nm, _) in enumerate(blocks)}
    n_tiles_total = NSEQ * (SEQ // TILE) + (1 if cfg.SAMPLE else 0)
    total_stream = n_tiles_total * NBLK
    ring_state = {"next": 0, "free": list(range(NSLOT)), "slot_of": {}}

    def ring_prefetch():
        while ring_state["free"] and ring_state["next"] < total_stream:
            q = ring_state["next"]
            ring_state["next"] += 1
            s = ring_state["free"].pop(0)
            ring_state["slot_of"][q] = s
            bi = q % NBLK
            n = blk_elems[bi]
            rt, rb = ring[s]
            dma(lambda h, rt=rt, bi=bi, n=n: h.dma_start(out=rt[:, 0:n], in_=wsc.ap()[bi, :, 0:n]), [b_wsc], [rb], f"ring{s}")

    def ring_get(tile_idx, name):
        q = tile_idx * NBLK + blk_index[name]
        while q not in ring_state["slot_of"]:
            assert ring_state["free"], f"weight ring deadlock at {name}"
            ring_prefetch()
        s = ring_state["slot_of"][q]
        return ring[s][0], ring[s][1], q

    def ring_release(q):
        s = ring_state["slot_of"].pop(q)
        ring_state["free"].append(s)
        ring_prefetch()

    rstd_ps = {}

    def keep_warm(dep_bufs, T):
        if T < 512:
            return
        bt, bb = nbank()
        for i in range(3):
            pe(lambda h, i=i: h.matmul(bt[:, 0:512], lhsT=onesb[:], rhs=hTb[:, 0:512], start=True, stop=True),
               [b_onesb, b_hTbg[0], b_hTbg[1]] + dep_bufs, [bb], inc=(i == 2))


    def rms_finish(T, sq_view, sq_base, to_psum):
        bt, bb = nbank()
        for kc in range(KC):
            pe(lambda h, kc=kc: h.matmul(bt[:, 0:T], lhsT=onesb[:], rhs=sq_view(kc), start=(kc == 0), stop=(kc == KC - 1)),
               [b_onesb] + AR(sq_base + kc), [bb], inc=(kc == KC - 1))
        act(lambda h: h.activation(out=rstd[:, 0:T], in_=bt[:, 0:T], func=AF.Ln, bias=EPS, scale=1.0 / D), [bb], [b_rstd])
        if to_psum:
            act(lambda h: h.activation(out=bt[:, 0:T], in_=rstd[:, 0:T], func=AF.Exp, scale=-0.5), [b_rstd], [bb])
        else:
            act(lambda h: h.activation(out=rstd[:, 0:T], in_=rstd[:, 0:T], func=AF.Exp, scale=-0.5), [b_rstd], [b_rstd])
        rstd_ps["t"], rstd_ps["b"] = bt, bb

    def norm_apply(gidx, T, X):
        xT, XT_B = X
        bt, bb = rstd_ps["t"], rstd_ps["b"]
        for kc in range(KC):
            dve(lambda h, kc=kc: h.scalar_tensor_tensor(out=xn[:, kc, 0:T], in0=xT[:, kc, 0:T], scalar=gains[:, gidx, kc:kc + 1],
                                                       in1=bt[:, 0:T], op0=ALU.mult, op1=ALU.mult),
                [XT_B[kc], b_gains, bb], [b_xn, b_tick[kc]])
            if kc % 2 == 1 and kc < KC - 1:
                keep_warm([b_tick[kc]], T)

    fout = bigB[:, 0:8192].bitcast(F32)

    def fview(m, T):
        return fout[:, m * TILE:m * TILE + T]

    def gt_for_f(m):
        lo = m * TILE * 2
        hi = lo + TILE * 2
        return [b_gt[c][g] for c in range(NCHM) for g in range(8) if c * 2048 + g * 256 < hi and c * 2048 + (g + 1) * 256 > lo]

    def sq_view_A(T):
        return lambda kc: bigA[:, 11264 + kc * TILE: 11264 + kc * TILE + T]

    def residual_update(cidx, T, next_sq, X):
        xT, XT_B = X
        for m in range(KC):
            if m % 4 == 3:
                tt, tb = tmpA[0]
                dve(lambda h, m=m, tt=tt: h.scalar_tensor_tensor(out=tt[:, 0:T], in0=fview(m, T), scalar=gpost[:, cidx, m:m + 1],
                                                                in1=rstd[:, 0:T], op0=ALU.mult, op1=ALU.mult),
                    gt_for_f(m) + [b_gpost, b_rstd], [tb])
                dve(lambda h, m=m, tt=tt: h.tensor_tensor(out=xT[:, m, 0:T], in0=tt[:, 0:T], in1=xT[:, m, 0:T], op=ALU.add),
                    [tb, XT_B[m]], [XT_B[m]])
            else:
                tt, tb = tmpP[m % 2]
                pool(lambda h, m=m, tt=tt: h.tensor_tensor(out=tt[:, 0:T], in0=fview(m, T), in1=rstd[:, 0:T], op=ALU.mult),
                     gt_for_f(m) + [b_rstd], [tb])
                dve(lambda h, m=m, tt=tt: h.scalar_tensor_tensor(out=xT[:, m, 0:T], in0=tt[:, 0:T], scalar=gpost[:, cidx, m:m + 1],
                                                                in1=xT[:, m, 0:T], op0=ALU.mult, op1=ALU.add),
                    [tb, XT_B[m], b_gpost], [XT_B[m]])
            if next_sq:
                act(lambda h, m=m: h.activation(out=bigA[:, (SQA + m) * TILE:(SQA + m) * TILE + T], in_=xT[:, m, 0:T], func=AF.Square),
                    [XT_B[m]], AR(SQA + m))
            if m % 2 == 1 and m < KC - 1:
                keep_warm([XT_B[m]], T)

    def ffn(tile_i, pfx, g_pre, T, X, prenorm, post, hook=None):
        sqv = sq_view_A(T)
        if prenorm:
            rms_finish(T, sqv, SQA, True)
            norm_apply(g_pre, T, X)
        hid = lambda hc: bigA[:, hc * TILE: hc * TILE + T]
        for j in range(11):
            if hook is not None and j == 3:
                hook()
            rt, rb, q = ring_get(tile_i, f"{pfx}_gu{j}")
            wv = rt[:, 0:4096].rearrange("p (a k n) -> p a k n", a=2, k=KC)
            for jj in range(2):
                hc = 2 * j + jj
                bg, bgb = nbank()
                mm_group(bg[:, 0:T], bgb, [(wv[:, 0, kc, jj * 128:(jj + 1) * 128], xn[:, kc, 0:T]) for kc in range(KC)], [rb, b_xn])
                bu, bub = nbank()
                mm_group(bu[:, 0:T], bub, [(wv[:, 1, kc, jj * 128:(jj + 1) * 128], xn[:, kc, 0:T]) for kc in range(KC)], [rb, b_xn])
                ta, tab = tmpA[hc % 2]
                act(lambda h, ta=ta, bg=bg: h.activation(out=ta[:, 0:T], in_=bg[:, 0:T], func=AF.Silu), [bgb], [tab])
                dve(lambda h, ta=ta, bu=bu, hc=hc: h.tensor_tensor(out=hid(hc), in0=ta[:, 0:T], in1=bu[:, 0:T], op=ALU.mult),
                    [tab, bub], AR(hc))
            ring_release(q)
        for m in range(KC):
            rt, rb, q = ring_get(tile_i, f"{pfx}_dn{m}")
            wv = rt[:, 0:FC * 128].rearrange("p (k n) -> p k n", k=FC)
            bt, bb = nbank()
            mm_group(bt[:, 0:T], bb, [(wv[:, kc, :], hid(kc)) for kc in range(FC)], [rb] + AR(0, FC))
            act(lambda h, m=m, bt=bt: h.copy(out=fview(m, T), in_=bt[:, 0:T]), [bb], gt_for_f(m))
            dve(lambda h, m=m, bt=bt: h.tensor_tensor(out=sqv(m), in0=bt[:, 0:T], in1=fview(m, T), op=ALU.mult), [bb] + gt_for_f(m), AR(SQA + m))
            ring_release(q)
        if post:
            rms_finish(T, sqv, SQA, False)
            residual_update((0 if pfx == "ffn1" else 2), T, pfx == "ffn1", X)

    def tok2feat_small(src_dram_rows, nrows_pad, width, dst_fn, dst_buf):
        nrows = src_dram_rows.shape[0]
        for p0 in range(0, width, 1024):
            dma(lambda h, p0=p0: h.dma_start(out=stg4[0:nrows, 0:1024], in_=src_dram_rows[:, p0:p0 + 1024]), [], [b_stg4], "stg4")
            for q0 in range(0, 8, 4):
                bt, bb = nbank()
                for mm in range(4):
                    ml = q0 + mm
                    pe(lambda h, ml=ml, mm=mm, bt=bt: h.transpose(out=bt[:, mm * nrows_pad:(mm + 1) * nrows_pad],
                                                               in_=stg4[0:nrows_pad, ml * 128:(ml + 1) * 128],
                                                               identity=ident[0:nrows_pad, 0:nrows_pad]),
                       [b_stg4, b_ident], [bb], inc=(mm == 3))
                m0 = p0 // 128 + q0
                act(lambda h, m0=m0, bt=bt: h.copy(out=dst_fn(m0), in_=bt[:, 0:4 * nrows_pad].rearrange("p (a b) -> p a b", a=4)),
                    [bb], [dst_buf])

    def feat2tok_small(src_fn, src_buf, nrows, nrows_pad, width, dst_dram):
        for p0 in range(0, width, 1024):
            for q0 in range(0, 8, 4):
                bt, bb = nbank()
                for mm in range(4):
                    m = p0 // 128 + q0 + mm
                    pe(lambda h, m=m, mm=mm, bt=bt: h.transpose(out=bt[0:nrows_pad, mm * 128:(mm + 1) * 128], in_=src_fn(m),
                                                               identity=ident[:]),
                       [src_buf, b_ident], [bb], inc=(mm == 3))
                act(lambda h, q0=q0, bt=bt: h.copy(out=stg4[0:nrows_pad, q0 * 128:(q0 + 4) * 128], in_=bt[0:nrows_pad, 0:512]),
                    [bb], [b_stg4])
            dma(lambda h, p0=p0: h.dma_start(out=dst_dram[:, p0:p0 + 1024], in_=stg4[0:nrows, 0:1024]), [b_stg4], [], "stg4o", eng="act")

    preloaded = set()

    def x_load(tile_i, x_rows, CH, c):
        if (tile_i, c) in preloaded:
            return
        preloaded.add((tile_i, c))
        xt_, xb_ = xin[c % 2]
        dma(lambda h: h.dma_start(out=xt_[0:CH, :], in_=x_rows[c * CH:(c + 1) * CH, :]), [], [xb_], f"xin{c % 2}")

    def head(tile_i, x_rows, T, CH, X):
        xT, XT_B = X
        NCH = T // CH
        for c in range(NCH):
            xt_, xb_ = xin[c % 2]
            x_load(tile_i, x_rows, CH, c)
            for half in range(2):
                bt, bb = nbank()
                for kk in range(4):
                    kc = half * 4 + kk
                    pe(lambda h, kc=kc, kk=kk, bt=bt, xt_=xt_: h.transpose(out=bt[:, kk * CH:(kk + 1) * CH],
                                                                        in_=xt_[0:CH, kc * 128:(kc + 1) * 128],
                                                                        identity=ident[0:CH, 0:CH]),
                       [xb_, b_ident], [bb], inc=(kk == 3))
                act(lambda h, half=half, bt=bt, c=c: h.copy(out=xT[:, half * 4:(half + 1) * 4, c * CH:(c + 1) * CH],
                                                          in_=bt[:, 0:4 * CH].rearrange("p (a b) -> p a b", a=4)),
                    [bb], XT_B[half * 4:half * 4 + 4])
                sq0 = half * 4 * TILE + c * CH
                dve(lambda h, half=half, bt=bt, c=c, sq0=sq0: h.tensor_tensor(
                        out=bigA[:, sq0:sq0 + 4 * TILE].rearrange("p (a b) -> p a b", a=4)[:, :, 0:CH],
                        in0=bt[:, 0:4 * CH].rearrange("p (a b) -> p a b", a=4),
                        in1=xT[:, half * 4:(half + 1) * 4, c * CH:(c + 1) * CH], op=ALU.mult),
                    [bb] + XT_B[half * 4:half * 4 + 4], AR(half * 4, 4))
        rms_finish(T, lambda kc: bigA[:, kc * TILE:kc * TILE + T], 0, True)
        norm_apply(0, T, X)

    def tail_a(T, X):
        rms_finish(T, sq_view_A(T), SQA, False)
        residual_update(2, T, False, X)

    def tail_b(y_rows, T, CH, X):
        xT, XT_B = X
        NCH = T // CH
        for c in range(NCH):
            yt_, yb_ = yout[c % 2]
            for half in range(2):
                bt, bb = nbank()
                for kk in range(4):
                    kc = half * 4 + kk
                    pe(lambda h, kc=kc, kk=kk, bt=bt, c=c: h.transpose(out=bt[0:CH, kk * 128:(kk + 1) * 128],
                                                                    in_=xT[:, kc, c * CH:(c + 1) * CH], identity=ident[:]),
                       [XT_B[kc], b_ident], [bb], inc=(kk == 3))
                act(lambda h, half=half, bt=bt, yt_=yt_: h.copy(out=yt_[0:CH, half * 512:(half + 1) * 512], in_=bt[0:CH, 0:512]),
                    [bb], [yb_])
            dma(lambda h, c=c, yt_=yt_: h.dma_start(out=y_rows[c * CH:(c + 1) * CH, :], in_=yt_[0:CH, :]), [yb_], [], f"yout{c % 2}", eng="act")

    def body(tile_i, T, CH, first, last, seq_out, nxt, X, hook):
        ffn(tile_i, "ffn1", 0, T, X, prenorm=False, post=True, hook=hook)
        mixer(tile_i, T, CH, first, last, seq_out, X)
        if nxt is not None:
            for c in range(min(2, nxt[2] // nxt[3])):
                x_load(nxt[0], nxt[1], nxt[3], c)
        ffn(tile_i, "ffn2", 4, T, X, prenorm=True, post=False)

    def mixer(tile_i, T, CH, first, last, seq_out, X):
        NCH = T // CH
        sqv = sq_view_A(T)
        rms_finish(T, sqv, SQA, True)
        norm_apply(2, T, X)
        xbc = lambda m: bigA[:, m * TILE: m * TILE + T]
        xbc_c = lambda m, c: bigA[:, m * TILE + c * CH: m * TILE + (c + 1) * CH]
        gtok = lambda c, lo, n: bigB[0:CH, c * 2048 + lo: c * 2048 + lo + n]
        pooled = lambda m: mixedS[:, m, 0:T]
        mixedv = lambda m: mixedS[:, m, 0:T]

        def pool_u(ub):
            rt, rb, q = ring_get(tile_i, f"u{ub}")
            wv = rt[:, 0:4096].rearrange("p (k n) -> p k n", k=KC)
            for mm in range(4):
                m = 4 * ub + mm
                gi = m // 2
                bt, bb = nbank()
                mm_group(bt[:, 0:T], bb, [(wv[:, kc, mm * 128:(mm + 1) * 128], xn[:, kc, 0:T]) for kc in range(KC)], [rb, b_xn])
                ut, ubuf = ust[m % 2]
                act(lambda h, ut=ut, bt=bt: h.copy(out=ut[:, 15:15 + T], in_=bt[:, 0:T]), [bb], [ubuf])
                pool(lambda h, ut=ut, m=m: h.tensor_copy(out=ut[:, 0:15], in_=uhist[:, m, 0:15]), [b_uhist], [ubuf])
                W_ = 15 + T
                src_t, src_b = ut, ubuf
                bufs2 = [tmpA[0], tmpA[1]]
                lo = 0
                for lvl in range(gi + 1):
                    sh = 2 ** lvl
                    lo = lo + sh
                    dt_, db_ = bufs2[lvl % 2]
                    pool(lambda h, dt_=dt_, src_t=src_t, sh=sh, lo=lo, W_=W_: h.tensor_tensor(out=dt_[:, lo:W_], in0=src_t[:, lo:W_],
                                                                                           in1=src_t[:, lo - sh:W_ - sh], op=ALU.add),
                         [src_b], [db_])
                    src_t, src_b = dt_, db_
                kk = float(2 ** (gi + 1))
                dve(lambda h, src_t=src_t, ut=ut, m=m, kk=kk: h.scalar_tensor_tensor(out=pooled(m), in0=src_t[:, 15:15 + T], scalar=1.0 / kk,
                                                                                   in1=ut[:, 15:15 + T], op0=ALU.mult, op1=ALU.subtract),
                    [src_b, ubuf], [b_mixedS])
                if first:
                    tt, tb = tmpP[0]
                    dve(lambda h, src_t=src_t, tt=tt, gi=gi: h.tensor_tensor(out=tt[:, 0:15], in0=src_t[:, 15:30], in1=invc[:, gi, 0:15], op=ALU.mult),
                        [src_b, b_invc], [tb])
                    dve(lambda h, tt=tt, ut=ut, m=m: h.tensor_tensor(out=pooled(m)[:, 0:15], in0=tt[:, 0:15], in1=ut[:, 15:30], op=ALU.subtract),
                        [tb, ubuf], [b_mixedS])
                pool(lambda h, ut=ut, m=m: h.tensor_copy(out=uhist[:, m, 0:15], in_=ut[:, T:T + 15]), [ubuf], [b_uhist])
            ring_release(q)

        def pool_pm():
            if last:
                feat2tok_small(lambda m: uhist[:, m, :], b_uhist, 15, 16, D, seq_out["pool"])
            rt, rb, q = ring_get(tile_i, "pm")
            wv = rt[:, 0:2048].rearrange("p (g k n) -> p g k n", g=4, k=2)
            for gi in range(4):
                grp = []
                for mo in range(2):
                    bt, bb = nbank()
                    mm_group(bt[:, 0:T], bb, [(wv[:, gi, kc, mo * 128:(mo + 1) * 128], pooled(2 * gi + kc)) for kc in range(2)], [rb, b_mixedS])
                    grp.append((bt, bb))
                for mo in range(2):
                    m = 2 * gi + mo
                    bt, bb = grp[mo]
                    act(lambda h, bt=bt, m=m: h.activation(out=mixedv(m), in_=bt[:, 0:T], func=AF.Copy, scale=pscale[:, m:m + 1]),
                        [bb, b_pscale], [b_mixedS])
            ring_release(q)

        def z_group(zb, c, wv, rb):
            bt, bb = nbank()
            mm_group(bt[0:CH, 0:512], bb, [(xn[:, kc, c * CH:(c + 1) * CH], wv[:, kc, :]) for kc in range(KC)], [rb, b_xn])
            act(lambda h: h.activation(out=gtok(c, zb * 512, 512), in_=bt[0:CH, 0:512], func=AF.Silu),
                [bb], [b_gt[c][2 * zb], b_gt[c][2 * zb + 1]])

        def conv_back(m):
            ta, tab = tmpA[m % 2]
            tp, tpb = tmpP[m % 2]
            pool(lambda h: h.tensor_tensor(out=tp[:, 0:T], in0=tp[:, 0:T], in1=ta[:, 0:T], op=ALU.add), [tpb, tab], [tpb])
            act(lambda h: h.activation(out=xbc(m), in_=tp[:, 0:T], func=AF.Silu), [tpb], AR(m))

        def xbc_chunk(m, wv, rb):
            mm = m % 4
            bt, bb = nbank()
            mm_group(bt[:, 0:T], bb, [(wv[:, kc, mm * 128:(mm + 1) * 128], xn[:, kc, 0:T]) for kc in range(KC)], [rb, b_xn])
            ta, tab = tmpA[m % 2]
            tp, tpb = tmpP[m % 2]
            act(lambda h: h.activation(out=ta[:, 0:T], in_=bt[:, 0:T], func=AF.Identity, bias=cb[:, m:m + 1], scale=cw[:, 3, m:m + 1]),
                [bb, b_cb, b_cw], [tab])
            dve(lambda h: h.scalar_tensor_tensor(out=ta[:, 0:2], in0=hist[:, m, 1:3], scalar=cw[:, 1, m:m + 1], in1=ta[:, 0:2],
                                                 op0=ALU.mult, op1=ALU.add), [b_hist, tab, b_cw], [tab])
            pool(lambda h: h.tensor_scalar(out=tp[:, 0:3], in0=hist[:, m, 0:3], scalar1=cw[:, 0, m:m + 1], scalar2=None, op0=ALU.mult),
                 [b_hist, b_cw], [tpb])
            dve(lambda h: h.scalar_tensor_tensor(out=tp[:, 0:1], in0=hist[:, m, 2:3], scalar=cw[:, 2, m:m + 1], in1=tp[:, 0:1],
                                                 op0=ALU.mult, op1=ALU.add), [b_hist, tpb, b_cw], [tpb])
            act(lambda h: h.activation(out=tp[:, 3:T], in_=bt[:, 0:T - 3], func=AF.Copy, scale=cw[:, 0, m:m + 1]), [bb, b_cw], [tpb])
            act(lambda h: h.copy(out=hist[:, m, 0:3], in_=bt[:, T - 3:T]), [bb], [b_hist])
            dve(lambda h: h.scalar_tensor_tensor(out=ta[:, 2:T], in0=bt[:, 0:T - 2], scalar=cw[:, 1, m:m + 1], in1=ta[:, 2:T],
                                                 op0=ALU.mult, op1=ALU.add), [bb, tab, b_cw], [tab])
            dve(lambda h: h.scalar_tensor_tensor(out=tp[:, 1:T], in0=bt[:, 0:T - 1], scalar=cw[:, 2, m:m + 1], in1=tp[:, 1:T],
                                                 op0=ALU.mult, op1=ALU.add), [bb, tpb, b_cw], [tpb])
            if m >= 1:
                conv_back(m - 1)

        dtst = {}

        def dt_a():
            rt, rb, q = ring_get(tile_i, "dt")
            wv = rt[:, 0:KC * 32].rearrange("p (k n) -> p k n", k=KC)
            for c in range(NCH):
                bt, bb = nbank()
                mm_group(bt[0:CH, 0:32], bb, [(xn[:, kc, c * CH:(c + 1) * CH], wv[:, kc, :]) for kc in range(KC)], [rb, b_xn])
                dve(lambda h, bt=bt, c=c: h.tensor_tensor(out=dtt[0:CH, c, :], in0=bt[0:CH, 0:32], in1=dtbias[0:CH, :], op=ALU.add),
                    [bb, b_dtbias], [b_dtt])
            ring_release(q)

        def dt_b():
            act(lambda h: h.activation(out=dtt[0:CH, 0:NCH, :], in_=dtt[0:CH, 0:NCH, :], func=AF.Exp), [b_dtt], [b_dtt])
            act(lambda h: h.activation(out=dtb[0:CH, 0:NCH, :], in_=dtt[0:CH, 0:NCH, :], func=AF.Ln, bias=1.0, scale=1.0), [b_dtt], [b_dtb])

        def dt_c():
            dve(lambda h: h.tensor_tensor(out=dta[0:CH, 0:NCH, :], in0=dtb[0:CH, 0:NCH, :], in1=bc_mid(abc[0:CH, :], NCH), op=ALU.mult),
                [b_dtb, b_abc], [b_dta])

        def dt_d():
            b2, bb2 = nbank()
            b3, bb3 = nbank()
            dtst["b2"], dtst["bb2"], dtst["b3"], dtst["bb3"] = b2, bb2, b3, bb3
            for c in range(NCH):
                pe(lambda h, c=c: h.matmul(b2[0:CH, c * 64:c * 64 + 32], lhsT=Lm[0:CH, 0:CH], rhs=dta[0:CH, c, :], start=True, stop=True),
                   [b_Lm, b_dta], [bb2], inc=False)
                pe(lambda h, c=c: h.matmul(b2[0:CH, c * 64 + 32:c * 64 + 64], lhsT=Um[0:CH, 0:CH], rhs=dta[0:CH, c, :], start=True, stop=True),
                   [b_Um, b_dta], [bb2], inc=(c == NCH - 1))
            for c in range(NCH):
                pe(lambda h, c=c: h.matmul(b3[:, c * 32:(c + 1) * 32], lhsT=onesf[0:CH, :], rhs=dta[0:CH, c, :], start=True, stop=True),
                   [b_onesf, b_dta], [bb3], inc=(c == NCH - 1))

        def dt_e():
            b2, bb2, b3, bb3 = dtst["b2"], dtst["bb2"], dtst["b3"], dtst["bb3"]
            act(lambda h: h.activation(out=e3[0:CH, 0:NCH, :], in_=b2[0:CH, 0:NCH * 64].rearrange("p (c n) -> p c n", c=NCH), func=AF.Exp), [bb2], [b_e3])
            act(lambda h: h.activation(out=dec[:, 0:NCH, :], in_=b3[:, 0:NCH * 32].rearrange("p (c n) -> p c n", c=NCH), func=AF.Exp), [bb3], [b_dec])

        def dt_f():
            dve(lambda h: h.tensor_tensor(out=wdt[0:CH, 0:NCH, :], in0=e3[0:CH, 0:NCH, 32:64], in1=dtb[0:CH, 0:NCH, :], op=ALU.mult),
                [b_e3, b_dtb], [b_wdt])

        DT_AT = {1: dt_a, 4: dt_b, 7: dt_c, 10: dt_d, 13: dt_e, 16: dt_f}

        for zp in range(4):
            rtz, rbz, qz = ring_get(tile_i, f"z{zp}")
            wvz = rtz[:, 0:4096].rearrange("p (k n) -> p k n", k=KC)
            zc = 0
            for half in range(2):
                rtx, rbx, qx = ring_get(tile_i, f"xbc{2 * zp + half}")
                wvx = rtx[:, 0:4096].rearrange("p (k n) -> p k n", k=KC)
                for mm in range(4):
                    xbc_chunk(4 * (2 * zp + half) + mm, wvx, rbx)
                    if 4 * (2 * zp + half) + mm in DT_AT:
                        DT_AT[4 * (2 * zp + half) + mm]()
                    if mm % 2 == 1 and zc < NCH:
                        z_group(zp, zc, wvz, rbz)
                        zc += 1
                ring_release(qx)
            while zc < NCH:
                z_group(zp, zc, wvz, rbz)
                zc += 1
            ring_release(qz)
        conv_back(31)
        if last:
            feat2tok_small(lambda m: hist[:, m, :], b_hist, 3, 4, CONVD, seq_out["conv"])
        iters = [(c, g) for c in range(NCH) for g in range(8)]
        nit = len(iters)
        STAGE_IDS = [0, 1, 15, 2, 3, 4, 5, 6]
        for step in range(nit + NSTG - 1):
            for pos in range(NSTG - 1, -1, -1):
                i = step - pos
                if 0 <= i < nit:
                    ssd_stage(STAGE_IDS[pos], i, iters[i][0], iters[i][1], CH, xbc_c, gtok)
        if last:
            for j0 in range(0, 16, 4):
                bt, bb = nbank()
                for jj in range(4):
                    j = j0 + jj
                    pe(lambda h, j=j, jj=jj, bt=bt: h.transpose(out=bt[:, jj * 128:(jj + 1) * 128], in_=hT[:, j * 128:(j + 1) * 128],
                                                              identity=ident[:]), b_hTg + [b_ident], [bb], inc=(jj == 3))
                yt_, yb_ = yout[(j0 // 4) % 2]
                act(lambda h, bt=bt, yt_=yt_: h.copy(out=yt_[:, 0:512], in_=bt[:, 0:512]), [bb], [yb_])
                dma(lambda h, j0=j0, yt_=yt_: h.dma_start(out=seq_out["ssm"][j0 * 128:(j0 + 4) * 128, :].rearrange("(a p) n -> p a n", p=128),
                                                          in_=yt_[:, 0:512].rearrange("p (a n) -> p a n", a=4)),
                    [yb_], [], f"yout{(j0 // 4) % 2}", eng="act")
        pool_u(0)
        pool_u(1)
        gybv = lambda m: bigA[:, 16 * TILE:32 * TILE].bitcast(F32)[:, m * TILE: m * TILE + T]
        merged = lambda m: bigA[:, m * TILE: m * TILE + T]
        sgt = [(rstd, b_rstd), tmpP[1]]

        def ynT_rhs(kcc):
            g_, j_ = kcc // 2, kcc % 2
            off = g_ * 256 + j_ * 128
            return bigB[:, 0:NCH * 2048].rearrange("p (c r) -> p c r", c=NCH)[:, :, off:off + CH]

        for j in range(2):
            rtg, rbg, qg = ring_get(tile_i, f"ga{j}")
            wg = rtg[:, 0:4096].rearrange("p (k n) -> p k n", k=KC)
            for j2 in range(2):
                rtp, rbp, qp = ring_get(tile_i, f"ps{2 * j + j2}")
                wp = rtp[:, 0:4096].rearrange("p (k n) -> p k n", k=16)
                for m2 in range(2):
                    mm = 2 * j2 + m2
                    m = 4 * j + mm
                    bg, bgb = nbank()
                    mm_group(bg[:, 0:T], bgb, [(wg[:, kc, mm * 128:(mm + 1) * 128], xn[:, kc, 0:T]) for kc in range(KC)], [rbg, b_xn])
                    by, byb = nbank()
                    mm_group(by[:, 0:T].rearrange("p (c l) -> p c l", c=NCH), byb,
                             [(wp[:, kc, m2 * 128:(m2 + 1) * 128], ynT_rhs(kc)) for kc in range(16)], [rbp] + ALL_GT)
                    ta, tab = sgt[m % 2]
                    act(lambda h, ta=ta, bg=bg: h.activation(out=ta[:, 0:T], in_=bg[:, 0:T], func=AF.Sigmoid), [bgb], [tab])
                    dve(lambda h, ta=ta, by=by, m=m: h.tensor_tensor(out=gybv(m), in0=ta[:, 0:T], in1=by[:, 0:T], op=ALU.mult),
                        [tab, byb], AR(16 + 2 * m, 2))
                ring_release(qp)
            ring_release(qg)
        pool_pm()
        for j in range(2):
            rtg, rbg, qg = ring_get(tile_i, f"gb{j}")
            rtp, rbp, qp = ring_get(tile_i, f"pp{j}")
            wg = rtg[:, 0:4096].rearrange("p (k n) -> p k n", k=KC)
            wp = rtp[:, 0:4096].rearrange("p (k n) -> p k n", k=KC)
            for mm in range(4):
                m = 4 * j + mm
                bg, bgb = nbank()
                mm_group(bg[:, 0:T], bgb, [(wg[:, kc, mm * 128:(mm + 1) * 128], xn[:, kc, 0:T]) for kc in range(KC)], [rbg, b_xn])
                by, byb = nbank()
                mm_group(by[:, 0:T], byb, [(wp[:, kc, mm * 128:(mm + 1) * 128], mixedv(kc)) for kc in range(KC)], [rbp, b_mixedS])
                ta, tab = tmpA[m % 2]
                tt, tb = tmpP[m % 2]
                act(lambda h, ta=ta, bg=bg: h.activation(out=ta[:, 0:T], in_=bg[:, 0:T], func=AF.Sigmoid), [bgb], [tab])
                dve(lambda h, ta=ta, by=by, tt=tt: h.tensor_tensor(out=tt[:, 0:T], in0=ta[:, 0:T], in1=by[:, 0:T], op=ALU.mult),
                    [tab, byb], [tb])
                pool(lambda h, tt=tt, m=m: h.tensor_tensor(out=merged(m), in0=tt[:, 0:T], in1=gybv(m), op=ALU.add),
                     [tb] + AR(16 + 2 * m, 2), AR(m))
            ring_release(qg)
            ring_release(qp)
        sqv2 = lambda kc: bigA[:, (8 + kc) * TILE:(8 + kc) * TILE + T]
        for j in range(2):
            rt, rb, q = ring_get(tile_i, f"wo{j}")
            wv = rt[:, 0:4096].rearrange("p (k n) -> p k n", k=KC)
            for mm in range(4):
                m = 4 * j + mm
                bt, bb = nbank()
                mm_group(bt[:, 0:T], bb, [(wv[:, kc, mm * 128:(mm + 1) * 128], merged(kc)) for kc in range(KC)], [rb] + AR(0, 8))
                act(lambda h, m=m, bt=bt: h.copy(out=fview(m, T), in_=bt[:, 0:T]), [bb], gt_for_f(m))
                dve(lambda h, m=m, bt=bt: h.tensor_tensor(out=sqv2(m), in0=bt[:, 0:T], in1=fview(m, T), op=ALU.mult), [bb] + gt_for_f(m), AR(SQ2 + m))
            ring_release(q)
        rms_finish(T, sqv2, SQ2, False)
        residual_update(1, T, True, X)

    def ssd_stage(k, i, c, g, CH, xbc_c, gtok):
        h4 = slice(4 * g, 4 * g + 4)
        xt_, xtb = xtok[i % 4]
        xw_, xwb = xw[i % 2]
        st_, stb = stm[i % 3]
        ud_, udb = ud[i % 2]
        e_, eb = eE[i % 2]
        mt_, mtb = mt[i % 2]
        t1_, t1b = t1[i % 2]
        yg_, ygb = yg[i % 4]
        sq_, sqb = ssq[i % 3]
        dg_, dgb = dgs[i % 2]
        ht_, htb = htmp[i % 2]
        ud3 = ud_[0:CH, 0:4 * CH].rearrange("p (a b) -> p a b", a=4)
        mt3 = mt_[0:CH, 0:4 * CH].rearrange("p (a b) -> p a b", a=4)
        e3v = e_[0:CH, 0:4 * CH].rearrange("p (a b) -> p a b", a=4)
        bA, bbA = banks[i % 2]
        bD, bbD = banks[2]
        bY, bbY = banks[4 + i % 2]
        bE, bbE = banks[(3, 6)[i % 2]]
        bF, bbF = banks[7]
        if k == 0:
            bAb = bA[:].bitcast(BF16)
            for ii, m in enumerate((2 * g, 2 * g + 1, 16 + g)):
                pe(lambda h, ii=ii, m=m: h.transpose(out=bAb[0:CH, ii * 128:(ii + 1) * 128], in_=xbc_c(m, c), identity=identb[:]),
                   AR(m) + [b_identb], [bbA], inc=False)
            pe(lambda h: h.matmul(bA[0:CH, 256:256 + CH], lhsT=xbc_c(16 + g, c), rhs=xbc_c(24 + g, c), start=True, stop=True), AR(16 + g) + AR(24 + g), [bbA])
            act(lambda h: h.copy(out=xt_[0:CH, 0:384], in_=bAb[0:CH, 0:384]), [bbA], [xtb])
            dve(lambda h: h.tensor_tensor(out=st_[0:CH, 0:CH], in0=bA[0:CH, 256:256 + CH], in1=Lm[0:CH, 0:CH], op=ALU.mult), [bbA, b_Lm, xtb], [stb])
            pool(lambda h: h.tensor_tensor(out=ud3, in0=bc_mid(Um[0:CH, 0:CH], 4), in1=bc_last(dta[0:CH, c, h4], CH), op=ALU.mult),
                 [b_Um, b_dta], [udb])
            x3 = xt_[0:CH, 0:256].rearrange("p (a b) -> p a b", a=4)
            pool(lambda h: h.tensor_tensor(out=xw_[0:CH, :].rearrange("p (a b) -> p a b", a=4), in0=x3, in1=bc_last(wdt[0:CH, c, h4], 64), op=ALU.mult),
                 [xtb, b_wdt], [xwb])
        elif k == 1:
            for hh in range(4):
                pe(lambda h, hh=hh: h.matmul(bD[0:CH, hh * CH:(hh + 1) * CH], lhsT=ud3[:, hh, :], rhs=Lm[0:CH, 0:CH], start=True, stop=True),
                   [udb, b_Lm], [bbD], inc=(hh == 3))
            pe(lambda h: h.matmul(bE[:, 0:256], lhsT=xt_[0:CH, 256:384], rhs=xw_[0:CH, :], start=True, stop=True), [xtb, xwb], [bbE])
            act(lambda h: h.activation(out=e_[0:CH, 0:4 * CH], in_=bD[0:CH, 0:4 * CH], func=AF.Exp), [bbD], [eb])
            hseg = hT[:, g * 256:(g + 1) * 256]
            pool(lambda h: h.tensor_tensor(out=ht_[:, :].rearrange("p (a b) -> p a b", a=4), in0=hseg.rearrange("p (a b) -> p a b", a=4),
                                           in1=bc_last(dec[:, c, h4], 64), op=ALU.mult), [b_hTg[g], b_dec], [htb])
        elif k == 15:
            hseg = hT[:, g * 256:(g + 1) * 256]
            pe(lambda h: h.matmul(bY[0:CH, 256:512], lhsT=xbc_c(24 + g, c), rhs=hTb[:, g * 256:(g + 1) * 256], start=True, stop=True),
               AR(24 + g) + [b_hTbg[g]], [bbY], inc=True)
            dve(lambda h: h.tensor_tensor(out=hseg, in0=ht_[:, :], in1=bE[:, 0:256], op=ALU.add), [htb, bbE], [b_hTg[g]])
            act(lambda h: h.copy(out=hTb[:, g * 256:(g + 1) * 256], in_=hseg), [b_hTg[g]], [b_hTbg[g]])
            for hh in range(4):
                dve(lambda h, hh=hh: h.scalar_tensor_tensor(out=mt3[:, hh, :], in0=e3v[:, hh, :], scalar=dtb[0:CH, c, 4 * g + hh:4 * g + hh + 1],
                                                           in1=st_[0:CH, 0:CH], op0=ALU.mult, op1=ALU.mult),
                    [eb, stb, b_dtb], [mtb])
        elif k == 2:
            for j in range(2):
                pe(lambda h, j=j: h.matmul(bY[0:CH, j * 128:(j + 1) * 128], lhsT=xbc_c(2 * g + j, c), rhs=Dg[:, 2 * g + j, :], start=True, stop=False),
                   AR(2 * g + j) + [b_Dg], [bbY], inc=False)
                for hh in (2 * j, 2 * j + 1):
                    pe(lambda h, hh=hh: h.matmul(bY[0:CH, hh * 64:(hh + 1) * 64], lhsT=mt3[:, hh, :], rhs=xt_[0:CH, hh * 64:(hh + 1) * 64], start=False, stop=(hh % 2 == 1)),
                       [mtb, xtb], [bbY], inc=(hh == 3))
            dve(lambda h: h.tensor_tensor(out=t1_[0:CH, :].rearrange("p (a b) -> p a b", a=4), in0=bY[0:CH, 256:512].rearrange("p (a b) -> p a b", a=4),
                                          in1=bc_last(e3[0:CH, c, h4], 64), op=ALU.mult), [bbY, b_e3], [t1b])
            dve(lambda h: h.tensor_tensor(out=t1_[0:CH, :], in0=bY[0:CH, 0:256], in1=t1_[0:CH, :], op=ALU.add), [bbY, t1b], [t1b])
        elif k == 3:
            pool(lambda h: h.tensor_tensor(out=yg_[0:CH, :], in0=t1_[0:CH, :], in1=gtok(c, g * 256, 256), op=ALU.mult), [t1b, b_gt[c][g]], [ygb])
            act(lambda h: h.activation(out=t1_[0:CH, :], in_=yg_[0:CH, :], func=AF.Square, accum_out=sq_[0:CH, 0:1]), [ygb], [t1b, sqb])
        elif k == 4:
            act(lambda h: h.activation(out=sq_[0:CH, 1:2], in_=sq_[0:CH, 0:1], func=AF.Ln, bias=EPS, scale=1.0 / 256), [sqb], [sqb])
            act(lambda h: h.activation(out=sq_[0:CH, 2:3], in_=sq_[0:CH, 1:2], func=AF.Exp, scale=-0.5), [sqb], [sqb])
        elif k == 5:
            act(lambda h: h.activation(out=dg_[0:CH, 0:CH], in_=identb[0:CH, 0:CH], func=AF.Copy, scale=sq_[0:CH, 2:3]), [b_identb, sqb], [dgb])
        else:
            for j in range(2):
                pe(lambda h, j=j: h.matmul(bF[:, j * CH:(j + 1) * CH], lhsT=yg_[0:CH, j * 128:(j + 1) * 128], rhs=dg_[0:CH, 0:CH], start=True, stop=True),
                   [ygb, dgb], [bbF], inc=(j == 1))
            off = c * 2048 + g * 256
            act(lambda h: h.copy(out=bigB[:, off:off + 256].rearrange("p (j l) -> p j l", j=2)[:, :, 0:CH],
                                 in_=bF[:, 0:2 * CH].rearrange("p (j l) -> p j l", j=2)), [bbF], [b_gt[c][g]])

    ntile_seq = SEQ // TILE
    descs = []
    for b in range(NSEQ):
        for ti in range(ntile_seq):
            r0 = b * SEQ + ti * TILE
            descs.append(dict(kind="p", b=b, ti=ti, x=x_prompt.ap()[r0:r0 + TILE, :], y=y_prompt.ap()[r0:r0 + TILE, :], T=TILE, CH=128))
    if cfg.SAMPLE:
        descs.append(dict(kind="s", x=x_sample.ap(), y=y_sample.ap(), T=DSEQ, CH=DSEQ))
    def sample_init():
        tok2feat_small(cache_conv.ap(), 4, CONVD, lambda m0: hist[:, m0:m0 + 4, :], b_hist)
        tok2feat_small(cache_pool.ap(), 16, D, lambda m0: uhist[:, m0:m0 + 4, :], b_uhist)
        for j0 in range(0, 16, 4):
            yt_, yb_ = yout[(j0 // 4) % 2]
            dma(lambda h, j0=j0, yt_=yt_: h.dma_start(out=yt_[:, 0:512].rearrange("p (a n) -> p a n", a=4),
                                                      in_=state_ssm.ap()[j0 * 128:(j0 + 4) * 128, :].rearrange("(a p) n -> p a n", p=128)),
                [], [yb_], f"yout{(j0 // 4) % 2}")
            bt, bb = nbank()
            for jj in range(4):
                pe(lambda h, jj=jj, bt=bt, yt_=yt_: h.transpose(out=bt[:, jj * 128:(jj + 1) * 128], in_=yt_[:, jj * 128:(jj + 1) * 128], identity=ident[:]),
                   [yb_, b_ident], [bb], inc=(jj == 3))
            act(lambda h, j0=j0, bt=bt: h.copy(out=hT[:, j0 * 128:(j0 + 4) * 128], in_=bt[:, 0:512]), [bb], b_hTg)
        act(lambda h: h.copy(out=hTb[:], in_=hT[:]), b_hTg, b_hTbg)

    head(0, descs[0]["x"], descs[0]["T"], descs[0]["CH"], XS[0])
    hook = None
    for tile_i, d in enumerate(descs):
        X = XS[tile_i % 2]
        nd = descs[tile_i + 1] if tile_i + 1 < len(descs) else None
        nxt = (tile_i + 1, nd["x"], nd["T"], nd["CH"]) if nd is not None else None
        if d["kind"] == "p":
            b, ti = d["b"], d["ti"]
            if ti == 0:
                pool(lambda h: h.memset(hist[:], 0.0), [], [b_hist])
                pool(lambda h: h.memset(uhist[:], 0.0), [], [b_uhist])
                pool(lambda h: h.memset(hT[:], 0.0), [], b_hTg)
                pool(lambda h: h.memset(hTb[:], 0.0), [], b_hTbg)
            seq_out = {"conv": o_conv_p.ap()[b * 3:(b + 1) * 3, :], "ssm": o_ssm_p.ap()[b * DIN:(b + 1) * DIN, :],
                       "pool": o_pool_p.ap()[b * 15:(b + 1) * 15, :]}
            body(tile_i, TILE, 128, (ti == 0), (ti == ntile_seq - 1), seq_out, nxt, X, hook)
        else:
            sample_init()
            seq_out = {"conv": o_conv_s.ap(), "ssm": o_ssm_s.ap(), "pool": o_pool_s.ap()}
            body(tile_i, DSEQ, DSEQ, False, True, seq_out, None, X, hook)
        if nd is not None:
            head(tile_i + 1, nd["x"], nd["T"], nd["CH"], XS[(tile_i + 1) % 2])
        tail_a(d["T"], X)
        hook = (lambda d=d, X=X: tail_b(d["y"], d["T"], d["CH"], X))
    hook()

    P.replay(nc, es)
    es.close()
    return nc


_CFG = Cfg


def kernel(**inputs):
    cfg = _CFG
    NSEQ, SEQ = cfg.NSEQ, cfg.SEQ
    f32 = lambda a: np.ascontiguousarray(np.asarray(a, dtype=np.float32))
    shared = {}
    for k in WEIGHT_SHAPES:
        shared[k] = f32(inputs[k]).reshape(WEIGHT_SHAPES[k])
    for k, n in VEC_SHAPES.items():
        shared[k] = f32(inputs[k]).reshape(n)
    shared["conv_w"] = f32(inputs["conv_w"]).reshape(4, CONVD)
    xp = f32(inputs["x_prompt"])
    xs = f32(inputs["x_sample"])
    cc = f32(inputs["cache_conv"])[0]
    ss = f32(inputs["state_ssm"])[0]
    cp = f32(inputs["cache_pool"])[0]
    in_maps = []
    for c in range(NCORES):
        m = dict(shared)
        m["x_prompt"] = xp[c * NSEQ:(c + 1) * NSEQ].reshape(NSEQ * SEQ, D)
        m["x_sample"] = xs[c].reshape(cfg.DSEQ, D)
        m["cache_conv"] = cc[c].reshape(3, CONVD)
        m["state_ssm"] = ss[c].reshape(DIN, 128)
        m["cache_pool"] = cp[c].reshape(15, D)
        in_maps.append(m)
    nc = build_program(cfg)
    res = run_bass_kernel_spmd(nc, in_maps, core_ids=list(range(NCORES)))
    R = res.results
    B = NCORES * NSEQ
    y_p = np.concatenate([R[c]["y_prompt"].reshape(NSEQ, SEQ, D) for c in range(NCORES)], axis=0)
    y_s = np.stack([R[c]["y_sample"].reshape(cfg.DSEQ, D) for c in range(NCORES)], axis=0)
    conv_p = np.concatenate([R[c]["new_conv_prompt"].reshape(NSEQ, 3, CONVD) for c in range(NCORES)], axis=0)[None]
    ssm_p = np.concatenate([R[c]["new_ssm_prompt"].reshape(NSEQ, NH, 64, 128) for c in range(NCORES)], axis=0)[None]
    pool_p = np.concatenate([R[c]["new_pool_prompt"].reshape(NSEQ, 15, D) for c in range(NCORES)], axis=0)[None]
    conv_s = np.stack([R[c]["new_conv_sample"].reshape(3, CONVD) for c in range(NCORES)], axis=0)[None]
    ssm_s = np.stack([R[c]["new_ssm_sample"].reshape(NH, 64, 128) for c in range(NCORES)], axis=0)[None]
    pool_s = np.stack([R[c]["new_pool_sample"].reshape(15, D) for c in range(NCORES)], axis=0)[None]
    return (y_p.astype(np.float32), y_s.astype(np.float32), conv_p.astype(np.float32), ssm_p.astype(np.float32),
            pool_p.astype(np.float32), conv_s.astype(np.float32), ssm_s.astype(np.float32), pool_s.astype(np.float32))
```

```python
import numpy as np
from collections import defaultdict
from contextlib import ExitStack

import concourse.bass as bass
import concourse.mybir as mybir
from concourse.bass_utils import run_bass_kernel_spmd

F32 = mybir.dt.float32
BF16 = mybir.dt.bfloat16
I32 = mybir.dt.int32
AF = mybir.ActivationFunctionType
ALU = mybir.AluOpType

D = 1024
KC = 8
DFF = 2816
FC = 22
DIN = 2048
CONVD = 4096
NH = 32
EPS = 1e-6
NCORES = 8

ENGS = ["pe", "act", "dve", "pool", "sp"]
SELF_SYNC = ("act", "dve", "pool")
SAME_ENGINE_NEAR = 2


class Cfg:
    NSEQ = 4
    SEQ = 2048
    TILE = 512
    DSEQ = 16
    SAMPLE = True


class Buf:
    __slots__ = ("name", "w", "r", "excl")

    def __init__(self, name, excl=False):
        self.name = name
        self.w = {}
        self.r = {}
        self.excl = excl


class Prog:
    def __init__(self):
        self.entries = {e: [] for e in ENGS}
        self.cnt = defaultdict(int)
        self.seen = {e: {} for e in ENGS}
        self.keys = set(ENGS)
        self.final_dma = {}

    def op(self, eng, fn, reads=(), writes=(), inc=True, dma=None):
        deps = {}
        for b in reads:
            for k, v in b.w.items():
                if deps.get(k, 0) < v:
                    deps[k] = v
            if b.excl:
                for k, v in b.r.items():
                    if k != eng and deps.get(k, 0) < v:
                        deps[k] = v
        far = self.cnt[eng] - SAME_ENGINE_NEAR
        for b in writes:
            for k, v in b.w.items():
                if (k != eng or v <= far) and deps.get(k, 0) < v:
                    deps[k] = v
            for k, v in b.r.items():
                if (k != eng or v <= far) and deps.get(k, 0) < v:
                    deps[k] = v
        waits = []
        seen = self.seen[eng]
        for k, v in deps.items():
            if k == eng and eng not in SELF_SYNC:
                continue
            if seen.get(k, 0) < v:
                waits.append((k, v))
                seen[k] = v
        if dma is not None:
            self.keys.add(dma)
            self.cnt[dma] += 16
            key, val = dma, self.cnt[dma]
            self.final_dma[dma] = val
        else:
            key = eng
            if inc:
                self.cnt[eng] += 1
                val = self.cnt[eng]
            else:
                val = self.cnt[eng] + 1
        self.entries[eng].append((waits, fn, inc, dma))
        for b in writes:
            b.w = {key: val}
            b.r = {}
        for b in reads:
            if b.r.get(key, 0) < val:
                b.r[key] = val

    def replay(self, nc, es):
        sems = {}
        for k in sorted(self.keys):
            sems[k] = es.enter_context(nc.semaphore("s_" + k.replace(":", "_")))
        block = es.enter_context(nc.Block())
        entries = self.entries
        final_dma = self.final_dma

        def run(eng_name, h):
            for waits, fn, inc, dma in entries[eng_name]:
                for k, v in waits:
                    h.wait_ge(sems[k], v)
                inst = fn(h)
                if dma is not None:
                    inst.then_inc(sems[dma], 16)
                elif inc:
                    inst.then_inc(sems[eng_name], 1)

        @block.tensor
        def _(h):
            run("pe", h)

        @block.scalar
        def _(h):
            run("act", h)

        @block.vector
        def _(h):
            run("dve", h)

        @block.gpsimd
        def _(h):
            run("pool", h)

        @block.sync
        def _(h):
            run("sp", h)
            for k, v in final_dma.items():
                h.wait_ge(sems[k], v)


def bc_last(ap, n):
    sh = list(ap.shape)
    return ap.unsqueeze(len(sh)).broadcast_to(sh + [n])


def bc_mid(ap, n):
    sh = list(ap.shape)
    return ap.unsqueeze(1).broadcast_to([sh[0], n] + sh[1:])


def weight_block_defs():
    blocks = []

    def ffn(pfx):
        for j in range(11):
            blocks.append((f"{pfx}_gu{j}", [(f"{pfx}_w_gate", KC, 256 * j, 256), (f"{pfx}_w_up", KC, 256 * j, 256)]))
        for m in range(8):
            blocks.append((f"{pfx}_dn{m}", [(f"{pfx}_w_down", FC, 128 * m, 128)]))

    ffn("ffn1")
    for kind, j in MIX_ORDER:
        if kind == "z":
            blocks.append((f"z{j}", [("w_in", KC, 512 * j, 512)]))
        elif kind == "x":
            blocks.append((f"xbc{j}", [("w_in", KC, 2048 + 512 * j, 512)]))
        else:
            blocks.append(("dt", [("w_in", KC, 6144, 32)]))
    for j in range(2):
        blocks.append((f"u{j}", [("w_in", KC, 6176 + 512 * j, 512)]))
    for j in range(2):
        blocks.append((f"ga{j}", [("w_in", KC, 7200 + 512 * j, 512)]))
        blocks.append((f"ps{2 * j}", [("w_proj_ssd", 16, 256 * (2 * j), 256)]))
        blocks.append((f"ps{2 * j + 1}", [("w_proj_ssd", 16, 256 * (2 * j + 1), 256)]))
    blocks.append(("pm", [("pool_mix", 8, 0, 256)]))
    for j in range(2):
        blocks.append((f"gb{j}", [("w_in", KC, 8224 + 512 * j, 512)]))
        blocks.append((f"pp{j}", [("w_proj_pool", KC, 512 * j, 512)]))
    for j in range(2):
        blocks.append((f"wo{j}", [("w_out", KC, 512 * j, 512)]))
    ffn("ffn2")
    return blocks


WEIGHT_SHAPES = {
    "ffn1_w_gate": (D, DFF), "ffn1_w_up": (D, DFF), "ffn1_w_down": (DFF, D),
    "ffn2_w_gate": (D, DFF), "ffn2_w_up": (D, DFF), "ffn2_w_down": (DFF, D),
    "w_in": (D, 9248), "w_proj_ssd": (DIN, D), "pool_mix": (1024, 256),
    "w_proj_pool": (D, D), "w_out": (D, D),
}
VEC_SHAPES = {
    "ffn1_pre_g": D, "ffn1_post_g": D, "mix_pre_g": D, "mix_post_g": D, "ffn2_pre_g": D, "ffn2_post_g": D,
    "conv_b": CONVD, "dt_bias": NH, "a_log": NH, "d_skip": NH, "ssd_norm_g": DIN, "pool_scale": D,
}
MIX_ORDER = [("z", 0), ("x", 0), ("dt", 0), ("x", 1), ("z", 1), ("x", 2), ("x", 3), ("z", 2), ("x", 4), ("x", 5),
             ("z", 3), ("x", 6), ("x", 7)]
RING_ELEMS = 4096
NSLOT = 4


def build_program(cfg):
    nc = bass.Bass("TRN2", target_bir_lowering=False)
    es = ExitStack()
    P = Prog()
    NSEQ, SEQ, TILE, DSEQ = cfg.NSEQ, cfg.SEQ, cfg.TILE, cfg.DSEQ
    TILE = min(TILE, SEQ)
    assert SEQ % TILE == 0 and TILE % 128 == 0

    def din(name, shape):
        return nc.dram_tensor(name, list(shape), F32, kind="ExternalInput")

    def dout(name, shape):
        return nc.dram_tensor(name, list(shape), F32, kind="ExternalOutput")

    x_prompt = din("x_prompt", [NSEQ * SEQ, D])
    x_sample = din("x_sample", [DSEQ, D])
    cache_conv = din("cache_conv", [3, CONVD])
    state_ssm = din("state_ssm", [DIN, 128])
    cache_pool = din("cache_pool", [15, D])
    conv_w = din("conv_w", [4, CONVD])
    W = {k: din(k, v) for k, v in WEIGHT_SHAPES.items()}
    V = {k: din(k, [v]) for k, v in VEC_SHAPES.items()}
    y_prompt = dout("y_prompt", [NSEQ * SEQ, D])
    y_sample = dout("y_sample", [DSEQ, D])
    o_conv_p = dout("new_conv_prompt", [NSEQ * 3, CONVD])
    o_ssm_p = dout("new_ssm_prompt", [NSEQ * DIN, 128])
    o_pool_p = dout("new_pool_prompt", [NSEQ * 15, D])
    o_conv_s = dout("new_conv_sample", [3, CONVD])
    o_ssm_s = dout("new_ssm_sample", [DIN, 128])
    o_pool_s = dout("new_pool_sample", [15, D])

    blocks = weight_block_defs()
    NBLK = len(blocks)
    wsc = nc.dram_tensor("wsc", [NBLK, 128, RING_ELEMS], BF16, kind="Internal")

    def sb(name, shape, dt=F32):
        t = es.enter_context(nc.sbuf_tensor(name, list(shape), dt))
        return t, Buf(name)

    ring = [sb(f"ring{i}", [128, RING_ELEMS], BF16) for i in range(NSLOT)]
    xT0, _u0 = sb("xT0", [128, KC, TILE])
    xT1, _u1 = sb("xT1", [128, KC, TILE])
    XS = [(xT0, [Buf(f"xT0_{k}") for k in range(KC)]), (xT1, [Buf(f"xT1_{k}") for k in range(KC)])]
    xT = xT0
    xin = [sb(f"xin{i}", [128, D]) for i in range(2)]
    yout = [sb(f"yout{i}", [128, D]) for i in range(2)]
    xn, b_xn = sb("xn", [128, KC, TILE], BF16)
    bigA, _b_bigA_unused = sb("bigA", [128, 16384], BF16)
    RA = [Buf(f"A{i}") for i in range(16384 // TILE)]

    def AR(lo, n=1):
        return RA[lo:lo + n]
    SQA = 11264 // TILE if TILE == 512 else 22
    SQ2 = 8
    bigB, _b_bigB_unused = sb("bigB", [128, 8192], BF16)
    rstd, b_rstd = sb("rstd", [128, TILE])
    tmpA = [sb(f"tmpA{i}", [128, TILE + 16]) for i in range(2)]
    tmpP = [sb(f"tmpP{i}", [128, TILE]) for i in range(2)]
    mixedS, b_mixedS = sb("mixedS", [128, KC, TILE], BF16)
    ust = [sb(f"ust{i}", [128, 15 + TILE]) for i in range(2)]
    hist, b_hist = sb("hist", [128, 32, 4])
    uhist, b_uhist = sb("uhist", [128, 8, 16])
    hT, _b_hT_unused = sb("hT", [128, DIN])
    hTb, _b_hTb_unused = sb("hTb", [128, DIN], BF16)
    NCHM = TILE // 128
    dtb, b_dtb = sb("dtb", [128, NCHM, 32])
    dta, b_dta = sb("dta", [128, NCHM, 32])
    e3, b_e3 = sb("e3", [128, NCHM, 64])
    wdt, b_wdt = sb("wdt", [128, NCHM, 32])
    dec, b_dec = sb("dec", [128, NCHM, 32])
    dtt, b_dtt = sb("dtt", [128, NCHM, 32])
    NSTG = 8
    xtok = [sb(f"xtok{i}", [128, 384], BF16) for i in range(4)]
    xw = [sb(f"xw{i}", [128, 256], BF16) for i in range(2)]
    stm = [sb(f"stm{i}", [128, 128]) for i in range(3)]
    ud = [sb(f"ud{i}", [128, 512]) for i in range(2)]
    eE = [sb(f"eE{i}", [128, 512], BF16) for i in range(2)]
    mt = [sb(f"mt{i}", [128, 512], BF16) for i in range(2)]
    t1 = [sb(f"t1_{i}", [128, 256]) for i in range(2)]
    yg = [sb(f"yg{i}", [128, 256], BF16) for i in range(4)]
    ssq = [sb(f"ssq{i}", [128, 4]) for i in range(3)]
    dgs = [sb(f"dgs{i}", [128, 128], BF16) for i in range(2)]
    htmp = [sb(f"htmp{i}", [128, 256]) for i in range(2)]
    b_hTg = [Buf(f"hT_g{g}") for g in range(8)]
    b_hTbg = [Buf(f"hTb_g{g}") for g in range(8)]
    b_gt = [[Buf(f"gt_{c}_{g}") for g in range(8)] for c in range(NCHM)]
    ALL_GT = [b for row in b_gt for b in row]
    ident, b_ident = sb("ident", [128, 128])
    identb, b_identb = sb("identb", [128, 128], BF16)
    onesf, b_onesf = sb("onesf", [128, 128])
    onesb, b_onesb = sb("onesb", [128, 128], BF16)
    Lm, b_Lm = sb("Lm", [128, 128])
    Um, b_Um = sb("Um", [128, 128])
    gains, b_gains = sb("gains", [128, 6, KC])
    gssd, b_gssd = sb("gssd", [128, 16])
    pscale, b_pscale = sb("pscale", [128, KC])
    cw, b_cw = sb("cw", [128, 4, 32])
    cb, b_cb = sb("cb", [128, 32])
    dtbias, b_dtbias = sb("dtbias", [128, 32])
    abc, b_abc = sb("abc", [128, 32])
    dsk, b_dsk = sb("dsk", [128, 32])
    invc, b_invc = sb("invc", [128, 4, 16])
    gpost, b_gpost = sb("gpost", [128, 3, KC])
    dskf, b_dskf = sb("dskf", [128, 16])
    Dg, b_Dg = sb("Dg", [128, 16, 128], BF16)
    ioti, b_ioti = sb("ioti", [128, 16], I32)
    stg4, b_stg4 = sb("stg4", [16, 1024])

    NBANK = 8
    banks = []
    for i in range(NBANK):
        t = es.enter_context(nc.psum_tensor(f"pb{i}", [128, 512], F32))
        banks.append((t, Buf(f"pb{i}", excl=True)))
    bank_rr = [0]

    def nbank():
        i = bank_rr[0]
        bank_rr[0] = (i + 1) % NBANK
        return banks[i]

    CONSTS = [b_ident, b_identb, b_onesf, b_onesb, b_Lm, b_Um, b_gains, b_gssd, b_pscale, b_cw, b_cb,
              b_dtbias, b_abc, b_dsk, b_invc]

    rr = {"cast": 0}

    def act(fn, reads, writes):
        P.op("act", fn, reads, writes)

    def dve(fn, reads, writes):
        P.op("dve", fn, reads, writes)

    def pool(fn, reads, writes):
        P.op("pool", fn, reads, writes)

    def pe(fn, reads, writes, inc=True):
        P.op("pe", fn, reads, writes, inc=inc)

    def dma(fn, reads, writes, key, eng="sp"):
        P.op(eng, fn, reads, writes, dma=key)

    def mm_group(out_ap, bank_buf, pairs, reads):
        n = len(pairs)
        for i, (l, r) in enumerate(pairs):
            pe(lambda h, l=l, r=r, i=i: h.matmul(out_ap, lhsT=l, rhs=r, start=(i == 0), stop=(i == n - 1)),
               reads, [bank_buf], inc=(i == n - 1))

    pool(lambda h: h.memset(ident[:], 0.0), [], [b_ident])
    pool(lambda h: h.affine_select(out=ident[:], in_=ident[:], pattern=[[-1, 128]], base=0, channel_multiplier=1,
                                   compare_op=ALU.not_equal, fill=1.0), [b_ident], [b_ident])
    pool(lambda h: h.tensor_copy(out=identb[:], in_=ident[:]), [b_ident], [b_identb])
    pool(lambda h: h.memset(onesf[:], 1.0), [], [b_onesf])
    pool(lambda h: h.memset(onesb[:], 1.0), [], [b_onesb])
    pool(lambda h: h.affine_select(out=Lm[:], in_=onesf[:], pattern=[[1, 128]], base=0, channel_multiplier=-1,
                                   compare_op=ALU.is_ge, fill=0.0), [b_onesf], [b_Lm])
    pool(lambda h: h.affine_select(out=Um[:], in_=onesf[:], pattern=[[-1, 128]], base=0, channel_multiplier=1,
                                   compare_op=ALU.is_gt, fill=0.0), [b_onesf], [b_Um])
    pool(lambda h: h.iota(ioti[:], pattern=[[1, 16]], base=1, channel_multiplier=0), [], [b_ioti])
    pool(lambda h: h.tensor_copy(out=invc[:, 0, :], in_=ioti[:]), [b_ioti], [b_invc])
    for gi in range(1, 4):
        pool(lambda h, gi=gi: h.tensor_copy(out=invc[:, gi, :], in_=invc[:, 0, :]), [b_invc], [b_invc])
    for gi in range(4):
        dve(lambda h, gi=gi: h.tensor_scalar(out=invc[:, gi, :], in0=invc[:, gi, :], scalar1=float(2 ** (gi + 1)),
                                             scalar2=None, op0=ALU.min), [b_invc], [b_invc])
    dve(lambda h: h.reciprocal(out=invc[:], in_=invc[:]), [b_invc], [b_invc])
    pool(lambda h: h.memset(hist[:], 0.0), [], [b_hist])
    pool(lambda h: h.memset(uhist[:], 0.0), [], [b_uhist])
    pool(lambda h: h.memset(stg4[:], 0.0), [], [b_stg4])

    GAIN_IDX = {"ffn1_pre_g": 0, "ffn1_post_g": 1, "mix_pre_g": 2, "mix_post_g": 3, "ffn2_pre_g": 4, "ffn2_post_g": 5}
    cs1 = xin[0][0]
    cs2 = xin[1][0]
    b_cs1, b_cs2 = xin[0][1], xin[1][1]
    pool(lambda h: h.memset(cs2[:, 0:128], 0.0), [], [b_cs2])
    dma(lambda h: h.dma_start(out=cs1[:, 0:128], in_=conv_w.ap().rearrange("k (m p) -> (k m) p", p=128)), [], [b_cs1], "xin0")
    for nm, gi in GAIN_IDX.items():
        dma(lambda h, nm=nm, gi=gi: h.dma_start(out=cs2[gi * 8:(gi + 1) * 8, 0:128], in_=V[nm].ap().rearrange("(k p) -> k p", p=128)),
            [], [b_cs2], "xin1")
    dma(lambda h: h.dma_start(out=cs2[64:96, 0:128], in_=V["conv_b"].ap().rearrange("(k p) -> k p", p=128)), [], [b_cs2], "xin1")
    dma(lambda h: h.dma_start(out=cs2[96:112, 0:128], in_=V["ssd_norm_g"].ap().rearrange("(k p) -> k p", p=128)), [], [b_cs2], "xin1")
    dma(lambda h: h.dma_start(out=cs2[112:120, 0:128], in_=V["pool_scale"].ap().rearrange("(k p) -> k p", p=128)), [], [b_cs2], "xin1")
    bt, bb = nbank()
    pe(lambda h: h.transpose(out=bt[:, 0:128], in_=cs1[:, 0:128], identity=ident[:]), [b_cs1, b_ident], [bb], inc=False)
    pe(lambda h: h.transpose(out=bt[:, 128:256], in_=cs2[:, 0:128], identity=ident[:]), [b_cs2, b_ident], [bb], inc=True)
    act(lambda h: h.copy(out=cw[:], in_=bt[:, 0:128].rearrange("p (k m) -> p k m", k=4)), [bb], [b_cw])
    act(lambda h: h.copy(out=gains[:], in_=bt[:, 128:176].rearrange("p (g k) -> p g k", g=6)), [bb], [b_gains])
    act(lambda h: h.copy(out=cb[:], in_=bt[:, 192:224]), [bb], [b_cb])
    act(lambda h: h.copy(out=gssd[:], in_=bt[:, 224:240]), [bb], [b_gssd])
    act(lambda h: h.copy(out=pscale[:], in_=bt[:, 240:248]), [bb], [b_pscale])

    def pbcast(src):
        return bass.AP(tensor=src, offset=0, ap=[[0, 128], [1, NH]])

    dma(lambda h: h.dma_start(out=dtbias[:], in_=pbcast(V["dt_bias"])), [], [b_dtbias], "c_dtbias")
    dma(lambda h: h.dma_start(out=abc[:], in_=pbcast(V["a_log"])), [], [b_abc], "c_abc")
    dma(lambda h: h.dma_start(out=dsk[:], in_=pbcast(V["d_skip"])), [], [b_dsk], "c_dsk")
    for ci, (gi_, cc_) in enumerate(((1, 0.5), (3, 1.0), (5, 0.5))):
        dve(lambda h, ci=ci, gi_=gi_, cc_=cc_: h.tensor_scalar(out=gpost[:, ci, :], in0=gains[:, gi_, :], scalar1=cc_, scalar2=None, op0=ALU.mult),
            [b_gains], [b_gpost])
    for two in range(2):
        dve(lambda h, two=two: h.tensor_copy(out=dskf[two * 64:(two + 1) * 64, :],
                                             in_=dsk[two * 64:(two + 1) * 64, :].rearrange("p (j t) -> p j t", t=2)[:, :, two]),
            [b_dsk], [b_dskf])
    for j in range(16):
        dve(lambda h, j=j: h.tensor_scalar(out=Dg[:, j, :], in0=identb[:], scalar1=dskf[:, j:j + 1], scalar2=None, op0=ALU.mult),
            [b_identb, b_dskf], [b_Dg])
    act(lambda h: h.activation(out=abc[:], in_=abc[:], func=AF.Exp), [b_abc], [b_abc])
    dve(lambda h: h.tensor_scalar(out=abc[:], in0=abc[:], scalar1=-1.0, scalar2=None, op0=ALU.mult), [b_abc], [b_abc])

    NST = 3
    stg32 = [bigA[:, 0:8192].bitcast(F32), bigA[:, 8192:16384].bitcast(F32), xT0[:].rearrange("p a b -> p (a b)")]
    stg16 = [bigB[:, 0:4096], bigB[:, 4096:8192], xn[:].rearrange("p a b -> p (a b)")]
    b_stg32 = [Buf(f"stg32_{i}") for i in range(NST)]
    b_stg16 = [Buf(f"stg16_{i}") for i in range(NST)]
    blk_elems = []

    def pl_load(bi):
        bname, parts = blocks[bi]
        s = bi % NST
        off = 0
        for (wname, nkc, c0, ncol) in parts:
            n = nkc * ncol
            src = W[wname].ap()[:, c0:c0 + ncol].rearrange("(k p) n -> p k n", p=128)
            dst = stg32[s][:, off:off + n].rearrange("p (k n) -> p k n", k=nkc)
            dma(lambda h, src=src, dst=dst: h.dma_start(out=dst, in_=src), [], [b_stg32[s]], f"pl{s}")
            off += n
        blk_elems.append(off)

    def pl_cast_store(bi):
        s = bi % NST
        off = blk_elems[bi]
        cut = (off * 5 // 9) // 2 * 2
        o16 = stg16[s][:, 0:off]
        if blocks[bi][0].startswith("ps"):
            for kc in range(16):
                lo = kc * 256
                if kc % 2 == 0:
                    act(lambda h, s=s, lo=lo, kc=kc: h.activation(out=stg16[s][:, lo:lo + 256], in_=stg32[s][:, lo:lo + 256], func=AF.Copy,
                                                                  scale=gssd[:, kc:kc + 1]), [b_stg32[s], b_gssd], [b_stg16[s]])
                else:
                    dve(lambda h, s=s, lo=lo, kc=kc: h.tensor_scalar(out=stg16[s][:, lo:lo + 256], in0=stg32[s][:, lo:lo + 256],
                                                                     scalar1=gssd[:, kc:kc + 1], scalar2=None, op0=ALU.mult),
                        [b_stg32[s], b_gssd], [b_stg16[s]])
        else:
            act(lambda h, s=s, cut=cut: h.copy(out=stg16[s][:, 0:cut], in_=stg32[s][:, 0:cut]), [b_stg32[s]], [b_stg16[s]])
            dve(lambda h, s=s, cut=cut, off=off: h.tensor_copy(out=stg16[s][:, cut:off], in_=stg32[s][:, cut:off]), [b_stg32[s]], [b_stg16[s]])
        dma(lambda h, bi=bi, o=o16, off=off: h.dma_start(out=wsc.ap()[bi, :, 0:off], in_=o), [b_stg16[s]], [], f"ps{s}")

    for bi in range(min(NST - 1, NBLK)):
        pl_load(bi)
    for bi in range(NBLK):
        if bi + NST - 1 < NBLK:
            pl_load(bi + NST - 1)
        pl_cast_store(bi)
    b_wsc = Buf("wsc")
    b_wsc.w = {f"ps{i}": P.cnt[f"ps{i}"] for i in range(NST)}
    for b in RA + XS[0][1] + XS[1][1] + [b_xn] + ALL_GT:
        for s in range(NST):
            for src in (b_stg32[s], b_stg16[s]):
                for k, v in list(src.w.items()) + list(src.r.items()):
                    if b.r.get(k, 0) < v:
                        b.r[k] = v

    blk_index = {nm: i for i, (nm, _) in enumerate(blocks)}
    n_tiles_total = NSEQ * (SEQ // TILE) + (1 if cfg.SAMPLE else 0)
    total_stream = n_tiles_total * NBLK
    ring_state = {"next": 0, "free": list(range(NSLOT)), "slot_of": {}}

    def ring_prefetch():
        while ring_state["free"] and ring_state["next"] < total_stream:
            q = ring_state["next"]
            ring_state["next"] += 1
            s = ring_state["free"].pop(0)
            ring_state["slot_of"][q] = s
            bi = q % NBLK
            n = blk_elems[bi]
            rt, rb = ring[s]
            dma(lambda h, rt=rt, bi=bi, n=n: h.dma_start(out=rt[:, 0:n], in_=wsc.ap()[bi, :, 0:n]), [b_wsc], [rb], f"ring{s}")

    def ring_get(tile_idx, name):
        q = tile_idx * NBLK + blk_index[name]
        while q not in ring_state["slot_of"]:
            assert ring_state["free"], f"weight ring deadlock at {name}"
            ring_prefetch()
        s = ring_state["slot_of"][q]
        return ring[s][0], ring[s][1], q

    def ring_release(q):
        s = ring_state["slot_of"].pop(q)
        ring_state["free"].append(s)
        ring_prefetch()

    rstd_ps = {}

    def rms_finish(T, sq_view, sq_base, to_psum):
        bt, bb = nbank()
        for kc in range(KC):
            pe(lambda h, kc=kc: h.matmul(bt[:, 0:T], lhsT=onesb[:], rhs=sq_view(kc), start=(kc == 0), stop=(kc == KC - 1)),
               [b_onesb] + AR(sq_base + kc), [bb], inc=(kc == KC - 1))
        act(lambda h: h.activation(out=rstd[:, 0:T], in_=bt[:, 0:T], func=AF.Ln, bias=EPS, scale=1.0 / D), [bb], [b_rstd])
        if to_psum:
            act(lambda h: h.activation(out=bt[:, 0:T], in_=rstd[:, 0:T], func=AF.Exp, scale=-0.5), [b_rstd], [bb])
        else:
            act(lambda h: h.activation(out=rstd[:, 0:T], in_=rstd[:, 0:T], func=AF.Exp, scale=-0.5), [b_rstd], [b_rstd])
        rstd_ps["t"], rstd_ps["b"] = bt, bb

    def norm_apply(gidx, T, X):
        xT, XT_B = X
        bt, bb = rstd_ps["t"], rstd_ps["b"]
        for kc in range(KC):
            dve(lambda h, kc=kc: h.scalar_tensor_tensor(out=xn[:, kc, 0:T], in0=xT[:, kc, 0:T], scalar=gains[:, gidx, kc:kc + 1],
                                                       in1=bt[:, 0:T], op0=ALU.mult, op1=ALU.mult),
                [XT_B[kc], b_gains, bb], [b_xn])

    fout = bigB[:, 0:8192].bitcast(F32)

    def fview(m, T):
        return fout[:, m * TILE:m * TILE + T]

    def gt_for_f(m):
        lo = m * TILE * 2
        hi = lo + TILE * 2
        return [b_gt[c][g] for c in range(NCHM) for g in range(8) if c * 2048 + g * 256 < hi and c * 2048 + (g + 1) * 256 > lo]

    def sq_view_A(T):
        return lambda kc: bigA[:, 11264 + kc * TILE: 11264 + kc * TILE + T]

    def residual_update(cidx, T, next_sq, X):
        xT, XT_B = X
        for m in range(KC):
            tt, tb = tmpP[m % 2]
            pool(lambda h, m=m, tt=tt: h.tensor_tensor(out=tt[:, 0:T], in0=fview(m, T), in1=rstd[:, 0:T], op=ALU.mult),
                 gt_for_f(m) + [b_rstd], [tb])
            dve(lambda h, m=m, tt=tt: h.scalar_tensor_tensor(out=xT[:, m, 0:T], in0=tt[:, 0:T], scalar=gpost[:, cidx, m:m + 1],
                                                            in1=xT[:, m, 0:T], op0=ALU.mult, op1=ALU.add),
                [tb, XT_B[m], b_gpost], [XT_B[m]])
            if next_sq:
                act(lambda h, m=m: h.activation(out=bigA[:, (SQA + m) * TILE:(SQA + m) * TILE + T], in_=xT[:, m, 0:T], func=AF.Square),
                    [XT_B[m]], AR(SQA + m))

    def ffn(tile_i, pfx, g_pre, T, X, prenorm, post, hook=None):
        sqv = sq_view_A(T)
        if prenorm:
            rms_finish(T, sqv, SQA, True)
            norm_apply(g_pre, T, X)
        hid = lambda hc: bigA[:, hc * TILE: hc * TILE + T]
        for j in range(11):
            if hook is not None and j == 3:
                hook()
            rt, rb, q = ring_get(tile_i, f"{pfx}_gu{j}")
            wv = rt[:, 0:4096].rearrange("p (a k n) -> p a k n", a=2, k=KC)
            for jj in range(2):
                hc = 2 * j + jj
                bg, bgb = nbank()
                mm_group(bg[:, 0:T], bgb, [(wv[:, 0, kc, jj * 128:(jj + 1) * 128], xn[:, kc, 0:T]) for kc in range(KC)], [rb, b_xn])
                bu, bub = nbank()
                mm_group(bu[:, 0:T], bub, [(wv[:, 1, kc, jj * 128:(jj + 1) * 128], xn[:, kc, 0:T]) for kc in range(KC)], [rb, b_xn])
                ta, tab = tmpA[hc % 2]
                act(lambda h, ta=ta, bg=bg: h.activation(out=ta[:, 0:T], in_=bg[:, 0:T], func=AF.Silu), [bgb], [tab])
                dve(lambda h, ta=ta, bu=bu, hc=hc: h.tensor_tensor(out=hid(hc), in0=ta[:, 0:T], in1=bu[:, 0:T], op=ALU.mult),
                    [tab, bub], AR(hc))
            ring_release(q)
        for m in range(KC):
            rt, rb, q = ring_get(tile_i, f"{pfx}_dn{m}")
            wv = rt[:, 0:FC * 128].rearrange("p (k n) -> p k n", k=FC)
            bt, bb = nbank()
            mm_group(bt[:, 0:T], bb, [(wv[:, kc, :], hid(kc)) for kc in range(FC)], [rb] + AR(0, FC))
            act(lambda h, m=m, bt=bt: h.copy(out=fview(m, T), in_=bt[:, 0:T]), [bb], gt_for_f(m))
            dve(lambda h, m=m, bt=bt: h.tensor_tensor(out=sqv(m), in0=bt[:, 0:T], in1=fview(m, T), op=ALU.mult), [bb] + gt_for_f(m), AR(SQA + m))
            ring_release(q)
        if post:
            rms_finish(T, sqv, SQA, False)
            residual_update((0 if pfx == "ffn1" else 2), T, pfx == "ffn1", X)

    def tok2feat_small(src_dram_rows, nrows_pad, width, dst_fn, dst_buf):
        nrows = src_dram_rows.shape[0]
        for p0 in range(0, width, 1024):
            dma(lambda h, p0=p0: h.dma_start(out=stg4[0:nrows, 0:1024], in_=src_dram_rows[:, p0:p0 + 1024]), [], [b_stg4], "stg4")
            for q0 in range(0, 8, 4):
                bt, bb = nbank()
                for mm in range(4):
                    ml = q0 + mm
                    pe(lambda h, ml=ml, mm=mm, bt=bt: h.transpose(out=bt[:, mm * nrows_pad:(mm + 1) * nrows_pad],
                                                               in_=stg4[0:nrows_pad, ml * 128:(ml + 1) * 128],
                                                               identity=ident[0:nrows_pad, 0:nrows_pad]),
                       [b_stg4, b_ident], [bb], inc=(mm == 3))
                m0 = p0 // 128 + q0
                act(lambda h, m0=m0, bt=bt: h.copy(out=dst_fn(m0), in_=bt[:, 0:4 * nrows_pad].rearrange("p (a b) -> p a b", a=4)),
                    [bb], [dst_buf])

    def feat2tok_small(src_fn, src_buf, nrows, nrows_pad, width, dst_dram):
        for p0 in range(0, width, 1024):
            for q0 in range(0, 8, 4):
                bt, bb = nbank()
                for mm in range(4):
                    m = p0 // 128 + q0 + mm
                    pe(lambda h, m=m, mm=mm, bt=bt: h.transpose(out=bt[0:nrows_pad, mm * 128:(mm + 1) * 128], in_=src_fn(m),
                                                               identity=ident[:]),
                       [src_buf, b_ident], [bb], inc=(mm == 3))
                act(lambda h, q0=q0, bt=bt: h.copy(out=stg4[0:nrows_pad, q0 * 128:(q0 + 4) * 128], in_=bt[0:nrows_pad, 0:512]),
                    [bb], [b_stg4])
            dma(lambda h, p0=p0: h.dma_start(out=dst_dram[:, p0:p0 + 1024], in_=stg4[0:nrows, 0:1024]), [b_stg4], [], "stg4o", eng="act")

    preloaded = set()

    def x_load(tile_i, x_rows, CH, c):
        if (tile_i, c) in preloaded:
            return
        preloaded.add((tile_i, c))
        xt_, xb_ = xin[c % 2]
        dma(lambda h: h.dma_start(out=xt_[0:CH, :], in_=x_rows[c * CH:(c + 1) * CH, :]), [], [xb_], f"xin{c % 2}")

    def head(tile_i, x_rows, T, CH, X):
        xT, XT_B = X
        NCH = T // CH
        for c in range(NCH):
            xt_, xb_ = xin[c % 2]
            x_load(tile_i, x_rows, CH, c)
            for half in range(2):
                bt, bb = nbank()
                for kk in range(4):
                    kc = half * 4 + kk
                    pe(lambda h, kc=kc, kk=kk, bt=bt, xt_=xt_: h.transpose(out=bt[:, kk * CH:(kk + 1) * CH],
                                                                        in_=xt_[0:CH, kc * 128:(kc + 1) * 128],
                                                                        identity=ident[0:CH, 0:CH]),
                       [xb_, b_ident], [bb], inc=(kk == 3))
                act(lambda h, half=half, bt=bt, c=c: h.copy(out=xT[:, half * 4:(half + 1) * 4, c * CH:(c + 1) * CH],
                                                          in_=bt[:, 0:4 * CH].rearrange("p (a b) -> p a b", a=4)),
                    [bb], XT_B[half * 4:half * 4 + 4])
                sq0 = half * 4 * TILE + c * CH
                dve(lambda h, half=half, bt=bt, c=c, sq0=sq0: h.tensor_tensor(
                        out=bigA[:, sq0:sq0 + 4 * TILE].rearrange("p (a b) -> p a b", a=4)[:, :, 0:CH],
                        in0=bt[:, 0:4 * CH].rearrange("p (a b) -> p a b", a=4),
                        in1=xT[:, half * 4:(half + 1) * 4, c * CH:(c + 1) * CH], op=ALU.mult),
                    [bb] + XT_B[half * 4:half * 4 + 4], AR(half * 4, 4))
        rms_finish(T, lambda kc: bigA[:, kc * TILE:kc * TILE + T], 0, True)
        norm_apply(0, T, X)

    def tail_a(T, X):
        rms_finish(T, sq_view_A(T), SQA, False)
        residual_update(2, T, False, X)

    def tail_b(y_rows, T, CH, X):
        xT, XT_B = X
        NCH = T // CH
        for c in range(NCH):
            yt_, yb_ = yout[c % 2]
            for half in range(2):
                bt, bb = nbank()
                for kk in range(4):
                    kc = half * 4 + kk
                    pe(lambda h, kc=kc, kk=kk, bt=bt, c=c: h.transpose(out=bt[0:CH, kk * 128:(kk + 1) * 128],
                                                                    in_=xT[:, kc, c * CH:(c + 1) * CH], identity=ident[:]),
                       [XT_B[kc], b_ident], [bb], inc=(kk == 3))
                act(lambda h, half=half, bt=bt, yt_=yt_: h.copy(out=yt_[0:CH, half * 512:(half + 1) * 512], in_=bt[0:CH, 0:512]),
                    [bb], [yb_])
            dma(lambda h, c=c, yt_=yt_: h.dma_start(out=y_rows[c * CH:(c + 1) * CH, :], in_=yt_[0:CH, :]), [yb_], [], f"yout{c % 2}", eng="act")

    def body(tile_i, T, CH, first, last, seq_out, nxt, X, hook):
        ffn(tile_i, "ffn1", 0, T, X, prenorm=False, post=True, hook=hook)
        mixer(tile_i, T, CH, first, last, seq_out, X)
        if nxt is not None:
            for c in range(min(2, nxt[2] // nxt[3])):
                x_load(nxt[0], nxt[1], nxt[3], c)
        ffn(tile_i, "ffn2", 4, T, X, prenorm=True, post=False)

    def mixer(tile_i, T, CH, first, last, seq_out, X):
        NCH = T // CH
        sqv = sq_view_A(T)
        rms_finish(T, sqv, SQA, True)
        norm_apply(2, T, X)
        xbc = lambda m: bigA[:, m * TILE: m * TILE + T]
        xbc_c = lambda m, c: bigA[:, m * TILE + c * CH: m * TILE + (c + 1) * CH]
        gtok = lambda c, lo, n: bigB[0:CH, c * 2048 + lo: c * 2048 + lo + n]
        pooled = lambda m: mixedS[:, m, 0:T]
        mixedv = lambda m: mixedS[:, m, 0:T]

        def pool_u(ub):
            rt, rb, q = ring_get(tile_i, f"u{ub}")
            wv = rt[:, 0:4096].rearrange("p (k n) -> p k n", k=KC)
            for mm in range(4):
                m = 4 * ub + mm
                gi = m // 2
                bt, bb = nbank()
                mm_group(bt[:, 0:T], bb, [(wv[:, kc, mm * 128:(mm + 1) * 128], xn[:, kc, 0:T]) for kc in range(KC)], [rb, b_xn])
                ut, ubuf = ust[m % 2]
                act(lambda h, ut=ut, bt=bt: h.copy(out=ut[:, 15:15 + T], in_=bt[:, 0:T]), [bb], [ubuf])
                pool(lambda h, ut=ut, m=m: h.tensor_copy(out=ut[:, 0:15], in_=uhist[:, m, 0:15]), [b_uhist], [ubuf])
                W_ = 15 + T
                src_t, src_b = ut, ubuf
                bufs2 = [tmpA[0], tmpA[1]]
                lo = 0
                for lvl in range(gi + 1):
                    sh = 2 ** lvl
                    lo = lo + sh
                    dt_, db_ = bufs2[lvl % 2]
                    pool(lambda h, dt_=dt_, src_t=src_t, sh=sh, lo=lo, W_=W_: h.tensor_tensor(out=dt_[:, lo:W_], in0=src_t[:, lo:W_],
                                                                                           in1=src_t[:, lo - sh:W_ - sh], op=ALU.add),
                         [src_b], [db_])
                    src_t, src_b = dt_, db_
                kk = float(2 ** (gi + 1))
                dve(lambda h, src_t=src_t, ut=ut, m=m, kk=kk: h.scalar_tensor_tensor(out=pooled(m), in0=src_t[:, 15:15 + T], scalar=1.0 / kk,
                                                                                   in1=ut[:, 15:15 + T], op0=ALU.mult, op1=ALU.subtract),
                    [src_b, ubuf], [b_mixedS])
                if first:
                    tt, tb = tmpP[0]
                    dve(lambda h, src_t=src_t, tt=tt, gi=gi: h.tensor_tensor(out=tt[:, 0:15], in0=src_t[:, 15:30], in1=invc[:, gi, 0:15], op=ALU.mult),
                        [src_b, b_invc], [tb])
                    dve(lambda h, tt=tt, ut=ut, m=m: h.tensor_tensor(out=pooled(m)[:, 0:15], in0=tt[:, 0:15], in1=ut[:, 15:30], op=ALU.subtract),
                        [tb, ubuf], [b_mixedS])
                pool(lambda h, ut=ut, m=m: h.tensor_copy(out=uhist[:, m, 0:15], in_=ut[:, T:T + 15]), [ubuf], [b_uhist])
            ring_release(q)

        def pool_pm():
            if last:
                feat2tok_small(lambda m: uhist[:, m, :], b_uhist, 15, 16, D, seq_out["pool"])
            rt, rb, q = ring_get(tile_i, "pm")
            wv = rt[:, 0:2048].rearrange("p (g k n) -> p g k n", g=4, k=2)
            for gi in range(4):
                grp = []
                for mo in range(2):
                    bt, bb = nbank()
                    mm_group(bt[:, 0:T], bb, [(wv[:, gi, kc, mo * 128:(mo + 1) * 128], pooled(2 * gi + kc)) for kc in range(2)], [rb, b_mixedS])
                    grp.append((bt, bb))
                for mo in range(2):
                    m = 2 * gi + mo
                    bt, bb = grp[mo]
                    act(lambda h, bt=bt, m=m: h.activation(out=mixedv(m), in_=bt[:, 0:T], func=AF.Copy, scale=pscale[:, m:m + 1]),
                        [bb, b_pscale], [b_mixedS])
            ring_release(q)

        def z_group(zb, c, wv, rb):
            bt, bb = nbank()
            mm_group(bt[0:CH, 0:512], bb, [(xn[:, kc, c * CH:(c + 1) * CH], wv[:, kc, :]) for kc in range(KC)], [rb, b_xn])
            act(lambda h: h.activation(out=gtok(c, zb * 512, 512), in_=bt[0:CH, 0:512], func=AF.Silu),
                [bb], [b_gt[c][2 * zb], b_gt[c][2 * zb + 1]])

        def conv_back(m):
            ta, tab = tmpA[m % 2]
            tp, tpb = tmpP[m % 2]
            pool(lambda h: h.tensor_tensor(out=tp[:, 0:T], in0=tp[:, 0:T], in1=ta[:, 0:T], op=ALU.add), [tpb, tab], [tpb])
            act(lambda h: h.activation(out=xbc(m), in_=tp[:, 0:T], func=AF.Silu), [tpb], AR(m))

        def xbc_chunk(m, wv, rb):
            mm = m % 4
            bt, bb = nbank()
            mm_group(bt[:, 0:T], bb, [(wv[:, kc, mm * 128:(mm + 1) * 128], xn[:, kc, 0:T]) for kc in range(KC)], [rb, b_xn])
            ta, tab = tmpA[m % 2]
            tp, tpb = tmpP[m % 2]
            act(lambda h: h.activation(out=ta[:, 0:T], in_=bt[:, 0:T], func=AF.Identity, bias=cb[:, m:m + 1], scale=cw[:, 3, m:m + 1]),
                [bb, b_cb, b_cw], [tab])
            dve(lambda h: h.scalar_tensor_tensor(out=ta[:, 0:2], in0=hist[:, m, 1:3], scalar=cw[:, 1, m:m + 1], in1=ta[:, 0:2],
                                                 op0=ALU.mult, op1=ALU.add), [b_hist, tab, b_cw], [tab])
            pool(lambda h: h.tensor_scalar(out=tp[:, 0:3], in0=hist[:, m, 0:3], scalar1=cw[:, 0, m:m + 1], scalar2=None, op0=ALU.mult),
                 [b_hist, b_cw], [tpb])
            dve(lambda h: h.scalar_tensor_tensor(out=tp[:, 0:1], in0=hist[:, m, 2:3], scalar=cw[:, 2, m:m + 1], in1=tp[:, 0:1],
                                                 op0=ALU.mult, op1=ALU.add), [b_hist, tpb, b_cw], [tpb])
            act(lambda h: h.activation(out=tp[:, 3:T], in_=bt[:, 0:T - 3], func=AF.Copy, scale=cw[:, 0, m:m + 1]), [bb, b_cw], [tpb])
            act(lambda h: h.copy(out=hist[:, m, 0:3], in_=bt[:, T - 3:T]), [bb], [b_hist])
            dve(lambda h: h.scalar_tensor_tensor(out=ta[:, 2:T], in0=bt[:, 0:T - 2], scalar=cw[:, 1, m:m + 1], in1=ta[:, 2:T],
                                                 op0=ALU.mult, op1=ALU.add), [bb, tab, b_cw], [tab])
            dve(lambda h: h.scalar_tensor_tensor(out=tp[:, 1:T], in0=bt[:, 0:T - 1], scalar=cw[:, 2, m:m + 1], in1=tp[:, 1:T],
                                                 op0=ALU.mult, op1=ALU.add), [bb, tpb, b_cw], [tpb])
            if m >= 1:
                conv_back(m - 1)

        dtst = {}

        def dt_a():
            rt, rb, q = ring_get(tile_i, "dt")
            wv = rt[:, 0:KC * 32].rearrange("p (k n) -> p k n", k=KC)
            for c in range(NCH):
                bt, bb = nbank()
                mm_group(bt[0:CH, 0:32], bb, [(xn[:, kc, c * CH:(c + 1) * CH], wv[:, kc, :]) for kc in range(KC)], [rb, b_xn])
                dve(lambda h, bt=bt, c=c: h.tensor_tensor(out=dtt[0:CH, c, :], in0=bt[0:CH, 0:32], in1=dtbias[0:CH, :], op=ALU.add),
                    [bb, b_dtbias], [b_dtt])
            ring_release(q)

        def dt_b():
            act(lambda h: h.activation(out=dtt[0:CH, 0:NCH, :], in_=dtt[0:CH, 0:NCH, :], func=AF.Exp), [b_dtt], [b_dtt])
            act(lambda h: h.activation(out=dtb[0:CH, 0:NCH, :], in_=dtt[0:CH, 0:NCH, :], func=AF.Ln, bias=1.0, scale=1.0), [b_dtt], [b_dtb])

        def dt_c():
            dve(lambda h: h.tensor_tensor(out=dta[0:CH, 0:NCH, :], in0=dtb[0:CH, 0:NCH, :], in1=bc_mid(abc[0:CH, :], NCH), op=ALU.mult),
                [b_dtb, b_abc], [b_dta])

        def dt_d():
            b2, bb2 = nbank()
            b3, bb3 = nbank()
            dtst["b2"], dtst["bb2"], dtst["b3"], dtst["bb3"] = b2, bb2, b3, bb3
            for c in range(NCH):
                pe(lambda h, c=c: h.matmul(b2[0:CH, c * 64:c * 64 + 32], lhsT=Lm[0:CH, 0:CH], rhs=dta[0:CH, c, :], start=True, stop=True),
                   [b_Lm, b_dta], [bb2], inc=False)
                pe(lambda h, c=c: h.matmul(b2[0:CH, c * 64 + 32:c * 64 + 64], lhsT=Um[0:CH, 0:CH], rhs=dta[0:CH, c, :], start=True, stop=True),
                   [b_Um, b_dta], [bb2], inc=(c == NCH - 1))
            for c in range(NCH):
                pe(lambda h, c=c: h.matmul(b3[:, c * 32:(c + 1) * 32], lhsT=onesf[0:CH, :], rhs=dta[0:CH, c, :], start=True, stop=True),
                   [b_onesf, b_dta], [bb3], inc=(c == NCH - 1))

        def dt_e():
            b2, bb2, b3, bb3 = dtst["b2"], dtst["bb2"], dtst["b3"], dtst["bb3"]
            act(lambda h: h.activation(out=e3[0:CH, 0:NCH, :], in_=b2[0:CH, 0:NCH * 64].rearrange("p (c n) -> p c n", c=NCH), func=AF.Exp), [bb2], [b_e3])
            act(lambda h: h.activation(out=dec[:, 0:NCH, :], in_=b3[:, 0:NCH * 32].rearrange("p (c n) -> p c n", c=NCH), func=AF.Exp), [bb3], [b_dec])

        def dt_f():
            dve(lambda h: h.tensor_tensor(out=wdt[0:CH, 0:NCH, :], in0=e3[0:CH, 0:NCH, 32:64], in1=dtb[0:CH, 0:NCH, :], op=ALU.mult),
                [b_e3, b_dtb], [b_wdt])

        DT_AT = {1: dt_a, 4: dt_b, 7: dt_c, 10: dt_d, 13: dt_e, 16: dt_f}

        for zp in range(4):
            rtz, rbz, qz = ring_get(tile_i, f"z{zp}")
            wvz = rtz[:, 0:4096].rearrange("p (k n) -> p k n", k=KC)
            zc = 0
            for half in range(2):
                rtx, rbx, qx = ring_get(tile_i, f"xbc{2 * zp + half}")
                wvx = rtx[:, 0:4096].rearrange("p (k n) -> p k n", k=KC)
                for mm in range(4):
                    xbc_chunk(4 * (2 * zp + half) + mm, wvx, rbx)
                    if 4 * (2 * zp + half) + mm in DT_AT:
                        DT_AT[4 * (2 * zp + half) + mm]()
                    if mm % 2 == 1 and zc < NCH:
                        z_group(zp, zc, wvz, rbz)
                        zc += 1
                ring_release(qx)
            while zc < NCH:
                z_group(zp, zc, wvz, rbz)
                zc += 1
            ring_release(qz)
        conv_back(31)
        if last:
            feat2tok_small(lambda m: hist[:, m, :], b_hist, 3, 4, CONVD, seq_out["conv"])
        iters = [(c, g) for c in range(NCH) for g in range(8)]
        nit = len(iters)
        STAGE_IDS = [0, 1, 15, 2, 3, 4, 5, 6]
        for step in range(nit + NSTG - 1):
            for pos in range(NSTG - 1, -1, -1):
                i = step - pos
                if 0 <= i < nit:
                    ssd_stage(STAGE_IDS[pos], i, iters[i][0], iters[i][1], CH, xbc_c, gtok)
        if last:
            for j0 in range(0, 16, 4):
                bt, bb = nbank()
                for jj in range(4):
                    j = j0 + jj
                    pe(lambda h, j=j, jj=jj, bt=bt: h.transpose(out=bt[:, jj * 128:(jj + 1) * 128], in_=hT[:, j * 128:(j + 1) * 128],
                                                              identity=ident[:]), b_hTg + [b_ident], [bb], inc=(jj == 3))
                yt_, yb_ = yout[(j0 // 4) % 2]
                act(lambda h, bt=bt, yt_=yt_: h.copy(out=yt_[:, 0:512], in_=bt[:, 0:512]), [bb], [yb_])
                dma(lambda h, j0=j0, yt_=yt_: h.dma_start(out=seq_out["ssm"][j0 * 128:(j0 + 4) * 128, :].rearrange("(a p) n -> p a n", p=128),
                                                          in_=yt_[:, 0:512].rearrange("p (a n) -> p a n", a=4)),
                    [yb_], [], f"yout{(j0 // 4) % 2}", eng="act")
        pool_u(0)
        pool_u(1)
        gybv = lambda m: bigA[:, 16 * TILE:32 * TILE].bitcast(F32)[:, m * TILE: m * TILE + T]
        merged = lambda m: bigA[:, m * TILE: m * TILE + T]
        sgt = [(rstd, b_rstd), tmpP[1]]

        def ynT_rhs(kcc):
            g_, j_ = kcc // 2, kcc % 2
            off = g_ * 256 + j_ * 128
            return bigB[:, 0:NCH * 2048].rearrange("p (c r) -> p c r", c=NCH)[:, :, off:off + CH]

        for j in range(2):
            rtg, rbg, qg = ring_get(tile_i, f"ga{j}")
            wg = rtg[:, 0:4096].rearrange("p (k n) -> p k n", k=KC)
            for j2 in range(2):
                rtp, rbp, qp = ring_get(tile_i, f"ps{2 * j + j2}")
                wp = rtp[:, 0:4096].rearrange("p (k n) -> p k n", k=16)
                for m2 in range(2):
                    mm = 2 * j2 + m2
                    m = 4 * j + mm
                    bg, bgb = nbank()
                    mm_group(bg[:, 0:T], bgb, [(wg[:, kc, mm * 128:(mm + 1) * 128], xn[:, kc, 0:T]) for kc in range(KC)], [rbg, b_xn])
                    by, byb = nbank()
                    mm_group(by[:, 0:T].rearrange("p (c l) -> p c l", c=NCH), byb,
                             [(wp[:, kc, m2 * 128:(m2 + 1) * 128], ynT_rhs(kc)) for kc in range(16)], [rbp] + ALL_GT)
                    ta, tab = sgt[m % 2]
                    act(lambda h, ta=ta, bg=bg: h.activation(out=ta[:, 0:T], in_=bg[:, 0:T], func=AF.Sigmoid), [bgb], [tab])
                    dve(lambda h, ta=ta, by=by, m=m: h.tensor_tensor(out=gybv(m), in0=ta[:, 0:T], in1=by[:, 0:T], op=ALU.mult),
                        [tab, byb], AR(16 + 2 * m, 2))
                ring_release(qp)
            ring_release(qg)
        pool_pm()
        for j in range(2):
            rtg, rbg, qg = ring_get(tile_i, f"gb{j}")
            rtp, rbp, qp = ring_get(tile_i, f"pp{j}")
            wg = rtg[:, 0:4096].rearrange("p (k n) -> p k n", k=KC)
            wp = rtp[:, 0:4096].rearrange("p (k n) -> p k n", k=KC)
            for mm in range(4):
                m = 4 * j + mm
                bg, bgb = nbank()
                mm_group(bg[:, 0:T], bgb, [(wg[:, kc, mm * 128:(mm + 1) * 128], xn[:, kc, 0:T]) for kc in range(KC)], [rbg, b_xn])
                by, byb = nbank()
                mm_group(by[:, 0:T], byb, [(wp[:, kc, mm * 128:(mm + 1) * 128], mixedv(kc)) for kc in range(KC)], [rbp, b_mixedS])
                ta, tab = tmpA[m % 2]
                tt, tb = tmpP[m % 2]
                act(lambda h, ta=ta, bg=bg: h.activation(out=ta[:, 0:T], in_=bg[:, 0:T], func=AF.Sigmoid), [bgb], [tab])
                dve(lambda h, ta=ta, by=by, tt=tt: h.tensor_tensor(out=tt[:, 0:T], in0=ta[:, 0:T], in1=by[:, 0:T], op=ALU.mult),
                    [tab, byb], [tb])
                pool(lambda h, tt=tt, m=m: h.tensor_tensor(out=merged(m), in0=tt[:, 0:T], in1=gybv(m), op=ALU.add),
                     [tb] + AR(16 + 2 * m, 2), AR(m))
            ring_release(qg)
            ring_release(qp)
        sqv2 = lambda kc: bigA[:, (8 + kc) * TILE:(8 + kc) * TILE + T]
        for j in range(2):
            rt, rb, q = ring_get(tile_i, f"wo{j}")
            wv = rt[:, 0:4096].rearrange("p (k n) -> p k n", k=KC)
            for mm in range(4):
                m = 4 * j + mm
                bt, bb = nbank()
                mm_group(bt[:, 0:T], bb, [(wv[:, kc, mm * 128:(mm + 1) * 128], merged(kc)) for kc in range(KC)], [rb] + AR(0, 8))
                act(lambda h, m=m, bt=bt: h.copy(out=fview(m, T), in_=bt[:, 0:T]), [bb], gt_for_f(m))
                dve(lambda h, m=m, bt=bt: h.tensor_tensor(out=sqv2(m), in0=bt[:, 0:T], in1=fview(m, T), op=ALU.mult), [bb] + gt_for_f(m), AR(SQ2 + m))
            ring_release(q)
        rms_finish(T, sqv2, SQ2, False)
        residual_update(1, T, True, X)

    def ssd_stage(k, i, c, g, CH, xbc_c, gtok):
        h4 = slice(4 * g, 4 * g + 4)
        xt_, xtb = xtok[i % 4]
        xw_, xwb = xw[i % 2]
        st_, stb = stm[i % 3]
        ud_, udb = ud[i % 2]
        e_, eb = eE[i % 2]
        mt_, mtb = mt[i % 2]
        t1_, t1b = t1[i % 2]
        yg_, ygb = yg[i % 4]
        sq_, sqb = ssq[i % 3]
        dg_, dgb = dgs[i % 2]
        ht_, htb = htmp[i % 2]
        ud3 = ud_[0:CH, 0:4 * CH].rearrange("p (a b) -> p a b", a=4)
        mt3 = mt_[0:CH, 0:4 * CH].rearrange("p (a b) -> p a b", a=4)
        e3v = e_[0:CH, 0:4 * CH].rearrange("p (a b) -> p a b", a=4)
        bA, bbA = banks[i % 2]
        bD, bbD = banks[2]
        bY, bbY = banks[4 + i % 2]
        bE, bbE = banks[(3, 6)[i % 2]]
        bF, bbF = banks[7]
        if k == 0:
            bAb = bA[:].bitcast(BF16)
            for ii, m in enumerate((2 * g, 2 * g + 1, 16 + g)):
                pe(lambda h, ii=ii, m=m: h.transpose(out=bAb[0:CH, ii * 128:(ii + 1) * 128], in_=xbc_c(m, c), identity=identb[:]),
                   AR(m) + [b_identb], [bbA], inc=False)
            pe(lambda h: h.matmul(bA[0:CH, 256:256 + CH], lhsT=xbc_c(16 + g, c), rhs=xbc_c(24 + g, c), start=True, stop=True), AR(16 + g) + AR(24 + g), [bbA])
            act(lambda h: h.copy(out=xt_[0:CH, 0:384], in_=bAb[0:CH, 0:384]), [bbA], [xtb])
            dve(lambda h: h.tensor_tensor(out=st_[0:CH, 0:CH], in0=bA[0:CH, 256:256 + CH], in1=Lm[0:CH, 0:CH], op=ALU.mult), [bbA, b_Lm, xtb], [stb])
            pool(lambda h: h.tensor_tensor(out=ud3, in0=bc_mid(Um[0:CH, 0:CH], 4), in1=bc_last(dta[0:CH, c, h4], CH), op=ALU.mult),
                 [b_Um, b_dta], [udb])
            x3 = xt_[0:CH, 0:256].rearrange("p (a b) -> p a b", a=4)
            pool(lambda h: h.tensor_tensor(out=xw_[0:CH, :].rearrange("p (a b) -> p a b", a=4), in0=x3, in1=bc_last(wdt[0:CH, c, h4], 64), op=ALU.mult),
                 [xtb, b_wdt], [xwb])
        elif k == 1:
            for hh in range(4):
                pe(lambda h, hh=hh: h.matmul(bD[0:CH, hh * CH:(hh + 1) * CH], lhsT=ud3[:, hh, :], rhs=Lm[0:CH, 0:CH], start=True, stop=True),
                   [udb, b_Lm], [bbD], inc=(hh == 3))
            pe(lambda h: h.matmul(bE[:, 0:256], lhsT=xt_[0:CH, 256:384], rhs=xw_[0:CH, :], start=True, stop=True), [xtb, xwb], [bbE])
            act(lambda h: h.activation(out=e_[0:CH, 0:4 * CH], in_=bD[0:CH, 0:4 * CH], func=AF.Exp), [bbD], [eb])
            hseg = hT[:, g * 256:(g + 1) * 256]
            pool(lambda h: h.tensor_tensor(out=ht_[:, :].rearrange("p (a b) -> p a b", a=4), in0=hseg.rearrange("p (a b) -> p a b", a=4),
                                           in1=bc_last(dec[:, c, h4], 64), op=ALU.mult), [b_hTg[g], b_dec], [htb])
        elif k == 15:
            hseg = hT[:, g * 256:(g + 1) * 256]
            pe(lambda h: h.matmul(bY[0:CH, 256:512], lhsT=xbc_c(24 + g, c), rhs=hTb[:, g * 256:(g + 1) * 256], start=True, stop=True),
               AR(24 + g) + [b_hTbg[g]], [bbY], inc=True)
            dve(lambda h: h.tensor_tensor(out=hseg, in0=ht_[:, :], in1=bE[:, 0:256], op=ALU.add), [htb, bbE], [b_hTg[g]])
            act(lambda h: h.copy(out=hTb[:, g * 256:(g + 1) * 256], in_=hseg), [b_hTg[g]], [b_hTbg[g]])
            for hh in range(4):
                dve(lambda h, hh=hh: h.scalar_tensor_tensor(out=mt3[:, hh, :], in0=e3v[:, hh, :], scalar=dtb[0:CH, c, 4 * g + hh:4 * g + hh + 1],
                                                           in1=st_[0:CH, 0:CH], op0=ALU.mult, op1=ALU.mult),
                    [eb, stb, b_dtb], [mtb])
        elif k == 2:
            for j in range(2):
                pe(lambda h, j=j: h.matmul(bY[0:CH, j * 128:(j + 1) * 128], lhsT=xbc_c(2 * g + j, c), rhs=Dg[:, 2 * g + j, :], start=True, stop=False),
                   AR(2 * g + j) + [b_Dg], [bbY], inc=False)
                for hh in (2 * j, 2 * j + 1):
                    pe(lambda h, hh=hh: h.matmul(bY[0:CH, hh * 64:(hh + 1) * 64], lhsT=mt3[:, hh, :], rhs=xt_[0:CH, hh * 64:(hh + 1) * 64], start=False, stop=(hh % 2 == 1)),
                       [mtb, xtb], [bbY], inc=(hh == 3))
            dve(lambda h: h.tensor_tensor(out=t1_[0:CH, :].rearrange("p (a b) -> p a b", a=4), in0=bY[0:CH, 256:512].rearrange("p (a b) -> p a b", a=4),
                                          in1=bc_last(e3[0:CH, c, h4], 64), op=ALU.mult), [bbY, b_e3], [t1b])
            dve(lambda h: h.tensor_tensor(out=t1_[0:CH, :], in0=bY[0:CH, 0:256], in1=t1_[0:CH, :], op=ALU.add), [bbY, t1b], [t1b])
        elif k == 3:
            pool(lambda h: h.tensor_tensor(out=yg_[0:CH, :], in0=t1_[0:CH, :], in1=gtok(c, g * 256, 256), op=ALU.mult), [t1b, b_gt[c][g]], [ygb])
            act(lambda h: h.activation(out=t1_[0:CH, :], in_=yg_[0:CH, :], func=AF.Square, accum_out=sq_[0:CH, 0:1]), [ygb], [t1b, sqb])
        elif k == 4:
            act(lambda h: h.activation(out=sq_[0:CH, 1:2], in_=sq_[0:CH, 0:1], func=AF.Ln, bias=EPS, scale=1.0 / 256), [sqb], [sqb])
            act(lambda h: h.activation(out=sq_[0:CH, 2:3], in_=sq_[0:CH, 1:2], func=AF.Exp, scale=-0.5), [sqb], [sqb])
        elif k == 5:
            act(lambda h: h.activation(out=dg_[0:CH, 0:CH], in_=identb[0:CH, 0:CH], func=AF.Copy, scale=sq_[0:CH, 2:3]), [b_identb, sqb], [dgb])
        else:
            for j in range(2):
                pe(lambda h, j=j: h.matmul(bF[:, j * CH:(j + 1) * CH], lhsT=yg_[0:CH, j * 128:(j + 1) * 128], rhs=dg_[0:CH, 0:CH], start=True, stop=True),
                   [ygb, dgb], [bbF], inc=(j == 1))
            off = c * 2048 + g * 256
            act(lambda h: h.copy(out=bigB[:, off:off + 256].rearrange("p (j l) -> p j l", j=2)[:, :, 0:CH],
                                 in_=bF[:, 0:2 * CH].rearrange("p (j l) -> p j l", j=2)), [bbF], [b_gt[c][g]])

    ntile_seq = SEQ // TILE
    descs = []
    for b in range(NSEQ):
        for ti in range(ntile_seq):
            r0 = b * SEQ + ti * TILE
            descs.append(dict(kind="p", b=b, ti=ti, x=x_prompt.ap()[r0:r0 + TILE, :], y=y_prompt.ap()[r0:r0 + TILE, :], T=TILE, CH=128))
    if cfg.SAMPLE:
        descs.append(dict(kind="s", x=x_sample.ap(), y=y_sample.ap(), T=DSEQ, CH=DSEQ))
    def sample_init():
        tok2feat_small(cache_conv.ap(), 4, CONVD, lambda m0: hist[:, m0:m0 + 4, :], b_hist)
        tok2feat_small(cache_pool.ap(), 16, D, lambda m0: uhist[:, m0:m0 + 4, :], b_uhist)
        for j0 in range(0, 16, 4):
            yt_, yb_ = yout[(j0 // 4) % 2]
            dma(lambda h, j0=j0, yt_=yt_: h.dma_start(out=yt_[:, 0:512].rearrange("p (a n) -> p a n", a=4),
                                                      in_=state_ssm.ap()[j0 * 128:(j0 + 4) * 128, :].rearrange("(a p) n -> p a n", p=128)),
                [], [yb_], f"yout{(j0 // 4) % 2}")
            bt, bb = nbank()
            for jj in range(4):
                pe(lambda h, jj=jj, bt=bt, yt_=yt_: h.transpose(out=bt[:, jj * 128:(jj + 1) * 128], in_=yt_[:, jj * 128:(jj + 1) * 128], identity=ident[:]),
                   [yb_, b_ident], [bb], inc=(jj == 3))
            act(lambda h, j0=j0, bt=bt: h.copy(out=hT[:, j0 * 128:(j0 + 4) * 128], in_=bt[:, 0:512]), [bb], b_hTg)
        act(lambda h: h.copy(out=hTb[:], in_=hT[:]), b_hTg, b_hTbg)

    head(0, descs[0]["x"], descs[0]["T"], descs[0]["CH"], XS[0])
    hook = None
    for tile_i, d in enumerate(descs):
        X = XS[tile_i % 2]
        nd = descs[tile_i + 1] if tile_i + 1 < len(descs) else None
        nxt = (tile_i + 1, nd["x"], nd["T"], nd["CH"]) if nd is not None else None
        if d["kind"] == "p":
            b, ti = d["b"], d["ti"]
            if ti == 0:
                pool(lambda h: h.memset(hist[:], 0.0), [], [b_hist])
                pool(lambda h: h.memset(uhist[:], 0.0), [], [b_uhist])
                pool(lambda h: h.memset(hT[:], 0.0), [], b_hTg)
                pool(lambda h: h.memset(hTb[:], 0.0), [], b_hTbg)
            seq_out = {"conv": o_conv_p.ap()[b * 3:(b + 1) * 3, :], "ssm": o_ssm_p.ap()[b * DIN:(b + 1) * DIN, :],
                       "pool": o_pool_p.ap()[b * 15:(b + 1) * 15, :]}
            body(tile_i, TILE, 128, (ti == 0), (ti == ntile_seq - 1), seq_out, nxt, X, hook)
        else:
            sample_init()
            seq_out = {"conv": o_conv_s.ap(), "ssm": o_ssm_s.ap(), "pool": o_pool_s.ap()}
            body(tile_i, DSEQ, DSEQ, False, True, seq_out, None, X, hook)
        if nd is not None:
            head(tile_i + 1, nd["x"], nd["T"], nd["CH"], XS[(tile_i + 1) % 2])
        tail_a(d["T"], X)
        hook = (lambda d=d, X=X: tail_b(d["y"], d["T"], d["CH"], X))
    hook()

    P.replay(nc, es)
    es.close()
    return nc


_CFG = Cfg


def kernel(**inputs):
    cfg = _CFG
    NSEQ, SEQ = cfg.NSEQ, cfg.SEQ
    f32 = lambda a: np.ascontiguousarray(np.asarray(a, dtype=np.float32))
    shared = {}
    for k in WEIGHT_SHAPES:
        shared[k] = f32(inputs[k]).reshape(WEIGHT_SHAPES[k])
    for k, n in VEC_SHAPES.items():
        shared[k] = f32(inputs[k]).reshape(n)
    shared["conv_w"] = f32(inputs["conv_w"]).reshape(4, CONVD)
    xp = f32(inputs["x_prompt"])
    xs = f32(inputs["x_sample"])
    cc = f32(inputs["cache_conv"])[0]
    ss = f32(inputs["state_ssm"])[0]
    cp = f32(inputs["cache_pool"])[0]
    in_maps = []
    for c in range(NCORES):
        m = dict(shared)
        m["x_prompt"] = xp[c * NSEQ:(c + 1) * NSEQ].reshape(NSEQ * SEQ, D)
        m["x_sample"] = xs[c].reshape(cfg.DSEQ, D)
        m["cache_conv"] = cc[c].reshape(3, CONVD)
        m["state_ssm"] = ss[c].reshape(DIN, 128)
        m["cache_pool"] = cp[c].reshape(15, D)
        in_maps.append(m)
    nc = build_program(cfg)
    res = run_bass_kernel_spmd(nc, in_maps, core_ids=list(range(NCORES)))
    R = res.results
    B = NCORES * NSEQ
    y_p = np.concatenate([R[c]["y_prompt"].reshape(NSEQ, SEQ, D) for c in range(NCORES)], axis=0)
    y_s = np.stack([R[c]["y_sample"].reshape(cfg.DSEQ, D) for c in range(NCORES)], axis=0)
    conv_p = np.concatenate([R[c]["new_conv_prompt"].reshape(NSEQ, 3, CONVD) for c in range(NCORES)], axis=0)[None]
    ssm_p = np.concatenate([R[c]["new_ssm_prompt"].reshape(NSEQ, NH, 64, 128) for c in range(NCORES)], axis=0)[None]
    pool_p = np.concatenate([R[c]["new_pool_prompt"].reshape(NSEQ, 15, D) for c in range(NCORES)], axis=0)[None]
    conv_s = np.stack([R[c]["new_conv_sample"].reshape(3, CONVD) for c in range(NCORES)], axis=0)[None]
    ssm_s = np.stack([R[c]["new_ssm_sample"].reshape(NH, 64, 128) for c in range(NCORES)], axis=0)[None]
    pool_s = np.stack([R[c]["new_pool_sample"].reshape(15, D) for c in range(NCORES)], axis=0)[None]
    return (y_p.astype(np.float32), y_s.astype(np.float32), conv_p.astype(np.float32), ssm_p.astype(np.float32),
            pool_p.astype(np.float32), conv_s.astype(np.float32), ssm_s.astype(np.float32), pool_s.astype(np.float32))
```

```python
import numpy as np
from collections import defaultdict
from contextlib import ExitStack

import concourse.bass as bass
import concourse.mybir as mybir
from concourse.bass_utils import run_bass_kernel_spmd

F32 = mybir.dt.float32
BF16 = mybir.dt.bfloat16
I32 = mybir.dt.int32
AF = mybir.ActivationFunctionType
ALU = mybir.AluOpType

D = 1024
KC = 8
DFF = 2816
FC = 22
DIN = 2048
CONVD = 4096
NH = 32
EPS = 1e-6
NCORES = 8

ENGS = ["pe", "act", "dve", "pool", "sp"]
SELF_SYNC = ("act", "dve", "pool")
SAME_ENGINE_NEAR = 3


class Cfg:
    NSEQ = 4
    SEQ = 2048
    TILE = 512
    DSEQ = 16
    SAMPLE = True


class Buf:
    __slots__ = ("name", "w", "r", "excl")

    def __init__(self, name, excl=False):
        self.name = name
        self.w = {}
        self.r = {}
        self.excl = excl


class Prog:
    def __init__(self):
        self.entries = {e: [] for e in ENGS}
        self.cnt = defaultdict(int)
        self.seen = {e: {} for e in ENGS}
        self.keys = set(ENGS)
        self.final_dma = {}

    def op(self, eng, fn, reads=(), writes=(), inc=True, dma=None):
        deps = {}
        for b in reads:
            for k, v in b.w.items():
                if deps.get(k, 0) < v:
                    deps[k] = v
            if b.excl:
                for k, v in b.r.items():
                    if k != eng and deps.get(k, 0) < v:
                        deps[k] = v
        far = self.cnt[eng] - SAME_ENGINE_NEAR
        for b in writes:
            for k, v in b.w.items():
                if (k != eng or v <= far) and deps.get(k, 0) < v:
                    deps[k] = v
            for k, v in b.r.items():
                if (k != eng or v <= far) and deps.get(k, 0) < v:
                    deps[k] = v
        waits = []
        seen = self.seen[eng]
        for k, v in deps.items():
            if k == eng and eng not in SELF_SYNC:
                continue
            if seen.get(k, 0) < v:
                waits.append((k, v))
                seen[k] = v
        if dma is not None:
            self.keys.add(dma)
            self.cnt[dma] += 16
            key, val = dma, self.cnt[dma]
            self.final_dma[dma] = val
        else:
            key = eng
            if inc:
                self.cnt[eng] += 1
                val = self.cnt[eng]
            else:
                val = self.cnt[eng] + 1
        self.entries[eng].append((waits, fn, inc, dma))
        for b in writes:
            b.w = {key: val}
            b.r = {}
        for b in reads:
            if b.r.get(key, 0) < val:
                b.r[key] = val

    def replay(self, nc, es):
        sems = {}
        for k in sorted(self.keys):
            sems[k] = es.enter_context(nc.semaphore("s_" + k.replace(":", "_")))
        block = es.enter_context(nc.Block())
        entries = self.entries
        final_dma = self.final_dma

        def run(eng_name, h):
            for waits, fn, inc, dma in entries[eng_name]:
                for k, v in waits:
                    h.wait_ge(sems[k], v)
                inst = fn(h)
                if dma is not None:
                    inst.then_inc(sems[dma], 16)
                elif inc:
                    inst.then_inc(sems[eng_name], 1)

        @block.tensor
        def _(h):
            run("pe", h)

        @block.scalar
        def _(h):
            run("act", h)

        @block.vector
        def _(h):
            run("dve", h)

        @block.gpsimd
        def _(h):
            run("pool", h)

        @block.sync
        def _(h):
            run("sp", h)
            for k, v in final_dma.items():
                h.wait_ge(sems[k], v)


def bc_last(ap, n):
    sh = list(ap.shape)
    return ap.unsqueeze(len(sh)).broadcast_to(sh + [n])


def bc_mid(ap, n):
    sh = list(ap.shape)
    return ap.unsqueeze(1).broadcast_to([sh[0], n] + sh[1:])


def weight_block_defs():
    blocks = []

    def ffn(pfx):
        for j in range(11):
            blocks.append((f"{pfx}_gu{j}", [(f"{pfx}_w_gate", KC, 256 * j, 256), (f"{pfx}_w_up", KC, 256 * j, 256)]))
        for m in range(8):
            blocks.append((f"{pfx}_dn{m}", [(f"{pfx}_w_down", FC, 128 * m, 128)]))

    ffn("ffn1")
    for kind, j in MIX_ORDER:
        if kind == "z":
            blocks.append((f"z{j}", [("w_in", KC, 512 * j, 512)]))
        elif kind == "x":
            blocks.append((f"xbc{j}", [("w_in", KC, 2048 + 512 * j, 512)]))
        else:
            blocks.append(("dt", [("w_in", KC, 6144, 32)]))
    for j in range(2):
        blocks.append((f"u{j}", [("w_in", KC, 6176 + 512 * j, 512)]))
    for j in range(2):
        blocks.append((f"ga{j}", [("w_in", KC, 7200 + 512 * j, 512)]))
        blocks.append((f"ps{2 * j}", [("w_proj_ssd", 16, 256 * (2 * j), 256)]))
        blocks.append((f"ps{2 * j + 1}", [("w_proj_ssd", 16, 256 * (2 * j + 1), 256)]))
    blocks.append(("pm", [("pool_mix", 8, 0, 256)]))
    for j in range(2):
        blocks.append((f"gb{j}", [("w_in", KC, 8224 + 512 * j, 512)]))
        blocks.append((f"pp{j}", [("w_proj_pool", KC, 512 * j, 512)]))
    for j in range(2):
        blocks.append((f"wo{j}", [("w_out", KC, 512 * j, 512)]))
    ffn("ffn2")
    return blocks


WEIGHT_SHAPES = {
    "ffn1_w_gate": (D, DFF), "ffn1_w_up": (D, DFF), "ffn1_w_down": (DFF, D),
    "ffn2_w_gate": (D, DFF), "ffn2_w_up": (D, DFF), "ffn2_w_down": (DFF, D),
    "w_in": (D, 9248), "w_proj_ssd": (DIN, D), "pool_mix": (1024, 256),
    "w_proj_pool": (D, D), "w_out": (D, D),
}
VEC_SHAPES = {
    "ffn1_pre_g": D, "ffn1_post_g": D, "mix_pre_g": D, "mix_post_g": D, "ffn2_pre_g": D, "ffn2_post_g": D,
    "conv_b": CONVD, "dt_bias": NH, "a_log": NH, "d_skip": NH, "ssd_norm_g": DIN, "pool_scale": D,
}
MIX_ORDER = [("z", 0), ("x", 0), ("dt", 0), ("x", 1), ("z", 1), ("x", 2), ("x", 3), ("z", 2), ("x", 4), ("x", 5),
             ("z", 3), ("x", 6), ("x", 7)]
RING_ELEMS = 4096
NSLOT = 4


def build_program(cfg):
    nc = bass.Bass("TRN2", target_bir_lowering=False)
    es = ExitStack()
    P = Prog()
    NSEQ, SEQ, TILE, DSEQ = cfg.NSEQ, cfg.SEQ, cfg.TILE, cfg.DSEQ
    TILE = min(TILE, SEQ)
    assert SEQ % TILE == 0 and TILE % 128 == 0

    def din(name, shape):
        return nc.dram_tensor(name, list(shape), F32, kind="ExternalInput")

    def dout(name, shape):
        return nc.dram_tensor(name, list(shape), F32, kind="ExternalOutput")

    x_prompt = din("x_prompt", [NSEQ * SEQ, D])
    x_sample = din("x_sample", [DSEQ, D])
    cache_conv = din("cache_conv", [3, CONVD])
    state_ssm = din("state_ssm", [DIN, 128])
    cache_pool = din("cache_pool", [15, D])
    conv_w = din("conv_w", [4, CONVD])
    W = {k: din(k, v) for k, v in WEIGHT_SHAPES.items()}
    V = {k: din(k, [v]) for k, v in VEC_SHAPES.items()}
    y_prompt = dout("y_prompt", [NSEQ * SEQ, D])
    y_sample = dout("y_sample", [DSEQ, D])
    o_conv_p = dout("new_conv_prompt", [NSEQ * 3, CONVD])
    o_ssm_p = dout("new_ssm_prompt", [NSEQ * DIN, 128])
    o_pool_p = dout("new_pool_prompt", [NSEQ * 15, D])
    o_conv_s = dout("new_conv_sample", [3, CONVD])
    o_ssm_s = dout("new_ssm_sample", [DIN, 128])
    o_pool_s = dout("new_pool_sample", [15, D])

    blocks = weight_block_defs()
    NBLK = len(blocks)
    wsc = nc.dram_tensor("wsc", [NBLK, 128, RING_ELEMS], BF16, kind="Internal")

    def sb(name, shape, dt=F32):
        t = es.enter_context(nc.sbuf_tensor(name, list(shape), dt))
        return t, Buf(name)

    ring = [sb(f"ring{i}", [128, RING_ELEMS], BF16) for i in range(NSLOT)]
    xT0, _u0 = sb("xT0", [128, KC, TILE])
    xT1, _u1 = sb("xT1", [128, KC, TILE])
    XS = [(xT0, [Buf(f"xT0_{k}") for k in range(KC)]), (xT1, [Buf(f"xT1_{k}") for k in range(KC)])]
    xT = xT0
    xin = [sb(f"xin{i}", [128, D]) for i in range(2)]
    yout = [sb(f"yout{i}", [128, D]) for i in range(2)]
    xn, b_xn = sb("xn", [128, KC, TILE], BF16)
    bigA, _b_bigA_unused = sb("bigA", [128, 16384], BF16)
    RA = [Buf(f"A{i}") for i in range(16384 // TILE)]

    def AR(lo, n=1):
        return RA[lo:lo + n]
    SQA = 11264 // TILE if TILE == 512 else 22
    SQ2 = 8
    bigB, _b_bigB_unused = sb("bigB", [128, 8192], BF16)
    rstd, b_rstd = sb("rstd", [128, TILE])
    tmpA = [sb(f"tmpA{i}", [128, TILE + 16]) for i in range(2)]
    tmpP = [sb(f"tmpP{i}", [128, TILE]) for i in range(2)]
    mixedS, b_mixedS = sb("mixedS", [128, KC, TILE], BF16)
    ust = [sb(f"ust{i}", [128, 15 + TILE]) for i in range(2)]
    hist, b_hist = sb("hist", [128, 32, 4])
    uhist, b_uhist = sb("uhist", [128, 8, 16])
    hT, _b_hT_unused = sb("hT", [128, DIN])
    hTb, _b_hTb_unused = sb("hTb", [128, DIN], BF16)
    NCHM = TILE // 128
    dtb, b_dtb = sb("dtb", [128, NCHM, 32])
    dta, b_dta = sb("dta", [128, NCHM, 32])
    e3, b_e3 = sb("e3", [128, NCHM, 64])
    wdt, b_wdt = sb("wdt", [128, NCHM, 32])
    dec, b_dec = sb("dec", [128, NCHM, 32])
    dtt, b_dtt = sb("dtt", [128, NCHM, 32])
    NSTG = 8
    xtok = [sb(f"xtok{i}", [128, 384], BF16) for i in range(4)]
    xw = [sb(f"xw{i}", [128, 256], BF16) for i in range(2)]
    stm = [sb(f"stm{i}", [128, 128]) for i in range(3)]
    ud = [sb(f"ud{i}", [128, 512]) for i in range(2)]
    eE = [sb(f"eE{i}", [128, 512], BF16) for i in range(2)]
    mt = [sb(f"mt{i}", [128, 512], BF16) for i in range(2)]
    t1 = [sb(f"t1_{i}", [128, 256]) for i in range(2)]
    yg = [sb(f"yg{i}", [128, 256], BF16) for i in range(4)]
    ssq = [sb(f"ssq{i}", [128, 4]) for i in range(3)]
    dgs = [sb(f"dgs{i}", [128, 128], BF16) for i in range(2)]
    htmp = [sb(f"htmp{i}", [128, 256]) for i in range(2)]
    b_hTg = [Buf(f"hT_g{g}") for g in range(8)]
    b_hTbg = [Buf(f"hTb_g{g}") for g in range(8)]
    b_gt = [[Buf(f"gt_{c}_{g}") for g in range(8)] for c in range(NCHM)]
    ALL_GT = [b for row in b_gt for b in row]
    ident, b_ident = sb("ident", [128, 128])
    identb, b_identb = sb("identb", [128, 128], BF16)
    onesf, b_onesf = sb("onesf", [128, 128])
    onesb, b_onesb = sb("onesb", [128, 128], BF16)
    Lm, b_Lm = sb("Lm", [128, 128])
    Um, b_Um = sb("Um", [128, 128])
    gains, b_gains = sb("gains", [128, 6, KC])
    gssd, b_gssd = sb("gssd", [128, 16])
    pscale, b_pscale = sb("pscale", [128, KC])
    cw, b_cw = sb("cw", [128, 4, 32])
    cb, b_cb = sb("cb", [128, 32])
    dtbias, b_dtbias = sb("dtbias", [128, 32])
    abc, b_abc = sb("abc", [128, 32])
    dsk, b_dsk = sb("dsk", [128, 32])
    invc, b_invc = sb("invc", [128, 4, 16])
    gpost, b_gpost = sb("gpost", [128, 3, KC])
    dskf, b_dskf = sb("dskf", [128, 16])
    Dg, b_Dg = sb("Dg", [128, 16, 128], BF16)
    ioti, b_ioti = sb("ioti", [128, 16], I32)
    stg4, b_stg4 = sb("stg4", [16, 1024])

    NBANK = 8
    banks = []
    for i in range(NBANK):
        t = es.enter_context(nc.psum_tensor(f"pb{i}", [128, 512], F32))
        banks.append((t, Buf(f"pb{i}", excl=True)))
    bank_rr = [0]

    def nbank():
        i = bank_rr[0]
        bank_rr[0] = (i + 1) % NBANK
        return banks[i]

    CONSTS = [b_ident, b_identb, b_onesf, b_onesb, b_Lm, b_Um, b_gains, b_gssd, b_pscale, b_cw, b_cb,
              b_dtbias, b_abc, b_dsk, b_invc]

    rr = {"cast": 0}

    def act(fn, reads, writes):
        P.op("act", fn, reads, writes)

    def dve(fn, reads, writes):
        P.op("dve", fn, reads, writes)

    def pool(fn, reads, writes):
        P.op("pool", fn, reads, writes)

    def pe(fn, reads, writes, inc=True):
        P.op("pe", fn, reads, writes, inc=inc)

    def dma(fn, reads, writes, key, eng="sp"):
        P.op(eng, fn, reads, writes, dma=key)

    def mm_group(out_ap, bank_buf, pairs, reads):
        n = len(pairs)
        for i, (l, r) in enumerate(pairs):
            pe(lambda h, l=l, r=r, i=i: h.matmul(out_ap, lhsT=l, rhs=r, start=(i == 0), stop=(i == n - 1)),
               reads, [bank_buf], inc=(i == n - 1))

    pool(lambda h: h.memset(ident[:], 0.0), [], [b_ident])
    pool(lambda h: h.affine_select(out=ident[:], in_=ident[:], pattern=[[-1, 128]], base=0, channel_multiplier=1,
                                   compare_op=ALU.not_equal, fill=1.0), [b_ident], [b_ident])
    pool(lambda h: h.tensor_copy(out=identb[:], in_=ident[:]), [b_ident], [b_identb])
    pool(lambda h: h.memset(onesf[:], 1.0), [], [b_onesf])
    pool(lambda h: h.memset(onesb[:], 1.0), [], [b_onesb])
    pool(lambda h: h.affine_select(out=Lm[:], in_=onesf[:], pattern=[[1, 128]], base=0, channel_multiplier=-1,
                                   compare_op=ALU.is_ge, fill=0.0), [b_onesf], [b_Lm])
    pool(lambda h: h.affine_select(out=Um[:], in_=onesf[:], pattern=[[-1, 128]], base=0, channel_multiplier=1,
                                   compare_op=ALU.is_gt, fill=0.0), [b_onesf], [b_Um])
    pool(lambda h: h.iota(ioti[:], pattern=[[1, 16]], base=1, channel_multiplier=0), [], [b_ioti])
    pool(lambda h: h.tensor_copy(out=invc[:, 0, :], in_=ioti[:]), [b_ioti], [b_invc])
    for gi in range(1, 4):
        pool(lambda h, gi=gi: h.tensor_copy(out=invc[:, gi, :], in_=invc[:, 0, :]), [b_invc], [b_invc])
    for gi in range(4):
        dve(lambda h, gi=gi: h.tensor_scalar(out=invc[:, gi, :], in0=invc[:, gi, :], scalar1=float(2 ** (gi + 1)),
                                             scalar2=None, op0=ALU.min), [b_invc], [b_invc])
    dve(lambda h: h.reciprocal(out=invc[:], in_=invc[:]), [b_invc], [b_invc])
    pool(lambda h: h.memset(hist[:], 0.0), [], [b_hist])
    pool(lambda h: h.memset(uhist[:], 0.0), [], [b_uhist])
    pool(lambda h: h.memset(stg4[:], 0.0), [], [b_stg4])

    GAIN_IDX = {"ffn1_pre_g": 0, "ffn1_post_g": 1, "mix_pre_g": 2, "mix_post_g": 3, "ffn2_pre_g": 4, "ffn2_post_g": 5}
    cs1 = xin[0][0]
    cs2 = xin[1][0]
    b_cs1, b_cs2 = xin[0][1], xin[1][1]
    pool(lambda h: h.memset(cs2[:, 0:128], 0.0), [], [b_cs2])
    dma(lambda h: h.dma_start(out=cs1[:, 0:128], in_=conv_w.ap().rearrange("k (m p) -> (k m) p", p=128)), [], [b_cs1], "xin0")
    for nm, gi in GAIN_IDX.items():
        dma(lambda h, nm=nm, gi=gi: h.dma_start(out=cs2[gi * 8:(gi + 1) * 8, 0:128], in_=V[nm].ap().rearrange("(k p) -> k p", p=128)),
            [], [b_cs2], "xin1")
    dma(lambda h: h.dma_start(out=cs2[64:96, 0:128], in_=V["conv_b"].ap().rearrange("(k p) -> k p", p=128)), [], [b_cs2], "xin1")
    dma(lambda h: h.dma_start(out=cs2[96:112, 0:128], in_=V["ssd_norm_g"].ap().rearrange("(k p) -> k p", p=128)), [], [b_cs2], "xin1")
    dma(lambda h: h.dma_start(out=cs2[112:120, 0:128], in_=V["pool_scale"].ap().rearrange("(k p) -> k p", p=128)), [], [b_cs2], "xin1")
    bt, bb = nbank()
    pe(lambda h: h.transpose(out=bt[:, 0:128], in_=cs1[:, 0:128], identity=ident[:]), [b_cs1, b_ident], [bb], inc=False)
    pe(lambda h: h.transpose(out=bt[:, 128:256], in_=cs2[:, 0:128], identity=ident[:]), [b_cs2, b_ident], [bb], inc=True)
    act(lambda h: h.copy(out=cw[:], in_=bt[:, 0:128].rearrange("p (k m) -> p k m", k=4)), [bb], [b_cw])
    act(lambda h: h.copy(out=gains[:], in_=bt[:, 128:176].rearrange("p (g k) -> p g k", g=6)), [bb], [b_gains])
    act(lambda h: h.copy(out=cb[:], in_=bt[:, 192:224]), [bb], [b_cb])
    act(lambda h: h.copy(out=gssd[:], in_=bt[:, 224:240]), [bb], [b_gssd])
    act(lambda h: h.copy(out=pscale[:], in_=bt[:, 240:248]), [bb], [b_pscale])

    def pbcast(src):
        return bass.AP(tensor=src, offset=0, ap=[[0, 128], [1, NH]])

    dma(lambda h: h.dma_start(out=dtbias[:], in_=pbcast(V["dt_bias"])), [], [b_dtbias], "c_dtbias")
    dma(lambda h: h.dma_start(out=abc[:], in_=pbcast(V["a_log"])), [], [b_abc], "c_abc")
    dma(lambda h: h.dma_start(out=dsk[:], in_=pbcast(V["d_skip"])), [], [b_dsk], "c_dsk")
    for ci, (gi_, cc_) in enumerate(((1, 0.5), (3, 1.0), (5, 0.5))):
        dve(lambda h, ci=ci, gi_=gi_, cc_=cc_: h.tensor_scalar(out=gpost[:, ci, :], in0=gains[:, gi_, :], scalar1=cc_, scalar2=None, op0=ALU.mult),
            [b_gains], [b_gpost])
    for two in range(2):
        dve(lambda h, two=two: h.tensor_copy(out=dskf[two * 64:(two + 1) * 64, :],
                                             in_=dsk[two * 64:(two + 1) * 64, :].rearrange("p (j t) -> p j t", t=2)[:, :, two]),
            [b_dsk], [b_dskf])
    for j in range(16):
        dve(lambda h, j=j: h.tensor_scalar(out=Dg[:, j, :], in0=identb[:], scalar1=dskf[:, j:j + 1], scalar2=None, op0=ALU.mult),
            [b_identb, b_dskf], [b_Dg])
    act(lambda h: h.activation(out=abc[:], in_=abc[:], func=AF.Exp), [b_abc], [b_abc])
    dve(lambda h: h.tensor_scalar(out=abc[:], in0=abc[:], scalar1=-1.0, scalar2=None, op0=ALU.mult), [b_abc], [b_abc])

    NST = 3
    stg32 = [bigA[:, 0:8192].bitcast(F32), bigA[:, 8192:16384].bitcast(F32), xT0[:].rearrange("p a b -> p (a b)")]
    stg16 = [bigB[:, 0:4096], bigB[:, 4096:8192], xn[:].rearrange("p a b -> p (a b)")]
    b_stg32 = [Buf(f"stg32_{i}") for i in range(NST)]
    b_stg16 = [Buf(f"stg16_{i}") for i in range(NST)]
    blk_elems = []

    def pl_load(bi):
        bname, parts = blocks[bi]
        s = bi % NST
        off = 0
        for (wname, nkc, c0, ncol) in parts:
            n = nkc * ncol
            src = W[wname].ap()[:, c0:c0 + ncol].rearrange("(k p) n -> p k n", p=128)
            dst = stg32[s][:, off:off + n].rearrange("p (k n) -> p k n", k=nkc)
            dma(lambda h, src=src, dst=dst: h.dma_start(out=dst, in_=src), [], [b_stg32[s]], f"pl{s}")
            off += n
        blk_elems.append(off)

    def pl_cast_store(bi):
        s = bi % NST
        off = blk_elems[bi]
        cut = (off * 5 // 9) // 2 * 2
        o16 = stg16[s][:, 0:off]
        if blocks[bi][0].startswith("ps"):
            for kc in range(16):
                lo = kc * 256
                if kc % 2 == 0:
                    act(lambda h, s=s, lo=lo, kc=kc: h.activation(out=stg16[s][:, lo:lo + 256], in_=stg32[s][:, lo:lo + 256], func=AF.Copy,
                                                                  scale=gssd[:, kc:kc + 1]), [b_stg32[s], b_gssd], [b_stg16[s]])
                else:
                    dve(lambda h, s=s, lo=lo, kc=kc: h.tensor_scalar(out=stg16[s][:, lo:lo + 256], in0=stg32[s][:, lo:lo + 256],
                                                                     scalar1=gssd[:, kc:kc + 1], scalar2=None, op0=ALU.mult),
                        [b_stg32[s], b_gssd], [b_stg16[s]])
        else:
            act(lambda h, s=s, cut=cut: h.copy(out=stg16[s][:, 0:cut], in_=stg32[s][:, 0:cut]), [b_stg32[s]], [b_stg16[s]])
            dve(lambda h, s=s, cut=cut, off=off: h.tensor_copy(out=stg16[s][:, cut:off], in_=stg32[s][:, cut:off]), [b_stg32[s]], [b_stg16[s]])
        dma(lambda h, bi=bi, o=o16, off=off: h.dma_start(out=wsc.ap()[bi, :, 0:off], in_=o), [b_stg16[s]], [], f"ps{s}")

    for bi in range(min(NST - 1, NBLK)):
        pl_load(bi)
    for bi in range(NBLK):
        if bi + NST - 1 < NBLK:
            pl_load(bi + NST - 1)
        pl_cast_store(bi)
    b_wsc = Buf("wsc")
    b_wsc.w = {f"ps{i}": P.cnt[f"ps{i}"] for i in range(NST)}
    for b in RA + XS[0][1] + XS[1][1] + [b_xn] + ALL_GT:
        for s in range(NST):
            for src in (b_stg32[s], b_stg16[s]):
                for k, v in list(src.w.items()) + list(src.r.items()):
                    if b.r.get(k, 0) < v:
                        b.r[k] = v

    blk_index = {nm: i for i, (nm, _) in enumerate(blocks)}
    n_tiles_total = NSEQ * (SEQ // TILE) + (1 if cfg.SAMPLE else 0)
    total_stream = n_tiles_total * NBLK
    ring_state = {"next": 0, "free": list(range(NSLOT)), "slot_of": {}}

    def ring_prefetch():
        while ring_state["free"] and ring_state["next"] < total_stream:
            q = ring_state["next"]
            ring_state["next"] += 1
            s = ring_state["free"].pop(0)
            ring_state["slot_of"][q] = s
            bi = q % NBLK
            n = blk_elems[bi]
            rt, rb = ring[s]
            dma(lambda h, rt=rt, bi=bi, n=n: h.dma_start(out=rt[:, 0:n], in_=wsc.ap()[bi, :, 0:n]), [b_wsc], [rb], f"ring{s}")

    def ring_get(tile_idx, name):
        q = tile_idx * NBLK + blk_index[name]
        while q not in ring_state["slot_of"]:
            assert ring_state["free"], f"weight ring deadlock at {name}"
            ring_prefetch()
        s = ring_state["slot_of"][q]
        return ring[s][0], ring[s][1], q

    def ring_release(q):
        s = ring_state["slot_of"].pop(q)
        ring_state["free"].append(s)
        ring_prefetch()

    rstd_ps = {}

    def rms_finish(T, sq_view, sq_base, to_psum):
        bt, bb = nbank()
        for kc in range(KC):
            pe(lambda h, kc=kc: h.matmul(bt[:, 0:T], lhsT=onesb[:], rhs=sq_view(kc), start=(kc == 0), stop=(kc == KC - 1)),
               [b_onesb] + AR(sq_base + kc), [bb], inc=(kc == KC - 1))
        act(lambda h: h.activation(out=rstd[:, 0:T], in_=bt[:, 0:T], func=AF.Ln, bias=EPS, scale=1.0 / D), [bb], [b_rstd])
        if to_psum:
            act(lambda h: h.activation(out=bt[:, 0:T], in_=rstd[:, 0:T], func=AF.Exp, scale=-0.5), [b_rstd], [bb])
        else:
            act(lambda h: h.activation(out=rstd[:, 0:T], in_=rstd[:, 0:T], func=AF.Exp, scale=-0.5), [b_rstd], [b_rstd])
        rstd_ps["t"], rstd_ps["b"] = bt, bb

    def norm_apply(gidx, T, X):
        xT, XT_B = X
        bt, bb = rstd_ps["t"], rstd_ps["b"]
        for kc in range(KC):
            dve(lambda h, kc=kc: h.scalar_tensor_tensor(out=xn[:, kc, 0:T], in0=xT[:, kc, 0:T], scalar=gains[:, gidx, kc:kc + 1],
                                                       in1=bt[:, 0:T], op0=ALU.mult, op1=ALU.mult),
                [XT_B[kc], b_gains, bb], [b_xn])

    fout = bigB[:, 0:8192].bitcast(F32)

    def fview(m, T):
        return fout[:, m * TILE:m * TILE + T]

    def gt_for_f(m):
        lo = m * TILE * 2
        hi = lo + TILE * 2
        return [b_gt[c][g] for c in range(NCHM) for g in range(8) if c * 2048 + g * 256 < hi and c * 2048 + (g + 1) * 256 > lo]

    def sq_view_A(T):
        return lambda kc: bigA[:, 11264 + kc * TILE: 11264 + kc * TILE + T]

    def residual_update(cidx, T, next_sq, X):
        xT, XT_B = X
        for m in range(KC):
            if m % 4 == 3:
                tt, tb = tmpA[0]
                dve(lambda h, m=m, tt=tt: h.scalar_tensor_tensor(out=tt[:, 0:T], in0=fview(m, T), scalar=gpost[:, cidx, m:m + 1],
                                                                in1=rstd[:, 0:T], op0=ALU.mult, op1=ALU.mult),
                    gt_for_f(m) + [b_gpost, b_rstd], [tb])
                dve(lambda h, m=m, tt=tt: h.tensor_tensor(out=xT[:, m, 0:T], in0=tt[:, 0:T], in1=xT[:, m, 0:T], op=ALU.add),
                    [tb, XT_B[m]], [XT_B[m]])
            else:
                tt, tb = tmpP[m % 2]
                pool(lambda h, m=m, tt=tt: h.tensor_tensor(out=tt[:, 0:T], in0=fview(m, T), in1=rstd[:, 0:T], op=ALU.mult),
                     gt_for_f(m) + [b_rstd], [tb])
                dve(lambda h, m=m, tt=tt: h.scalar_tensor_tensor(out=xT[:, m, 0:T], in0=tt[:, 0:T], scalar=gpost[:, cidx, m:m + 1],
                                                                in1=xT[:, m, 0:T], op0=ALU.mult, op1=ALU.add),
                    [tb, XT_B[m], b_gpost], [XT_B[m]])
            if next_sq:
                act(lambda h, m=m: h.activation(out=bigA[:, (SQA + m) * TILE:(SQA + m) * TILE + T], in_=xT[:, m, 0:T], func=AF.Square),
                    [XT_B[m]], AR(SQA + m))

    def ffn(tile_i, pfx, g_pre, T, X, prenorm, post, hook=None):
        sqv = sq_view_A(T)
        if prenorm:
            rms_finish(T, sqv, SQA, True)
            norm_apply(g_pre, T, X)
        hid = lambda hc: bigA[:, hc * TILE: hc * TILE + T]
        for j in range(11):
            if hook is not None and j == 3:
                hook()
            rt, rb, q = ring_get(tile_i, f"{pfx}_gu{j}")
            wv = rt[:, 0:4096].rearrange("p (a k n) -> p a k n", a=2, k=KC)
            for jj in range(2):
                hc = 2 * j + jj
                bg, bgb = nbank()
                mm_group(bg[:, 0:T], bgb, [(wv[:, 0, kc, jj * 128:(jj + 1) * 128], xn[:, kc, 0:T]) for kc in range(KC)], [rb, b_xn])
                bu, bub = nbank()
                mm_group(bu[:, 0:T], bub, [(wv[:, 1, kc, jj * 128:(jj + 1) * 128], xn[:, kc, 0:T]) for kc in range(KC)], [rb, b_xn])
                ta, tab = tmpA[hc % 2]
                act(lambda h, ta=ta, bg=bg: h.activation(out=ta[:, 0:T], in_=bg[:, 0:T], func=AF.Silu), [bgb], [tab])
                dve(lambda h, ta=ta, bu=bu, hc=hc: h.tensor_tensor(out=hid(hc), in0=ta[:, 0:T], in1=bu[:, 0:T], op=ALU.mult),
                    [tab, bub], AR(hc))
            ring_release(q)
        for m in range(KC):
            rt, rb, q = ring_get(tile_i, f"{pfx}_dn{m}")
            wv = rt[:, 0:FC * 128].rearrange("p (k n) -> p k n", k=FC)
            bt, bb = nbank()
            mm_group(bt[:, 0:T], bb, [(wv[:, kc, :], hid(kc)) for kc in range(FC)], [rb] + AR(0, FC))
            act(lambda h, m=m, bt=bt: h.copy(out=fview(m, T), in_=bt[:, 0:T]), [bb], gt_for_f(m))
            dve(lambda h, m=m, bt=bt: h.tensor_tensor(out=sqv(m), in0=bt[:, 0:T], in1=fview(m, T), op=ALU.mult), [bb] + gt_for_f(m), AR(SQA + m))
            ring_release(q)
        if post:
            rms_finish(T, sqv, SQA, False)
            residual_update((0 if pfx == "ffn1" else 2), T, pfx == "ffn1", X)

    def tok2feat_small(src_dram_rows, nrows_pad, width, dst_fn, dst_buf):
        nrows = src_dram_rows.shape[0]
        for p0 in range(0, width, 1024):
            dma(lambda h, p0=p0: h.dma_start(out=stg4[0:nrows, 0:1024], in_=src_dram_rows[:, p0:p0 + 1024]), [], [b_stg4], "stg4")
            for q0 in range(0, 8, 4):
                bt, bb = nbank()
                for mm in range(4):
                    ml = q0 + mm
                    pe(lambda h, ml=ml, mm=mm, bt=bt: h.transpose(out=bt[:, mm * nrows_pad:(mm + 1) * nrows_pad],
                                                               in_=stg4[0:nrows_pad, ml * 128:(ml + 1) * 128],
                                                               identity=ident[0:nrows_pad, 0:nrows_pad]),
                       [b_stg4, b_ident], [bb], inc=(mm == 3))
                m0 = p0 // 128 + q0
                act(lambda h, m0=m0, bt=bt: h.copy(out=dst_fn(m0), in_=bt[:, 0:4 * nrows_pad].rearrange("p (a b) -> p a b", a=4)),
                    [bb], [dst_buf])

    def feat2tok_small(src_fn, src_buf, nrows, nrows_pad, width, dst_dram):
        for p0 in range(0, width, 1024):
            for q0 in range(0, 8, 4):
                bt, bb = nbank()
                for mm in range(4):
                    m = p0 // 128 + q0 + mm
                    pe(lambda h, m=m, mm=mm, bt=bt: h.transpose(out=bt[0:nrows_pad, mm * 128:(mm + 1) * 128], in_=src_fn(m),
                                                               identity=ident[:]),
                       [src_buf, b_ident], [bb], inc=(mm == 3))
                act(lambda h, q0=q0, bt=bt: h.copy(out=stg4[0:nrows_pad, q0 * 128:(q0 + 4) * 128], in_=bt[0:nrows_pad, 0:512]),
                    [bb], [b_stg4])
            dma(lambda h, p0=p0: h.dma_start(out=dst_dram[:, p0:p0 + 1024], in_=stg4[0:nrows, 0:1024]), [b_stg4], [], "stg4o", eng="act")

    preloaded = set()

    def x_load(tile_i, x_rows, CH, c):
        if (tile_i, c) in preloaded:
            return
        preloaded.add((tile_i, c))
        xt_, xb_ = xin[c % 2]
        dma(lambda h: h.dma_start(out=xt_[0:CH, :], in_=x_rows[c * CH:(c + 1) * CH, :]), [], [xb_], f"xin{c % 2}")

    def head(tile_i, x_rows, T, CH, X):
        xT, XT_B = X
        NCH = T // CH
        for c in range(NCH):
            xt_, xb_ = xin[c % 2]
            x_load(tile_i, x_rows, CH, c)
            for half in range(2):
                bt, bb = nbank()
                for kk in range(4):
                    kc = half * 4 + kk
                    pe(lambda h, kc=kc, kk=kk, bt=bt, xt_=xt_: h.transpose(out=bt[:, kk * CH:(kk + 1) * CH],
                                                                        in_=xt_[0:CH, kc * 128:(kc + 1) * 128],
                                                                        identity=ident[0:CH, 0:CH]),
                       [xb_, b_ident], [bb], inc=(kk == 3))
                act(lambda h, half=half, bt=bt, c=c: h.copy(out=xT[:, half * 4:(half + 1) * 4, c * CH:(c + 1) * CH],
                                                          in_=bt[:, 0:4 * CH].rearrange("p (a b) -> p a b", a=4)),
                    [bb], XT_B[half * 4:half * 4 + 4])
                sq0 = half * 4 * TILE + c * CH
                dve(lambda h, half=half, bt=bt, c=c, sq0=sq0: h.tensor_tensor(
                        out=bigA[:, sq0:sq0 + 4 * TILE].rearrange("p (a b) -> p a b", a=4)[:, :, 0:CH],
                        in0=bt[:, 0:4 * CH].rearrange("p (a b) -> p a b", a=4),
                        in1=xT[:, half * 4:(half + 1) * 4, c * CH:(c + 1) * CH], op=ALU.mult),
                    [bb] + XT_B[half * 4:half * 4 + 4], AR(half * 4, 4))
        rms_finish(T, lambda kc: bigA[:, kc * TILE:kc * TILE + T], 0, True)
        norm_apply(0, T, X)

    def tail_a(T, X):
        rms_finish(T, sq_view_A(T), SQA, False)
        residual_update(2, T, False, X)

    def tail_b(y_rows, T, CH, X):
        xT, XT_B = X
        NCH = T // CH
        for c in range(NCH):
            yt_, yb_ = yout[c % 2]
            for half in range(2):
                bt, bb = nbank()
                for kk in range(4):
                    kc = half * 4 + kk
                    pe(lambda h, kc=kc, kk=kk, bt=bt, c=c: h.transpose(out=bt[0:CH, kk * 128:(kk + 1) * 128],
                                                                    in_=xT[:, kc, c * CH:(c + 1) * CH], identity=ident[:]),
                       [XT_B[kc], b_ident], [bb], inc=(kk == 3))
                act(lambda h, half=half, bt=bt, yt_=yt_: h.copy(out=yt_[0:CH, half * 512:(half + 1) * 512], in_=bt[0:CH, 0:512]),
                    [bb], [yb_])
            dma(lambda h, c=c, yt_=yt_: h.dma_start(out=y_rows[c * CH:(c + 1) * CH, :], in_=yt_[0:CH, :]), [yb_], [], f"yout{c % 2}", eng="act")

    def body(tile_i, T, CH, first, last, seq_out, nxt, X, hook):
        ffn(tile_i, "ffn1", 0, T, X, prenorm=False, post=True, hook=hook)
        mixer(tile_i, T, CH, first, last, seq_out, X)
        if nxt is not None:
            for c in range(min(2, nxt[2] // nxt[3])):
                x_load(nxt[0], nxt[1], nxt[3], c)
        ffn(tile_i, "ffn2", 4, T, X, prenorm=True, post=False)

    def mixer(tile_i, T, CH, first, last, seq_out, X):
        NCH = T // CH
        sqv = sq_view_A(T)
        rms_finish(T, sqv, SQA, True)
        norm_apply(2, T, X)
        xbc = lambda m: bigA[:, m * TILE: m * TILE + T]
        xbc_c = lambda m, c: bigA[:, m * TILE + c * CH: m * TILE + (c + 1) * CH]
        gtok = lambda c, lo, n: bigB[0:CH, c * 2048 + lo: c * 2048 + lo + n]
        pooled = lambda m: mixedS[:, m, 0:T]
        mixedv = lambda m: mixedS[:, m, 0:T]

        def pool_u(ub):
            rt, rb, q = ring_get(tile_i, f"u{ub}")
            wv = rt[:, 0:4096].rearrange("p (k n) -> p k n", k=KC)
            for mm in range(4):
                m = 4 * ub + mm
                gi = m // 2
                bt, bb = nbank()
                mm_group(bt[:, 0:T], bb, [(wv[:, kc, mm * 128:(mm + 1) * 128], xn[:, kc, 0:T]) for kc in range(KC)], [rb, b_xn])
                ut, ubuf = ust[m % 2]
                act(lambda h, ut=ut, bt=bt: h.copy(out=ut[:, 15:15 + T], in_=bt[:, 0:T]), [bb], [ubuf])
                pool(lambda h, ut=ut, m=m: h.tensor_copy(out=ut[:, 0:15], in_=uhist[:, m, 0:15]), [b_uhist], [ubuf])
                W_ = 15 + T
                src_t, src_b = ut, ubuf
                bufs2 = [tmpA[0], tmpA[1]]
                lo = 0
                for lvl in range(gi + 1):
                    sh = 2 ** lvl
                    lo = lo + sh
                    dt_, db_ = bufs2[lvl % 2]
                    pool(lambda h, dt_=dt_, src_t=src_t, sh=sh, lo=lo, W_=W_: h.tensor_tensor(out=dt_[:, lo:W_], in0=src_t[:, lo:W_],
                                                                                           in1=src_t[:, lo - sh:W_ - sh], op=ALU.add),
                         [src_b], [db_])
                    src_t, src_b = dt_, db_
                kk = float(2 ** (gi + 1))
                dve(lambda h, src_t=src_t, ut=ut, m=m, kk=kk: h.scalar_tensor_tensor(out=pooled(m), in0=src_t[:, 15:15 + T], scalar=1.0 / kk,
                                                                                   in1=ut[:, 15:15 + T], op0=ALU.mult, op1=ALU.subtract),
                    [src_b, ubuf], [b_mixedS])
                if first:
                    tt, tb = tmpP[0]
                    dve(lambda h, src_t=src_t, tt=tt, gi=gi: h.tensor_tensor(out=tt[:, 0:15], in0=src_t[:, 15:30], in1=invc[:, gi, 0:15], op=ALU.mult),
                        [src_b, b_invc], [tb])
                    dve(lambda h, tt=tt, ut=ut, m=m: h.tensor_tensor(out=pooled(m)[:, 0:15], in0=tt[:, 0:15], in1=ut[:, 15:30], op=ALU.subtract),
                        [tb, ubuf], [b_mixedS])
                pool(lambda h, ut=ut, m=m: h.tensor_copy(out=uhist[:, m, 0:15], in_=ut[:, T:T + 15]), [ubuf], [b_uhist])
            ring_release(q)

        def pool_pm():
            if last:
                feat2tok_small(lambda m: uhist[:, m, :], b_uhist, 15, 16, D, seq_out["pool"])
            rt, rb, q = ring_get(tile_i, "pm")
            wv = rt[:, 0:2048].rearrange("p (g k n) -> p g k n", g=4, k=2)
            for gi in range(4):
                grp = []
                for mo in range(2):
                    bt, bb = nbank()
                    mm_group(bt[:, 0:T], bb, [(wv[:, gi, kc, mo * 128:(mo + 1) * 128], pooled(2 * gi + kc)) for kc in range(2)], [rb, b_mixedS])
                    grp.append((bt, bb))
                for mo in range(2):
                    m = 2 * gi + mo
                    bt, bb = grp[mo]
                    act(lambda h, bt=bt, m=m: h.activation(out=mixedv(m), in_=bt[:, 0:T], func=AF.Copy, scale=pscale[:, m:m + 1]),
                        [bb, b_pscale], [b_mixedS])
            ring_release(q)

        def z_group(zb, c, wv, rb):
            bt, bb = nbank()
            mm_group(bt[0:CH, 0:512], bb, [(xn[:, kc, c * CH:(c + 1) * CH], wv[:, kc, :]) for kc in range(KC)], [rb, b_xn])
            act(lambda h: h.activation(out=gtok(c, zb * 512, 512), in_=bt[0:CH, 0:512], func=AF.Silu),
                [bb], [b_gt[c][2 * zb], b_gt[c][2 * zb + 1]])

        def conv_back(m):
            ta, tab = tmpA[m % 2]
            tp, tpb = tmpP[m % 2]
            pool(lambda h: h.tensor_tensor(out=tp[:, 0:T], in0=tp[:, 0:T], in1=ta[:, 0:T], op=ALU.add), [tpb, tab], [tpb])
            act(lambda h: h.activation(out=xbc(m), in_=tp[:, 0:T], func=AF.Silu), [tpb], AR(m))

        def xbc_chunk(m, wv, rb):
            mm = m % 4
            bt, bb = nbank()
            mm_group(bt[:, 0:T], bb, [(wv[:, kc, mm * 128:(mm + 1) * 128], xn[:, kc, 0:T]) for kc in range(KC)], [rb, b_xn])
            ta, tab = tmpA[m % 2]
            tp, tpb = tmpP[m % 2]
            act(lambda h: h.activation(out=ta[:, 0:T], in_=bt[:, 0:T], func=AF.Identity, bias=cb[:, m:m + 1], scale=cw[:, 3, m:m + 1]),
                [bb, b_cb, b_cw], [tab])
            dve(lambda h: h.scalar_tensor_tensor(out=ta[:, 0:2], in0=hist[:, m, 1:3], scalar=cw[:, 1, m:m + 1], in1=ta[:, 0:2],
                                                 op0=ALU.mult, op1=ALU.add), [b_hist, tab, b_cw], [tab])
            pool(lambda h: h.tensor_scalar(out=tp[:, 0:3], in0=hist[:, m, 0:3], scalar1=cw[:, 0, m:m + 1], scalar2=None, op0=ALU.mult),
                 [b_hist, b_cw], [tpb])
            dve(lambda h: h.scalar_tensor_tensor(out=tp[:, 0:1], in0=hist[:, m, 2:3], scalar=cw[:, 2, m:m + 1], in1=tp[:, 0:1],
                                                 op0=ALU.mult, op1=ALU.add), [b_hist, tpb, b_cw], [tpb])
            act(lambda h: h.activation(out=tp[:, 3:T], in_=bt[:, 0:T - 3], func=AF.Copy, scale=cw[:, 0, m:m + 1]), [bb, b_cw], [tpb])
            act(lambda h: h.copy(out=hist[:, m, 0:3], in_=bt[:, T - 3:T]), [bb], [b_hist])
            dve(lambda h: h.scalar_tensor_tensor(out=ta[:, 2:T], in0=bt[:, 0:T - 2], scalar=cw[:, 1, m:m + 1], in1=ta[:, 2:T],
                                                 op0=ALU.mult, op1=ALU.add), [bb, tab, b_cw], [tab])
            dve(lambda h: h.scalar_tensor_tensor(out=tp[:, 1:T], in0=bt[:, 0:T - 1], scalar=cw[:, 2, m:m + 1], in1=tp[:, 1:T],
                                                 op0=ALU.mult, op1=ALU.add), [bb, tpb, b_cw], [tpb])
            if m >= 1:
                conv_back(m - 1)

        dtst = {}

        def dt_a():
            rt, rb, q = ring_get(tile_i, "dt")
            wv = rt[:, 0:KC * 32].rearrange("p (k n) -> p k n", k=KC)
            for c in range(NCH):
                bt, bb = nbank()
                mm_group(bt[0:CH, 0:32], bb, [(xn[:, kc, c * CH:(c + 1) * CH], wv[:, kc, :]) for kc in range(KC)], [rb, b_xn])
                dve(lambda h, bt=bt, c=c: h.tensor_tensor(out=dtt[0:CH, c, :], in0=bt[0:CH, 0:32], in1=dtbias[0:CH, :], op=ALU.add),
                    [bb, b_dtbias], [b_dtt])
            ring_release(q)

        def dt_b():
            act(lambda h: h.activation(out=dtt[0:CH, 0:NCH, :], in_=dtt[0:CH, 0:NCH, :], func=AF.Exp), [b_dtt], [b_dtt])
            act(lambda h: h.activation(out=dtb[0:CH, 0:NCH, :], in_=dtt[0:CH, 0:NCH, :], func=AF.Ln, bias=1.0, scale=1.0), [b_dtt], [b_dtb])

        def dt_c():
            dve(lambda h: h.tensor_tensor(out=dta[0:CH, 0:NCH, :], in0=dtb[0:CH, 0:NCH, :], in1=bc_mid(abc[0:CH, :], NCH), op=ALU.mult),
                [b_dtb, b_abc], [b_dta])

        def dt_d():
            b2, bb2 = nbank()
            b3, bb3 = nbank()
            dtst["b2"], dtst["bb2"], dtst["b3"], dtst["bb3"] = b2, bb2, b3, bb3
            for c in range(NCH):
                pe(lambda h, c=c: h.matmul(b2[0:CH, c * 64:c * 64 + 32], lhsT=Lm[0:CH, 0:CH], rhs=dta[0:CH, c, :], start=True, stop=True),
                   [b_Lm, b_dta], [bb2], inc=False)
                pe(lambda h, c=c: h.matmul(b2[0:CH, c * 64 + 32:c * 64 + 64], lhsT=Um[0:CH, 0:CH], rhs=dta[0:CH, c, :], start=True, stop=True),
                   [b_Um, b_dta], [bb2], inc=(c == NCH - 1))
            for c in range(NCH):
                pe(lambda h, c=c: h.matmul(b3[:, c * 32:(c + 1) * 32], lhsT=onesf[0:CH, :], rhs=dta[0:CH, c, :], start=True, stop=True),
                   [b_onesf, b_dta], [bb3], inc=(c == NCH - 1))

        def dt_e():
            b2, bb2, b3, bb3 = dtst["b2"], dtst["bb2"], dtst["b3"], dtst["bb3"]
            act(lambda h: h.activation(out=e3[0:CH, 0:NCH, :], in_=b2[0:CH, 0:NCH * 64].rearrange("p (c n) -> p c n", c=NCH), func=AF.Exp), [bb2], [b_e3])
            act(lambda h: h.activation(out=dec[:, 0:NCH, :], in_=b3[:, 0:NCH * 32].rearrange("p (c n) -> p c n", c=NCH), func=AF.Exp), [bb3], [b_dec])

        def dt_f():
            dve(lambda h: h.tensor_tensor(out=wdt[0:CH, 0:NCH, :], in0=e3[0:CH, 0:NCH, 32:64], in1=dtb[0:CH, 0:NCH, :], op=ALU.mult),
                [b_e3, b_dtb], [b_wdt])

        DT_AT = {1: dt_a, 4: dt_b, 7: dt_c, 10: dt_d, 13: dt_e, 16: dt_f}

        for zp in range(4):
            rtz, rbz, qz = ring_get(tile_i, f"z{zp}")
            wvz = rtz[:, 0:4096].rearrange("p (k n) -> p k n", k=KC)
            zc = 0
            for half in range(2):
                rtx, rbx, qx = ring_get(tile_i, f"xbc{2 * zp + half}")
                wvx = rtx[:, 0:4096].rearrange("p (k n) -> p k n", k=KC)
                for mm in range(4):
                    xbc_chunk(4 * (2 * zp + half) + mm, wvx, rbx)
                    if 4 * (2 * zp + half) + mm in DT_AT:
                        DT_AT[4 * (2 * zp + half) + mm]()
                    if mm % 2 == 1 and zc < NCH:
                        z_group(zp, zc, wvz, rbz)
                        zc += 1
                ring_release(qx)
            while zc < NCH:
                z_group(zp, zc, wvz, rbz)
                zc += 1
            ring_release(qz)
        conv_back(31)
        if last:
            feat2tok_small(lambda m: hist[:, m, :], b_hist, 3, 4, CONVD, seq_out["conv"])
        iters = [(c, g) for c in range(NCH) for g in range(8)]
        nit = len(iters)
        STAGE_IDS = [0, 1, 15, 2, 3, 4, 5, 6]
        for step in range(nit + NSTG - 1):
            for pos in range(NSTG - 1, -1, -1):
                i = step - pos
                if 0 <= i < nit:
                    ssd_stage(STAGE_IDS[pos], i, iters[i][0], iters[i][1], CH, xbc_c, gtok)
        if last:
            for j0 in range(0, 16, 4):
                bt, bb = nbank()
                for jj in range(4):
                    j = j0 + jj
                    pe(lambda h, j=j, jj=jj, bt=bt: h.transpose(out=bt[:, jj * 128:(jj + 1) * 128], in_=hT[:, j * 128:(j + 1) * 128],
                                                              identity=ident[:]), b_hTg + [b_ident], [bb], inc=(jj == 3))
                yt_, yb_ = yout[(j0 // 4) % 2]
                act(lambda h, bt=bt, yt_=yt_: h.copy(out=yt_[:, 0:512], in_=bt[:, 0:512]), [bb], [yb_])
                dma(lambda h, j0=j0, yt_=yt_: h.dma_start(out=seq_out["ssm"][j0 * 128:(j0 + 4) * 128, :].rearrange("(a p) n -> p a n", p=128),
                                                          in_=yt_[:, 0:512].rearrange("p (a n) -> p a n", a=4)),
                    [yb_], [], f"yout{(j0 // 4) % 2}", eng="act")
        pool_u(0)
        pool_u(1)
        gybv = lambda m: bigA[:, 16 * TILE:32 * TILE].bitcast(F32)[:, m * TILE: m * TILE + T]
        merged = lambda m: bigA[:, m * TILE: m * TILE + T]
        sgt = [(rstd, b_rstd), tmpP[1]]

        def ynT_rhs(kcc):
            g_, j_ = kcc // 2, kcc % 2
            off = g_ * 256 + j_ * 128
            return bigB[:, 0:NCH * 2048].rearrange("p (c r) -> p c r", c=NCH)[:, :, off:off + CH]

        for j in range(2):
            rtg, rbg, qg = ring_get(tile_i, f"ga{j}")
            wg = rtg[:, 0:4096].rearrange("p (k n) -> p k n", k=KC)
            for j2 in range(2):
                rtp, rbp, qp = ring_get(tile_i, f"ps{2 * j + j2}")
                wp = rtp[:, 0:4096].rearrange("p (k n) -> p k n", k=16)
                for m2 in range(2):
                    mm = 2 * j2 + m2
                    m = 4 * j + mm
                    bg, bgb = nbank()
                    mm_group(bg[:, 0:T], bgb, [(wg[:, kc, mm * 128:(mm + 1) * 128], xn[:, kc, 0:T]) for kc in range(KC)], [rbg, b_xn])
                    by, byb = nbank()
                    mm_group(by[:, 0:T].rearrange("p (c l) -> p c l", c=NCH), byb,
                             [(wp[:, kc, m2 * 128:(m2 + 1) * 128], ynT_rhs(kc)) for kc in range(16)], [rbp] + ALL_GT)
                    ta, tab = sgt[m % 2]
                    act(lambda h, ta=ta, bg=bg: h.activation(out=ta[:, 0:T], in_=bg[:, 0:T], func=AF.Sigmoid), [bgb], [tab])
                    dve(lambda h, ta=ta, by=by, m=m: h.tensor_tensor(out=gybv(m), in0=ta[:, 0:T], in1=by[:, 0:T], op=ALU.mult),
                        [tab, byb], AR(16 + 2 * m, 2))
                ring_release(qp)
            ring_release(qg)
        pool_pm()
        for j in range(2):
            rtg, rbg, qg = ring_get(tile_i, f"gb{j}")
            rtp, rbp, qp = ring_get(tile_i, f"pp{j}")
            wg = rtg[:, 0:4096].rearrange("p (k n) -> p k n", k=KC)
            wp = rtp[:, 0:4096].rearrange("p (k n) -> p k n", k=KC)
            for mm in range(4):
                m = 4 * j + mm
                bg, bgb = nbank()
                mm_group(bg[:, 0:T], bgb, [(wg[:, kc, mm * 128:(mm + 1) * 128], xn[:, kc, 0:T]) for kc in range(KC)], [rbg, b_xn])
                by, byb = nbank()
                mm_group(by[:, 0:T], byb, [(wp[:, kc, mm * 128:(mm + 1) * 128], mixedv(kc)) for kc in range(KC)], [rbp, b_mixedS])
                ta, tab = tmpA[m % 2]
                tt, tb = tmpP[m % 2]
                act(lambda h, ta=ta, bg=bg: h.activation(out=ta[:, 0:T], in_=bg[:, 0:T], func=AF.Sigmoid), [bgb], [tab])
                dve(lambda h, ta=ta, by=by, tt=tt: h.tensor_tensor(out=tt[:, 0:T], in0=ta[:, 0:T], in1=by[:, 0:T], op=ALU.mult),
                    [tab, byb], [tb])
                pool(lambda h, tt=tt, m=m: h.tensor_tensor(out=merged(m), in0=tt[:, 0:T], in1=gybv(m), op=ALU.add),
                     [tb] + AR(16 + 2 * m, 2), AR(m))
            ring_release(qg)
            ring_release(qp)
        sqv2 = lambda kc: bigA[:, (8 + kc) * TILE:(8 + kc) * TILE + T]
        for j in range(2):
            rt, rb, q = ring_get(tile_i, f"wo{j}")
            wv = rt[:, 0:4096].rearrange("p (k n) -> p k n", k=KC)
            for mm in range(4):
                m = 4 * j + mm
                bt, bb = nbank()
                mm_group(bt[:, 0:T], bb, [(wv[:, kc, mm * 128:(mm + 1) * 128], merged(kc)) for kc in range(KC)], [rb] + AR(0, 8))
                act(lambda h, m=m, bt=bt: h.copy(out=fview(m, T), in_=bt[:, 0:T]), [bb], gt_for_f(m))
                dve(lambda h, m=m, bt=bt: h.tensor_tensor(out=sqv2(m), in0=bt[:, 0:T], in1=fview(m, T), op=ALU.mult), [bb] + gt_for_f(m), AR(SQ2 + m))
            ring_release(q)
        rms_finish(T, sqv2, SQ2, False)
        residual_update(1, T, True, X)

    def ssd_stage(k, i, c, g, CH, xbc_c, gtok):
        h4 = slice(4 * g, 4 * g + 4)
        xt_, xtb = xtok[i % 4]
        xw_, xwb = xw[i % 2]
        st_, stb = stm[i % 3]
        ud_, udb = ud[i % 2]
        e_, eb = eE[i % 2]
        mt_, mtb = mt[i % 2]
        t1_, t1b = t1[i % 2]
        yg_, ygb = yg[i % 4]
        sq_, sqb = ssq[i % 3]
        dg_, dgb = dgs[i % 2]
        ht_, htb = htmp[i % 2]
        ud3 = ud_[0:CH, 0:4 * CH].rearrange("p (a b) -> p a b", a=4)
        mt3 = mt_[0:CH, 0:4 * CH].rearrange("p (a b) -> p a b", a=4)
        e3v = e_[0:CH, 0:4 * CH].rearrange("p (a b) -> p a b", a=4)
        bA, bbA = banks[i % 2]
        bD, bbD = banks[2]
        bY, bbY = banks[4 + i % 2]
        bE, bbE = banks[(3, 6)[i % 2]]
        bF, bbF = banks[7]
        if k == 0:
            bAb = bA[:].bitcast(BF16)
            for ii, m in enumerate((2 * g, 2 * g + 1, 16 + g)):
                pe(lambda h, ii=ii, m=m: h.transpose(out=bAb[0:CH, ii * 128:(ii + 1) * 128], in_=xbc_c(m, c), identity=identb[:]),
                   AR(m) + [b_identb], [bbA], inc=False)
            pe(lambda h: h.matmul(bA[0:CH, 256:256 + CH], lhsT=xbc_c(16 + g, c), rhs=xbc_c(24 + g, c), start=True, stop=True), AR(16 + g) + AR(24 + g), [bbA])
            act(lambda h: h.copy(out=xt_[0:CH, 0:384], in_=bAb[0:CH, 0:384]), [bbA], [xtb])
            dve(lambda h: h.tensor_tensor(out=st_[0:CH, 0:CH], in0=bA[0:CH, 256:256 + CH], in1=Lm[0:CH, 0:CH], op=ALU.mult), [bbA, b_Lm, xtb], [stb])
            pool(lambda h: h.tensor_tensor(out=ud3, in0=bc_mid(Um[0:CH, 0:CH], 4), in1=bc_last(dta[0:CH, c, h4], CH), op=ALU.mult),
                 [b_Um, b_dta], [udb])
            x3 = xt_[0:CH, 0:256].rearrange("p (a b) -> p a b", a=4)
            pool(lambda h: h.tensor_tensor(out=xw_[0:CH, :].rearrange("p (a b) -> p a b", a=4), in0=x3, in1=bc_last(wdt[0:CH, c, h4], 64), op=ALU.mult),
                 [xtb, b_wdt], [xwb])
        elif k == 1:
            for hh in range(4):
                pe(lambda h, hh=hh: h.matmul(bD[0:CH, hh * CH:(hh + 1) * CH], lhsT=ud3[:, hh, :], rhs=Lm[0:CH, 0:CH], start=True, stop=True),
                   [udb, b_Lm], [bbD], inc=(hh == 3))
            pe(lambda h: h.matmul(bE[:, 0:256], lhsT=xt_[0:CH, 256:384], rhs=xw_[0:CH, :], start=True, stop=True), [xtb, xwb], [bbE])
            act(lambda h: h.activation(out=e_[0:CH, 0:4 * CH], in_=bD[0:CH, 0:4 * CH], func=AF.Exp), [bbD], [eb])
            hseg = hT[:, g * 256:(g + 1) * 256]
            pool(lambda h: h.tensor_tensor(out=ht_[:, :].rearrange("p (a b) -> p a b", a=4), in0=hseg.rearrange("p (a b) -> p a b", a=4),
                                           in1=bc_last(dec[:, c, h4], 64), op=ALU.mult), [b_hTg[g], b_dec], [htb])
        elif k == 15:
            hseg = hT[:, g * 256:(g + 1) * 256]
            pe(lambda h: h.matmul(bY[0:CH, 256:512], lhsT=xbc_c(24 + g, c), rhs=hTb[:, g * 256:(g + 1) * 256], start=True, stop=True),
               AR(24 + g) + [b_hTbg[g]], [bbY], inc=True)
            dve(lambda h: h.tensor_tensor(out=hseg, in0=ht_[:, :], in1=bE[:, 0:256], op=ALU.add), [htb, bbE], [b_hTg[g]])
            act(lambda h: h.copy(out=hTb[:, g * 256:(g + 1) * 256], in_=hseg), [b_hTg[g]], [b_hTbg[g]])
            for hh in range(4):
                dve(lambda h, hh=hh: h.scalar_tensor_tensor(out=mt3[:, hh, :], in0=e3v[:, hh, :], scalar=dtb[0:CH, c, 4 * g + hh:4 * g + hh + 1],
                                                           in1=st_[0:CH, 0:CH], op0=ALU.mult, op1=ALU.mult),
                    [eb, stb, b_dtb], [mtb])
        elif k == 2:
            for j in range(2):
                pe(lambda h, j=j: h.matmul(bY[0:CH, j * 128:(j + 1) * 128], lhsT=xbc_c(2 * g + j, c), rhs=Dg[:, 2 * g + j, :], start=True, stop=False),
                   AR(2 * g + j) + [b_Dg], [bbY], inc=False)
                for hh in (2 * j, 2 * j + 1):
                    pe(lambda h, hh=hh: h.matmul(bY[0:CH, hh * 64:(hh + 1) * 64], lhsT=mt3[:, hh, :], rhs=xt_[0:CH, hh * 64:(hh + 1) * 64], start=False, stop=(hh % 2 == 1)),
                       [mtb, xtb], [bbY], inc=(hh == 3))
            dve(lambda h: h.tensor_tensor(out=t1_[0:CH, :].rearrange("p (a b) -> p a b", a=4), in0=bY[0:CH, 256:512].rearrange("p (a b) -> p a b", a=4),
                                          in1=bc_last(e3[0:CH, c, h4], 64), op=ALU.mult), [bbY, b_e3], [t1b])
            dve(lambda h: h.tensor_tensor(out=t1_[0:CH, :], in0=bY[0:CH, 0:256], in1=t1_[0:CH, :], op=ALU.add), [bbY, t1b], [t1b])
        elif k == 3:
            pool(lambda h: h.tensor_tensor(out=yg_[0:CH, :], in0=t1_[0:CH, :], in1=gtok(c, g * 256, 256), op=ALU.mult), [t1b, b_gt[c][g]], [ygb])
            act(lambda h: h.activation(out=t1_[0:CH, :], in_=yg_[0:CH, :], func=AF.Square, accum_out=sq_[0:CH, 0:1]), [ygb], [t1b, sqb])
        elif k == 4:
            act(lambda h: h.activation(out=sq_[0:CH, 1:2], in_=sq_[0:CH, 0:1], func=AF.Ln, bias=EPS, scale=1.0 / 256), [sqb], [sqb])
            act(lambda h: h.activation(out=sq_[0:CH, 2:3], in_=sq_[0:CH, 1:2], func=AF.Exp, scale=-0.5), [sqb], [sqb])
        elif k == 5:
            act(lambda h: h.activation(out=dg_[0:CH, 0:CH], in_=identb[0:CH, 0:CH], func=AF.Copy, scale=sq_[0:CH, 2:3]), [b_identb, sqb], [dgb])
        else:
            for j in range(2):
                pe(lambda h, j=j: h.matmul(bF[:, j * CH:(j + 1) * CH], lhsT=yg_[0:CH, j * 128:(j + 1) * 128], rhs=dg_[0:CH, 0:CH], start=True, stop=True),
                   [ygb, dgb], [bbF], inc=(j == 1))
            off = c * 2048 + g * 256
            act(lambda h: h.copy(out=bigB[:, off:off + 256].rearrange("p (j l) -> p j l", j=2)[:, :, 0:CH],
                                 in_=bF[:, 0:2 * CH].rearrange("p (j l) -> p j l", j=2)), [bbF], [b_gt[c][g]])

    ntile_seq = SEQ // TILE
    descs = []
    for b in range(NSEQ):
        for ti in range(ntile_seq):
            r0 = b * SEQ + ti * TILE
            descs.append(dict(kind="p", b=b, ti=ti, x=x_prompt.ap()[r0:r0 + TILE, :], y=y_prompt.ap()[r0:r0 + TILE, :], T=TILE, CH=128))
    if cfg.SAMPLE:
        descs.append(dict(kind="s", x=x_sample.ap(), y=y_sample.ap(), T=DSEQ, CH=DSEQ))
    def sample_init():
        tok2feat_small(cache_conv.ap(), 4, CONVD, lambda m0: hist[:, m0:m0 + 4, :], b_hist)
        tok2feat_small(cache_pool.ap(), 16, D, lambda m0: uhist[:, m0:m0 + 4, :], b_uhist)
        for j0 in range(0, 16, 4):
            yt_, yb_ = yout[(j0 // 4) % 2]
            dma(lambda h, j0=j0, yt_=yt_: h.dma_start(out=yt_[:, 0:512].rearrange("p (a n) -> p a n", a=4),
                                                      in_=state_ssm.ap()[j0 * 128:(j0 + 4) * 128, :].rearrange("(a p) n -> p a n", p=128)),
                [], [yb_], f"yout{(j0 // 4) % 2}")
            bt, bb = nbank()
            for jj in range(4):
                pe(lambda h, jj=jj, bt=bt, yt_=yt_: h.transpose(out=bt[:, jj * 128:(jj + 1) * 128], in_=yt_[:, jj * 128:(jj + 1) * 128], identity=ident[:]),
                   [yb_, b_ident], [bb], inc=(jj == 3))
            act(lambda h, j0=j0, bt=bt: h.copy(out=hT[:, j0 * 128:(j0 + 4) * 128], in_=bt[:, 0:512]), [bb], b_hTg)
        act(lambda h: h.copy(out=hTb[:], in_=hT[:]), b_hTg, b_hTbg)

    head(0, descs[0]["x"], descs[0]["T"], descs[0]["CH"], XS[0])
    hook = None
    for tile_i, d in enumerate(descs):
        X = XS[tile_i % 2]
        nd = descs[tile_i + 1] if tile_i + 1 < len(descs) else None
        nxt = (tile_i + 1, nd["x"], nd["T"], nd["CH"]) if nd is not None else None
        if d["kind"] == "p":
            b, ti = d["b"], d["ti"]
            if ti == 0:
                pool(lambda h: h.memset(hist[:], 0.0), [], [b_hist])
                pool(lambda h: h.memset(uhist[:], 0.0), [], [b_uhist])
                pool(lambda h: h.memset(hT[:], 0.0), [], b_hTg)
                pool(lambda h: h.memset(hTb[:], 0.0), [], b_hTbg)
            seq_out = {"conv": o_conv_p.ap()[b * 3:(b + 1) * 3, :], "ssm": o_ssm_p.ap()[b * DIN:(b + 1) * DIN, :],
                       "pool": o_pool_p.ap()[b * 15:(b + 1) * 15, :]}
            body(tile_i, TILE, 128, (ti == 0), (ti == ntile_seq - 1), seq_out, nxt, X, hook)
        else:
            sample_init()
            seq_out = {"conv": o_conv_s.ap(), "ssm": o_ssm_s.ap(), "pool": o_pool_s.ap()}
            body(tile_i, DSEQ, DSEQ, False, True, seq_out, None, X, hook)
        if nd is not None:
            head(tile_i + 1, nd["x"], nd["T"], nd["CH"], XS[(tile_i + 1) % 2])
        tail_a(d["T"], X)
        hook = (lambda d=d, X=X: tail_b(d["y"], d["T"], d["CH"], X))
    hook()

    P.replay(nc, es)
    es.close()
    return nc


_CFG = Cfg


def kernel(**inputs):
    cfg = _CFG
    NSEQ, SEQ = cfg.NSEQ, cfg.SEQ
    f32 = lambda a: np.ascontiguousarray(np.asarray(a, dtype=np.float32))
    shared = {}
    for k in WEIGHT_SHAPES:
        shared[k] = f32(inputs[k]).reshape(WEIGHT_SHAPES[k])
    for k, n in VEC_SHAPES.items():
        shared[k] = f32(inputs[k]).reshape(n)
    shared["conv_w"] = f32(inputs["conv_w"]).reshape(4, CONVD)
    xp = f32(inputs["x_prompt"])
    xs = f32(inputs["x_sample"])
    cc = f32(inputs["cache_conv"])[0]
    ss = f32(inputs["state_ssm"])[0]
    cp = f32(inputs["cache_pool"])[0]
    in_maps = []
    for c in range(NCORES):
        m = dict(shared)
        m["x_prompt"] = xp[c * NSEQ:(c + 1) * NSEQ].reshape(NSEQ * SEQ, D)
        m["x_sample"] = xs[c].reshape(cfg.DSEQ, D)
        m["cache_conv"] = cc[c].reshape(3, CONVD)
        m["state_ssm"] = ss[c].reshape(DIN, 128)
        m["cache_pool"] = cp[c].reshape(15, D)
        in_maps.append(m)
    nc = build_program(cfg)
    res = run_bass_kernel_spmd(nc, in_maps, core_ids=list(range(NCORES)))
    R = res.results
    B = NCORES * NSEQ
    y_p = np.concatenate([R[c]["y_prompt"].reshape(NSEQ, SEQ, D) for c in range(NCORES)], axis=0)
    y_s = np.stack([R[c]["y_sample"].reshape(cfg.DSEQ, D) for c in range(NCORES)], axis=0)
    conv_p = np.concatenate([R[c]["new_conv_prompt"].reshape(NSEQ, 3, CONVD) for c in range(NCORES)], axis=0)[None]
    ssm_p = np.concatenate([R[c]["new_ssm_prompt"].reshape(NSEQ, NH, 64, 128) for c in range(NCORES)], axis=0)[None]
    pool_p = np.concatenate([R[c]["new_pool_prompt"].reshape(NSEQ, 15, D) for c in range(NCORES)], axis=0)[None]
    conv_s = np.stack([R[c]["new_conv_sample"].reshape(3, CONVD) for c in range(NCORES)], axis=0)[None]
    ssm_s = np.stack([R[c]["new_ssm_sample"].reshape(NH, 64, 128) for c in range(NCORES)], axis=0)[None]
    pool_s = np.stack([R[c]["new_pool_sample"].reshape(15, D) for c in range(NCORES)], axis=0)[None]
    return (y_p.astype(np.float32), y_s.astype(np.float32), conv_p.astype(np.float32), ssm_p.astype(np.float32),
            pool_p.astype(np.float32), conv_s.astype(np.float32), ssm_s.astype(np.float32), pool_s.astype(np.float32))
```
